# Optimizing a Trainium2 kernel written in Bass

```python
import jax, jax.numpy as jnp
from jax import lax
import numpy as np

D_MODEL = 2048
BATCH = 32
SEQ = 256
DEPTH = 2
DEC_BATCH = 8
DEC_SEQ = 1024
PAST_LEN = 512

GRID_W = 64
N_MIXERS = 2
N_RG = (DEPTH + 1) // 2
N_CM = DEPTH // 2
D_RNN = D_MODEL
RG_HEADS = 16
RG_BW = D_RNN // RG_HEADS
RG_CONV_W = 4
RG_C = 8.0
CHUNK = 128
CM_WIDTH = D_MODEL
CM_GROUPS = 8
CM_GW = CM_WIDTH // CM_GROUPS
D_FF = 5632
FFN_CONV_W = 3
EPS = 1e-6
LN_EPS = 1e-5

kernel_name = 'hybrid_rglru_chunkgmlp_diffusion_step'


def _rmsnorm(x, g):
    x32 = x.astype(jnp.float32)
    y = x32 * lax.rsqrt(jnp.mean(x32 * x32, axis=-1, keepdims=True) + EPS)
    return (y * g.astype(jnp.float32)).astype(x.dtype)


def _modulation(cond, w, b):
    m = jax.nn.silu(cond) @ w + b
    return jnp.split(m[..., None, :], 6, axis=-1)


def _dwconv(x, w, b, pad_l, pad_r):
    T = x.shape[1]
    xp = jnp.pad(x, ((0, 0), (pad_l, pad_r), (0, 0)))
    y = xp[:, 0:T] * w[0]
    for k in range(1, w.shape[0]):
        y = y + xp[:, k:k + T] * w[k]
    return y + b


def _grid_pos_embed(rows, dtype):
    nf = D_MODEL // 4
    omega = 1.0 / (10000.0 ** (jnp.arange(nf, dtype=jnp.float32) / nf))
    rr, cc = jnp.meshgrid(jnp.arange(rows, dtype=jnp.float32),
                          jnp.arange(GRID_W, dtype=jnp.float32), indexing='ij')
    ar = rr.reshape(-1, 1) * omega
    ac = cc.reshape(-1, 1) * omega
    emb = jnp.concatenate([jnp.sin(ar), jnp.cos(ar), jnp.sin(ac), jnp.cos(ac)], axis=-1)
    return emb.astype(dtype)


def _rglru_mixer(xn, h0, w_in, conv_w, conv_b, w_a, b_a, w_x, b_x, lam, w_out):
    B, T, _ = xn.shape
    gate_in, xb = jnp.split(xn @ w_in, 2, axis=-1)
    gate = jax.nn.gelu(gate_in, approximate=True)
    xb = _dwconv(xb, conv_w, conv_b, 2, 1)
    xh = xb.reshape(B, T, RG_HEADS, RG_BW)
    r = jax.nn.sigmoid(jnp.einsum('bthi,dhij->btdhj', xh, w_a).reshape(B, T, 2, D_RNN) + b_a)
    i = jax.nn.sigmoid(jnp.einsum('bthi,dhij->btdhj', xh, w_x).reshape(B, T, 2, D_RNN) + b_x)
    log_a = -RG_C * r.astype(jnp.float32) * jax.nn.softplus(-lam.astype(jnp.float32))
    a = jnp.exp(log_a)
    u = jnp.sqrt(-jnp.expm1(2.0 * log_a)) * (i * xb[:, :, None, :]).astype(jnp.float32)
    a = jnp.stack([a[:, :, 0], a[:, ::-1, 1]], axis=2)
    u = jnp.stack([u[:, :, 0], u[:, ::-1, 1]], axis=2)

    def step(h, au):
        a_t, u_t = au
        h = a_t * h + u_t
        return h, h

    h_final, hs = lax.scan(step, h0.astype(jnp.float32),
                           (jnp.moveaxis(a, 1, 0), jnp.moveaxis(u, 1, 0)))
    y = hs[:, :, 0] + hs[::-1, :, 1]
    y = jnp.moveaxis(y, 0, 1).astype(xn.dtype)
    return (y * gate) @ w_out, h_final


def _chunk_gmlp_mixer(xn, w_in, b_in, ln_g, ln_b, w_s, b_s, w_out):
    B, T, _ = xn.shape
    n_chunks = T // CHUNK
    u, v = jnp.split(jax.nn.gelu(xn @ w_in + b_in, approximate=True), 2, axis=-1)
    v32 = v.astype(jnp.float32)
    mu = jnp.mean(v32, axis=-1, keepdims=True)
    var = jnp.mean(jnp.square(v32 - mu), axis=-1, keepdims=True)
    v = ((v32 - mu) * lax.rsqrt(var + LN_EPS) * ln_g + ln_b).astype(xn.dtype)
    v = v.reshape(B, n_chunks, CHUNK, CM_GROUPS, CM_GW)
    s = jnp.einsum('gpq,bnqgc->bnpgc', w_s, v) + b_s.T[:, :, None]
    return (u * s.reshape(B, T, CM_WIDTH)) @ w_out


def _conv_ffn(xn, w_up, conv_w, conv_b, w_down):
    h = _dwconv(xn @ w_up, conv_w, conv_b, 1, 1)
    g, v = jnp.split(h, 2, axis=-1)
    return (jax.nn.silu(g) * v) @ w_down


def _trunk(x, cond, h0, norm1_g, norm2_g, w_ada, b_ada,
           rg_w_in, rg_conv_w, rg_conv_b, rg_w_a, rg_b_a, rg_w_x, rg_b_x, rg_lam, rg_w_out,
           cm_w_in, cm_b_in, cm_ln_g, cm_ln_b, cm_w_s, cm_b_s, cm_w_out,
           ffn_w_up, ffn_conv_w, ffn_conv_b, ffn_w_down, final_g):
    h_finals = []
    for l in range(DEPTH):
        sh1, sc1, g1, sh2, sc2, g2 = _modulation(cond, w_ada[l], b_ada[l])
        xn = _rmsnorm(x, norm1_g[l]) * (1 + sc1) + sh1
        j = l // N_MIXERS
        if l % N_MIXERS == 0:
            mix, h_fin = _rglru_mixer(xn, h0[:, j], rg_w_in[j], rg_conv_w[j], rg_conv_b[j],
                                      rg_w_a[j], rg_b_a[j], rg_w_x[j], rg_b_x[j], rg_lam[j],
                                      rg_w_out[j])
            h_finals.append(h_fin)
        else:
            mix = _chunk_gmlp_mixer(xn, cm_w_in[j], cm_b_in[j], cm_ln_g[j], cm_ln_b[j],
                                    cm_w_s[j], cm_b_s[j], cm_w_out[j])
        x = x + g1 * mix
        xn = _rmsnorm(x, norm2_g[l]) * (1 + sc2) + sh2
        x = x + g2 * _conv_ffn(xn, ffn_w_up[l], ffn_conv_w[l], ffn_conv_b[l], ffn_w_down[l])
    return _rmsnorm(x, final_g), jnp.stack(h_finals, axis=1)


def setup_inputs(seed: int = 0) -> dict:
    key = jax.random.key(seed)
    ks = iter(jax.random.split(key, 40))
    nrm = lambda shape, s: jax.random.normal(next(ks), shape, jnp.float32) * s
    D = D_MODEL
    a0 = jax.random.uniform(next(ks), (N_RG, 2, D_RNN), jnp.float32, 0.9, 0.999)
    sg = a0 ** (1.0 / RG_C)
    rg_lam = jnp.log(sg) - jnp.log1p(-sg)
    return {
        'x_prompt': nrm((BATCH, SEQ, D), 1.0),
        'x_sample': nrm((DEC_BATCH, DEC_SEQ, D), 1.0),
        'state_rglru': nrm((DEC_BATCH, N_RG, 2, D_RNN), 0.5),
        'c': nrm((DEC_BATCH, D), 1.0),
        'c_ctx': nrm((D,), 1.0),
        'norm1_g': 1.0 + nrm((DEPTH, D), 0.05),
        'norm2_g': 1.0 + nrm((DEPTH, D), 0.05),
        'w_ada': nrm((DEPTH, D, 6 * D), 0.5 * D ** -0.5),
        'b_ada': nrm((DEPTH, 6 * D), 0.02),
        'rg_w_in': nrm((N_RG, D, 2 * D_RNN), D ** -0.5),
        'rg_conv_w': nrm((N_RG, RG_CONV_W, D_RNN), RG_CONV_W ** -0.5),
        'rg_conv_b': nrm((N_RG, D_RNN), 0.02),
        'rg_w_a': nrm((N_RG, 2, RG_HEADS, RG_BW, RG_BW), RG_BW ** -0.5),
        'rg_b_a': nrm((N_RG, 2, D_RNN), 0.02),
        'rg_w_x': nrm((N_RG, 2, RG_HEADS, RG_BW, RG_BW), RG_BW ** -0.5),
        'rg_b_x': nrm((N_RG, 2, D_RNN), 0.02),
        'rg_lam': rg_lam,
        'rg_w_out': nrm((N_RG, D_RNN, D), D_RNN ** -0.5),
        'cm_w_in': nrm((N_CM, D, 2 * CM_WIDTH), D ** -0.5),
        'cm_b_in': nrm((N_CM, 2 * CM_WIDTH), 0.02),
        'cm_ln_g': 1.0 + nrm((N_CM, CM_WIDTH), 0.05),
        'cm_ln_b': nrm((N_CM, CM_WIDTH), 0.02),
        'cm_w_s': nrm((N_CM, CM_GROUPS, CHUNK, CHUNK), CHUNK ** -0.5),
        'cm_b_s': 1.0 + nrm((N_CM, CM_GROUPS, CHUNK), 0.05),
        'cm_w_out': nrm((N_CM, CM_WIDTH, D), CM_WIDTH ** -0.5),
        'ffn_w_up': nrm((DEPTH, D, 2 * D_FF), D ** -0.5),
        'ffn_conv_w': nrm((DEPTH, FFN_CONV_W, 2 * D_FF), FFN_CONV_W ** -0.5),
        'ffn_conv_b': nrm((DEPTH, 2 * D_FF), 0.02),
        'ffn_w_down': nrm((DEPTH, D_FF, D), D_FF ** -0.5),
        'final_g': 1.0 + nrm((D,), 0.05),
    }


def reference(x_prompt, x_sample, state_rglru, c, c_ctx, norm1_g, norm2_g, w_ada, b_ada,
              rg_w_in, rg_conv_w, rg_conv_b, rg_w_a, rg_b_a, rg_w_x, rg_b_x, rg_lam, rg_w_out,
              cm_w_in, cm_b_in, cm_ln_g, cm_ln_b, cm_w_s, cm_b_s, cm_w_out,
              ffn_w_up, ffn_conv_w, ffn_conv_b, ffn_w_down, final_g):
    h0_ctx = jnp.zeros((x_prompt.shape[0], N_RG, 2, D_RNN), jnp.float32)
    y_prompt, new_state_rglru = _trunk(
        x_prompt, c_ctx, h0_ctx, norm1_g, norm2_g, w_ada, b_ada,
        rg_w_in, rg_conv_w, rg_conv_b, rg_w_a, rg_b_a, rg_w_x, rg_b_x, rg_lam, rg_w_out,
        cm_w_in, cm_b_in, cm_ln_g, cm_ln_b, cm_w_s, cm_b_s, cm_w_out,
        ffn_w_up, ffn_conv_w, ffn_conv_b, ffn_w_down, final_g)
    rows = x_sample.shape[1] // GRID_W
    xs = x_sample + _grid_pos_embed(rows, x_sample.dtype)
    y_sample, _ = _trunk(
        xs, c, state_rglru, norm1_g, norm2_g, w_ada, b_ada,
        rg_w_in, rg_conv_w, rg_conv_b, rg_w_a, rg_b_a, rg_w_x, rg_b_x, rg_lam, rg_w_out,
        cm_w_in, cm_b_in, cm_ln_g, cm_ln_b, cm_w_s, cm_b_s, cm_w_out,
        ffn_w_up, ffn_conv_w, ffn_conv_b, ffn_w_down, final_g)
    return (y_prompt, y_sample, new_state_rglru)
```

```python
import contextlib
import math
import os
import numpy as np
import concourse.bass as bass
import concourse.mybir as mybir
from concourse.bass_utils import run_bass_kernel_spmd

F32 = mybir.dt.float32
BF16 = mybir.dt.bfloat16
I32 = mybir.dt.int32
AF = mybir.ActivationFunctionType
ALU = mybir.AluOpType
AX = mybir.AxisListType

ENGS = ("sp", "act", "pe", "dve", "pool")
NCORES = 8
D = 2048
NCH = 16
T = 1024
DFF = 5632
NFC = DFF // 128
EPS = 1e-6
LN_EPS = 1e-5


class Buf:
    __slots__ = ("name", "w", "r", "sem", "semval")

    def __init__(self, name):
        self.name = name
        self.w = None
        self.r = []
        self.sem = None
        self.semval = 0


class Prog:
    def __init__(self, nc):
        self.nc = nc
        self.ops = {e: [] for e in ENGS}
        self.cnt = {e: 0 for e in ENGS}
        self.seen = {e: {} for e in ENGS}
        self.esem = {}
        self._ctx = []
        for e in ("act", "pe", "dve", "pool"):
            cm = nc.semaphore("s_" + e)
            self.esem[e] = cm.__enter__()
            self._ctx.append(cm)
        self.bufs = {}
        self.dma_pending = {}

    def buf(self, name):
        b = self.bufs.get(name)
        if b is None:
            b = Buf(name)
            self.bufs[name] = b
        return b

    def _dma_sem(self, b):
        if b.sem is None:
            cm = self.nc.semaphore("d_" + b.name)
            b.sem = cm.__enter__()
            self._ctx.append(cm)
        return b.sem

    def _collect(self, eng, reads, writes):
        best = {}
        def add(t):
            kind, key, val = t
            if kind == "e" and key == "pe" and eng == "pe":
                return
            k = (kind, key)
            if best.get(k, 0) < val:
                best[k] = val
        for b in reads:
            if b.w is not None:
                add(b.w)
        for b in writes:
            if b.w is not None:
                add(b.w)
            for t in b.r:
                add(t)
        waits = []
        seen = self.seen[eng]
        for k, val in best.items():
            if seen.get(k, 0) >= val:
                continue
            seen[k] = val
            kind, key = k
            sem = self.esem[key] if kind == "e" else key.sem
            waits.append((sem, val))
        return waits

    def op(self, eng, fn, reads=(), writes=(), signal=True):
        waits = self._collect(eng, reads, writes)
        if signal:
            self.cnt[eng] += 1
            t = ("e", eng, self.cnt[eng])
        else:
            t = ("e", eng, self.cnt[eng] + 1)
        self.ops[eng].append((fn, waits, signal, None))
        for b in writes:
            b.w = t
            b.r = []
        for b in reads:
            b.r.append(t)
        return t

    def dma(self, eng, fn, reads=(), writes=(), semb=None):
        if semb is None:
            semb = writes[0] if writes else reads[0]
        sem = self._dma_sem(semb)
        waits = self._collect(eng, reads, writes)
        semb.semval += 16
        t = ("d", semb, semb.semval)
        self.dma_pending[semb] = semb.semval
        self.ops[eng].append((fn, waits, False, (sem, 16)))
        for b in writes:
            b.w = t
            b.r = []
        for b in reads:
            b.r.append(t)
        return t

    def barrier(self, engs=ENGS):
        cur = {f: self.cnt[f] for f in ("act", "pe", "dve", "pool")}
        for e in engs:
            waits = []
            seen = self.seen[e]
            for f, v in cur.items():
                if f == e or v == 0:
                    continue
                k = ("e", f)
                if seen.get(k, 0) >= v:
                    continue
                seen[k] = v
                waits.append((self.esem[f], v))
            for b, val in self.dma_pending.items():
                k = ("d", b)
                if seen.get(k, 0) >= val:
                    continue
                seen[k] = val
                waits.append((b.sem, val))
            self.ops[e].append((None, waits, False, None))
        self.dma_pending = {}

    def emit(self):
        nc = self.nc
        ops = self.ops
        esem = self.esem

        def run(engname, e):
            for fn, waits, signal, dinc in ops[engname]:
                for sem, val in waits:
                    e.wait_ge(sem, val)
                if fn is None:
                    continue
                ins = fn(e)
                if dinc is not None:
                    ins.then_inc(dinc[0], dinc[1])
                elif signal:
                    ins.then_inc(esem[engname], 1)

        with nc.Block() as block:
            @block.sync
            def _(e):
                run("sp", e)

            @block.scalar
            def _(e):
                run("act", e)

            @block.tensor
            def _(e):
                run("pe", e)

            @block.vector
            def _(e):
                run("dve", e)

            @block.gpsimd
            def _(e):
                run("pool", e)

    def close(self):
        for cm in reversed(self._ctx):
            cm.__exit__(None, None, None)


def rap(base, dims):
    return bass.AP(base.tensor, base.offset, [list(base.ap[0])] + [list(d) for d in dims])


PARAM_SEGS = [
    ("cond", "cond", "g (n p) -> (g n) p", 32),
    ("st", "st", "d (n p) -> (d n) p", 32),
    ("n1g", "norm1_g", "l (n p) -> (l n) p", 32),
    ("n2g", "norm2_g", "l (n p) -> (l n) p", 32),
    ("fg", "final_g", "(n p) -> n p", 16),
    ("bada", "b_ada", "l (n p) -> (l n) p", 192),
    ("rcw", "rg_conv_w", "a k (n p) -> (a k n) p", 64),
    ("rcb", "rg_conv_b", "a (n p) -> (a n) p", 16),
    ("rba", "rg_b_a", "a d (n p) -> (a d n) p", 32),
    ("rbx", "rg_b_x", "a d (n p) -> (a d n) p", 32),
    ("lam", "rg_lam", "a d (n p) -> (a d n) p", 32),
    ("cbi", "cm_b_in", "a (n p) -> (a n) p", 32),
    ("clg", "cm_ln_g", "a (n p) -> (a n) p", 16),
    ("fcw", "ffn_conv_w", "l k (n p) -> (l k n) p", 528),
    ("fcb", "ffn_conv_b", "l (n p) -> (l n) p", 176),
]
NPROWS = sum(s[3] for s in PARAM_SEGS)
NPBLK = (NPROWS + 127) // 128

IN_SHAPES = {
    "xp": [T, D], "xs": [T, D], "st": [2, D], "cond": [2, D],
    "norm1_g": [2, D], "norm2_g": [2, D], "w_ada": [2, D, 6 * D], "b_ada": [2, 6 * D],
    "rg_w_in": [1, D, 2 * D], "rg_conv_w": [1, 4, D], "rg_conv_b": [1, D],
    "rg_w_a": [1, 2, 16, 128, 128], "rg_b_a": [1, 2, D], "rg_w_x": [1, 2, 16, 128, 128],
    "rg_b_x": [1, 2, D], "rg_lam": [1, 2, D], "rg_w_out": [1, D, D],
    "cm_w_in": [1, D, 2 * D], "cm_b_in": [1, 2 * D], "cm_ln_g": [1, D], "cm_ln_b": [1, D],
    "cm_w_s": [1, 8, 128, 128], "cm_b_s": [1, 8, 128], "cm_w_out": [1, D, D],
    "ffn_w_up": [2, D, 2 * DFF], "ffn_conv_w": [2, 3, 2 * DFF], "ffn_conv_b": [2, 2 * DFF],
    "ffn_w_down": [2, DFF, D], "final_g": [D],
}


def build(stop_after=99, groups=(0, 1), only=None):
    nc = bass.Bass("TRN2", target_bir_lowering=False)
    dr = {}
    for name, shp in IN_SHAPES.items():
        dr[name] = nc.dram_tensor(name, list(shp), F32, kind="ExternalInput").ap()
    yp = nc.dram_tensor("yp", [T, D], F32, kind="ExternalOutput").ap()
    ys = nc.dram_tensor("ys", [T, D], F32, kind="ExternalOutput").ap()
    nsd = nc.dram_tensor("ns", [8, D], F32, kind="ExternalOutput").ap()

    P = Prog(nc)
    es = contextlib.ExitStack()
    uid = [0]

    def sb(stack, name, shape, dt):
        uid[0] += 1
        return stack.enter_context(nc.sbuf_tensor("%s_%d" % (name, uid[0]), list(shape), dt))

    X = sb(es, "X", [128, NCH, T], F32)
    XN = sb(es, "XN", [128, NCH, T], BF16)
    PT = sb(es, "PT", [128, NPBLK * 128], F32)
    MOD = sb(es, "MOD", [128, 2, 96, 2], F32)
    GM = sb(es, "GM", [128, 2, 2, NCH, 2], F32)
    CL = sb(es, "CL", [128, 2, 32], F32)
    HB_ = sb(es, "HBt", [128, 2, 32], F32)
    TR = sb(es, "TR", [128, 8, 16], F32)
    TC = sb(es, "TC", [128, 8, 64], F32)
    NS = sb(es, "NS", [128, 128], F32)
    IDF = sb(es, "IDF", [128, 128], F32)
    ONB = sb(es, "ONB", [128, 128], BF16)
    ONF = sb(es, "ONF", [128, 128], F32)
    SC = sb(es, "SC", [128, NCH, 2], BF16)
    NRING = 4
    RINGT = sb(es, "RINGT", [128, 4 * 4096], BF16)
    RING = [RINGT[:, i * 4096:(i + 1) * 4096] for i in range(4)]
    PS = es.enter_context(nc.psum_tensor("PS", [128, 8 * 512], F32))

    bX = [P.buf("X%d" % c) for c in range(NCH)]
    bXN = [P.buf("XN%d" % c) for c in range(NCH)]
    bPT, bMOD, bGM, bCL, bHB, bTR, bTC, bNS = (P.buf(n) for n in ("PT", "MOD", "GM", "CL", "HB", "TR", "TC", "NS"))
    bIDF, bONB, bONF, bSC = (P.buf(n) for n in ("IDF", "ONB", "ONF", "SC"))
    bWG = [P.buf("WG%d" % i) for i in range(2)]
    bRING = [P.buf("RING%d" % i) for i in range(NRING)]
    bPS = [P.buf("PS%d" % i) for i in range(8)]

    def bank(b, n=512):
        return PS[:, b * 512:b * 512 + n]

    poff = {}
    o = 0
    for name, _, _, rows in PARAM_SEGS:
        poff[name] = o
        o += rows

    def pcol(name, idx):
        return PT[:, poff[name] + idx:poff[name] + idx + 1]

    ring_state = {"i": 0, "n": 4}

    def ring_load(src_ap, view):
        s = ring_state["i"] % ring_state["n"]
        ring_state["i"] += 1
        dst = view(RING[s])
        P.dma("pool", lambda e, dst=dst, src=src_ap: e.dma_start(out=dst, in_=src), writes=[bRING[s]])
        return s

    def v_k16(tile, ncols):
        return tile[:, 0:16 * ncols].rearrange("p (k c) -> p k c", k=16)

    prefetched = {}

    def load_wcols(w2d, c0, ncols=256):
        key = (w2d.tensor.name, w2d.offset, c0, ncols)
        if key in prefetched:
            return prefetched.pop(key)
        src = w2d[:, c0:c0 + ncols].rearrange("(k p) c -> p k c", p=128)
        s = ring_load(src, lambda t: v_k16(t, ncols))
        return s, v_k16(RING[s], ncols)

    pair_pre = {}

    def load_pair(w2d, c0, pair):
        key = (w2d.tensor.name, w2d.offset, c0, pair)
        dst = RINGT[:, pair * 8192:(pair + 1) * 8192].rearrange("p (k c) -> p k c", k=16)
        if key in pair_pre:
            pair_pre.pop(key)
            return dst
        src = w2d[:, c0:c0 + 512].rearrange("(k p) c -> p k c", p=128)
        P.dma("pool", lambda e, dst=dst, src=src: e.dma_start(out=dst, in_=src),
              writes=[bRING[2 * pair], bRING[2 * pair + 1]])
        return dst

    def prefetch_pair(w2d, c0, pair):
        load_pair(w2d, c0, pair)
        pair_pre[(w2d.tensor.name, w2d.offset, c0, pair)] = True

    def prefetch_wcols(w2d, c0, ncols=256):
        key = (w2d.tensor.name, w2d.offset, c0, ncols)
        assert key not in prefetched
        r_ = load_wcols(w2d, c0, ncols)
        prefetched[key] = r_

    NDEF = 15
    NSET1 = 48 - NDEF

    def mod_window_issue(stack, blks):
        items = []
        for i, blk in enumerate(blks):
            t = sb(stack, "DW%d" % i, [128, 4096], BF16)
            b = P.buf("DW%d" % i)
            src = dr["w_ada"][1][:, blk * 256:(blk + 1) * 256].rearrange("(k p) c -> p k c", p=128)
            dst = v_k16(t, 256)
            P.dma("pool", lambda e, dst=dst, src=src: e.dma_start(out=dst, in_=src), writes=[b])
            items.append((b, dst, blk))
        return items

    def mod_window_consume(items, last=False):
        for (b, wv, blk) in items:
            for n2 in range(2):
                n = blk * 2 + n2
                for kc in range(16):
                    P.op("pe", lambda e, wv=wv, n2=n2, kc=kc, n=n:
                         e.matmul(bank(6)[:, 2 * n:2 * n + 2], lhsT=wv[:, kc, n2 * 128:(n2 + 1) * 128],
                                  rhs=SC[:, kc, :], start=(kc == 0), stop=(kc == 15)),
                         reads=[b, bSC], writes=[bPS[6]], signal=(kc == 15))
        n0 = items[0][2] * 2
        cnt = len(items) * 2
        for g in range(2):
            P.op("dve", lambda e, g=g, n0=n0, cnt=cnt:
                 e.tensor_copy(out=MOD[:, 1, n0:n0 + cnt, g], in_=rap(bank(6)[:, 2 * n0 + g:2 * n0 + g + 1], [[2, cnt]])),
                 reads=[bPS[6]], writes=[bMOD])
        if last:
            for g in range(2):
                P.op("dve", lambda e, g=g:
                     e.tensor_tensor(out=MOD[:, 1, :, g], in0=MOD[:, 1, :, g],
                                     in1=PT[:, poff["bada"] + 96:poff["bada"] + 192], op=ALU.add),
                     reads=[bMOD, bPT], writes=[bMOD])
            mod_gm(1)

    def mod_gen(l, pb, loader=None, ncols=256, nblk=None):
        for blk in range(6 * D // ncols if nblk is None else nblk):
            if loader is None:
                s, wv = load_wcols(dr["w_ada"][l], blk * ncols, ncols)
                rb = bRING[s]
            else:
                rb, wv = loader(dr["w_ada"][l], blk * ncols)
            for n2 in range(ncols // 128):
                n = blk * (ncols // 128) + n2
                for kc in range(16):
                    P.op("pe", lambda e, wv=wv, n2=n2, kc=kc, n=n, pb=pb:
                         e.matmul(bank(pb)[:, 2 * n:2 * n + 2], lhsT=wv[:, kc, n2 * 128:(n2 + 1) * 128],
                                  rhs=SC[:, kc, :], start=(kc == 0), stop=(kc == 15)),
                         reads=[rb, bSC], writes=[bPS[pb]], signal=(kc == 15))
            yield

    def mod_finalize(l, pb):
        for g in range(2):
            P.op("dve", lambda e, l=l, g=g, pb=pb:
                 e.tensor_tensor(out=MOD[:, l, :, g], in0=rap(bank(pb)[:, g:g + 1], [[2, 96]]),
                                 in1=PT[:, poff["bada"] + l * 96:poff["bada"] + (l + 1) * 96], op=ALU.add),
                 reads=[bPS[pb], bPT], writes=[bMOD])
        mod_gm(l)

    def mod_gm(l):
        for which in range(2):
            nm = "n1g" if which == 0 else "n2g"
            for g in range(2):
                P.op("dve", lambda e, l=l, which=which, g=g, nm=nm:
                     e.scalar_tensor_tensor(out=GM[:, l, which, :, g],
                                            in0=MOD[:, l, (1 + 3 * which) * 16:(2 + 3 * which) * 16, g],
                                            scalar=1.0,
                                            in1=PT[:, poff[nm] + l * 16:poff[nm] + (l + 1) * 16],
                                            op0=ALU.add, op1=ALU.mult),
                     reads=[bMOD, bPT], writes=[bGM])

    with contextlib.ExitStack() as ph:
        STG = sb(ph, "STG", [128, NPBLK, 128], F32)
        bSTG = P.buf("STG")
        TMPA = sb(ph, "TMPA", [128, 512], F32)
        bTMPA = P.buf("TMPA")

        P.op("pool", lambda e: e.memset(IDF[:], 0.0), writes=[bIDF])
        P.op("pool", lambda e: e.affine_select(out=IDF[:], in_=IDF[:], compare_op=ALU.not_equal, fill=1.0,
                                               base=0, pattern=[[-1, 128]], channel_multiplier=1),
             reads=[bIDF], writes=[bIDF])
        P.op("pool", lambda e: e.memset(ONB[:], 1.0), writes=[bONB])
        P.op("pool", lambda e: e.memset(ONF[:], 1.0), writes=[bONF])
        bSTG0 = P.buf("STG0")
        P.op("dve", lambda e: e.memset(STG[:], 0.0), writes=[bSTG, bSTG0])
        P.op("dve", lambda e: e.memset(NS[:], 0.0), writes=[bNS])

        r = 0
        for name, dname, pat, rows in PARAM_SEGS:
            seg = dr[dname].rearrange(pat, p=128)
            a = 0
            while a < rows:
                blk, r0 = divmod(r + a, 128)
                n = min(rows - a, 128 - r0)
                P.dma("sp", lambda e, blk=blk, r0=r0, n=n, seg=seg, a=a:
                      e.dma_start(out=STG[r0:r0 + n, blk, :], in_=seg[a:a + n, :]), writes=[bSTG0 if blk == 0 else bSTG])
                a += n
            r += rows
        def pt_block(blk):
            b = blk % 2
            P.op("pe", lambda e, blk=blk, b=b: e.transpose(bank(b, 128), STG[:, blk, :], IDF[:]),
                 reads=[bSTG0 if blk == 0 else bSTG, bIDF], writes=[bPS[b]])
            P.op("dve", lambda e, blk=blk, b=b: e.tensor_copy(out=PT[:, blk * 128:(blk + 1) * 128], in_=bank(b, 128)),
                 reads=[bPS[b]], writes=[bPT])
        pt_block(0)

        for g in range(2):
            P.op("act", lambda e, g=g: e.activation(out=SC[:, :, g], in_=PT[:, poff["cond"] + g * 16:poff["cond"] + g * 16 + 16],
                                                    func=AF.Silu), reads=[bPT], writes=[bSC])

        ring_state["n"] = 4
        for _ in mod_gen(0, 2):
            pass
        for blk in range(1, NPBLK):
            pt_block(blk)
        mod_finalize(0, 2)
        for _ in mod_gen(1, 3, nblk=NSET1):
            pass
        for g in range(2):
            P.op("dve", lambda e, g=g:
                 e.tensor_copy(out=MOD[:, 1, 0:2 * NSET1, g], in_=rap(bank(3)[:, g:g + 1], [[2, 2 * NSET1]])),
                 reads=[bPS[3]], writes=[bMOD])
        lam = PT[:, poff["lam"]:poff["lam"] + 32]
        P.op("act", lambda e: e.activation(out=TMPA[:, 0:32], in_=lam, func=AF.Exp, scale=-1.0), reads=[bPT], writes=[bTMPA])
        P.op("act", lambda e: e.activation(out=TMPA[:, 32:64], in_=TMPA[:, 0:32], func=AF.Ln, bias=1.0, scale=1.0),
             reads=[bTMPA], writes=[bTMPA])
        P.op("dve", lambda e: e.tensor_scalar(out=CL[:, 0, :], in0=TMPA[:, 32:64], scalar1=-4.0, scalar2=None, op0=ALU.mult),
             reads=[bTMPA], writes=[bCL])
        P.op("dve", lambda e: e.tensor_scalar(out=CL[:, 1, :], in0=TMPA[:, 32:64], scalar1=-8.0, scalar2=None, op0=ALU.mult),
             reads=[bTMPA], writes=[bCL])
        P.op("dve", lambda e: e.tensor_scalar(out=HB_[:, 0, :], in0=PT[:, poff["rba"]:poff["rba"] + 32], scalar1=0.5,
                                              scalar2=None, op0=ALU.mult), reads=[bPT], writes=[bHB])
        P.op("dve", lambda e: e.tensor_scalar(out=HB_[:, 1, :], in0=PT[:, poff["rbx"]:poff["rbx"] + 32], scalar1=0.5,
                                              scalar2=None, op0=ALU.mult), reads=[bPT], writes=[bHB])

        P.barrier()
    with contextlib.ExitStack() as ph:
        ARG2 = sb(ph, "ARG2", [128, 640], F32)
        KI = sb(ph, "KI", [128, 640], I32)
        KF = sb(ph, "KF", [128, 640], F32)
        bARG2, bKI, bKF = P.buf("ARG2"), P.buf("KI"), P.buf("KF")
        OM2 = sb(ph, "OM2", [128, 4], F32)
        POS2 = sb(ph, "POS2", [128, 80], F32)
        TI2 = sb(ph, "TI2", [128, 96], I32)
        bOM2, bPOS2, bTI2 = P.buf("OM2"), P.buf("POS2"), P.buf("TI2")
        P.op("pool", lambda e: e.iota(out=TI2[:, 0:4], pattern=[[128, 4]], base=0, channel_multiplier=1), writes=[bTI2])
        P.op("pool", lambda e: e.iota(out=TI2[:, 16:32], pattern=[[1, 16]], base=0, channel_multiplier=0),
             reads=[bTI2], writes=[bTI2])
        P.op("pool", lambda e: e.iota(out=TI2[:, 32:96], pattern=[[1, 64]], base=0, channel_multiplier=0),
             reads=[bTI2], writes=[bTI2])
        P.op("dve", lambda e: e.tensor_copy(out=OM2[:], in_=TI2[:, 0:4]), reads=[bTI2], writes=[bOM2])
        P.op("dve", lambda e: e.tensor_copy(out=POS2[:], in_=TI2[:, 16:96]), reads=[bTI2], writes=[bPOS2])
        P.op("act", lambda e: e.activation(out=OM2[:], in_=OM2[:], func=AF.Exp, scale=-math.log(10000.0) / 512.0),
             reads=[bOM2], writes=[bOM2])
        A4 = ARG2[:].rearrange("p (a c n) -> p a c n", a=2, c=4)
        for cc in range(4):
            P.op("dve", lambda e, cc=cc: e.tensor_scalar(out=A4[:, 0, cc, :], in0=POS2[:], scalar1=OM2[:, cc:cc + 1],
                                                         scalar2=None, op0=ALU.mult),
                 reads=[bPOS2, bOM2], writes=[bARG2])
        P.op("dve", lambda e: e.tensor_scalar(out=A4[:, 1, :, :], in0=A4[:, 0, :, :], scalar1=math.pi / 2, scalar2=None,
                                              op0=ALU.add), reads=[bARG2], writes=[bARG2])
        P.op("dve", lambda e: e.tensor_scalar(out=KI[:], in0=ARG2[:], scalar1=1.0 / (2 * math.pi), scalar2=None,
                                              op0=ALU.mult), reads=[bARG2], writes=[bKI])
        P.op("dve", lambda e: e.tensor_copy(out=KF[:], in_=KI[:]), reads=[bKI], writes=[bKF])
        C1 = 6.28125
        C2 = 2 * math.pi - 6.28125
        P.op("dve", lambda e: e.scalar_tensor_tensor(out=ARG2[:], in0=KF[:], scalar=-C1, in1=ARG2[:], op0=ALU.mult,
                                                     op1=ALU.add), reads=[bKF, bARG2], writes=[bARG2])
        P.op("dve", lambda e: e.scalar_tensor_tensor(out=ARG2[:], in0=KF[:], scalar=-C2, in1=ARG2[:], op0=ALU.mult,
                                                     op1=ALU.add), reads=[bKF, bARG2], writes=[bARG2])
        P.op("dve", lambda e: e.tensor_scalar(out=ARG2[:], in0=ARG2[:], scalar1=-math.pi, scalar2=math.pi, op0=ALU.max,
                                              op1=ALU.min), reads=[bARG2], writes=[bARG2])
        P.op("act", lambda e: e.activation(out=ARG2[:], in_=ARG2[:], func=AF.Sin), reads=[bARG2], writes=[bARG2])
        for a in range(2):
            P.op("dve", lambda e, a=a: e.tensor_copy(out=TR[:, a * 4:(a + 1) * 4, :], in_=A4[:, a, :, 0:16]),
                 reads=[bARG2], writes=[bTR])
            P.op("dve", lambda e, a=a: e.tensor_copy(out=TC[:, a * 4:(a + 1) * 4, :], in_=A4[:, a, :, 16:80]),
                 reads=[bARG2], writes=[bTC])
        P.barrier()

    def modcol(l, k6, c, g):
        return MOD[:, l, k6 * 16 + c, g:g + 1]

    def load_group(g, defer=None):
        src = dr["xp"] if g == 0 else dr["xs"]
        with contextlib.ExitStack() as ph:
            STI = [sb(ph, "STI%d" % i, [128, D], F32) for i in range(2)]
            bSTI = [P.buf("STI%d" % i) for i in range(2)]
            ditems = mod_window_issue(ph, defer) if defer else None
            for tt in range(8):
                si = tt % 2
                P.dma("sp", lambda e, si=si, tt=tt: e.dma_start(out=STI[si][:], in_=src[tt * 128:(tt + 1) * 128, :]),
                      writes=[bSTI[si]])
                for q in range(4):
                    pb = (tt * 4 + q) % 4
                    for j in range(4):
                        c = q * 4 + j
                        P.op("pe", lambda e, si=si, c=c, pb=pb, j=j:
                             e.transpose(bank(pb)[:, j * 128:(j + 1) * 128], STI[si][:, c * 128:(c + 1) * 128], IDF[:]),
                             reads=[bSTI[si], bIDF], writes=[bPS[pb]], signal=(j == 3))
                    outap = X[:, q * 4:(q + 1) * 4, tt * 128:(tt + 1) * 128]
                    inap = bank(pb).rearrange("p (j t) -> p j t", j=4)
                    xb_ = [bX[q * 4 + j] for j in range(4)]
                    if g == 0:
                        P.op("act", lambda e, outap=outap, inap=inap: e.activation(out=outap, in_=inap, func=AF.Copy),
                             reads=[bPS[pb]], writes=xb_)
                    else:
                        o4 = rap(X[:, q * 4, tt * 128:tt * 128 + 1], [[T, 4], [64, 2], [1, 64]])
                        i4 = rap(bank(pb)[:, 0:1], [[128, 4], [64, 2], [1, 64]])
                        if q < 2:
                            t4 = rap(TR[:, q * 4, 2 * tt:2 * tt + 1], [[16, 4], [1, 2], [0, 64]])
                            rb = bTR
                        else:
                            t4 = rap(TC[:, (q - 2) * 4, 0:1], [[64, 4], [0, 2], [1, 64]])
                            rb = bTC
                        P.op("dve", lambda e, o4=o4, i4=i4, t4=t4: e.tensor_tensor(out=o4, in0=i4, in1=t4, op=ALU.add),
                             reads=[bPS[pb], rb], writes=xb_)
            if ditems:
                mod_window_consume(ditems)
            P.barrier()

    def rstd_half(ph_bufs, th):
        SQ, bSQ, RS, bRS = ph_bufs
        tok = slice(th * 512, (th + 1) * 512)
        for q in range(4):
            rd = [bX[q * 4 + j] for j in range(4)]
            if q % 2 == 0:
                P.op("act", lambda e, q=q, tok=tok: e.activation(out=SQ[q][:], in_=X[:, q * 4:(q + 1) * 4, tok], func=AF.Square),
                     reads=rd, writes=[bSQ[q]])
            else:
                P.op("dve", lambda e, q=q, tok=tok: e.tensor_tensor(out=SQ[q][:], in0=X[:, q * 4:(q + 1) * 4, tok],
                                                                    in1=X[:, q * 4:(q + 1) * 4, tok], op=ALU.mult),
                     reads=rd, writes=[bSQ[q]])
        for q in (0, 1, 2, 3):
            for j in range(4):
                P.op("pe", lambda e, q=q, j=j:
                     e.matmul(bank(7), lhsT=ONB[:], rhs=SQ[q][:, j, :],
                              start=(q == 0 and j == 0), stop=(q == 3 and j == 3)),
                     reads=[bSQ[q], bONB], writes=[bPS[7]], signal=(j == 3))
        P.op("act", lambda e, th=th: e.activation(out=RS[th][:], in_=bank(7), func=AF.Ln, scale=1.0 / D, bias=EPS_T[:, 0:1]),
             reads=[bPS[7], bEPS], writes=[bRS[th]])
        P.op("act", lambda e, th=th: e.activation(out=RS[th][:], in_=RS[th][:], func=AF.Exp, scale=-0.5),
             reads=[bRS[th]], writes=[bRS[th]])

    def norm_mod(l, which, g, nxt=None, defer=None, dlast=False):
        ring_state["n"] = 4
        if nxt == "rg":
            prefetch_wcols(dr["rg_w_in"][0], D)
            prefetch_wcols(dr["rg_w_in"][0], 0)
        elif nxt == "ffn":
            prefetch_wcols(dr["ffn_w_up"][l], 0)
            prefetch_wcols(dr["ffn_w_up"][l], DFF)
        elif nxt == "cm":
            prefetch_pair(dr["cm_w_in"][0], D, 0)
        with contextlib.ExitStack() as ph:
            SQ = [sb(ph, "SQ%d" % i, [128, 4, 512], BF16) for i in range(4)]
            bSQ = [P.buf("SQ%d" % i) for i in range(4)]
            RS = [sb(ph, "RS%d" % i, [128, 512], F32) for i in range(2)]
            bRS = [P.buf("RS%d" % i) for i in range(2)]
            TN = [sb(ph, "TN%d" % i, [128, 512], F32) for i in range(2)]
            bTN = [P.buf("TN%d" % i) for i in range(2)]
            ditems = mod_window_issue(ph, defer) if defer else None
            for th in range(2):
                tok = slice(th * 512, (th + 1) * 512)
                rstd_half((SQ, bSQ, RS, bRS), th)
                for c in range(NCH):
                    ti = c % 2
                    P.op("dve", lambda e, c=c, ti=ti, tok=tok, th=th:
                         e.scalar_tensor_tensor(out=TN[ti][:], in0=X[:, c, tok], scalar=GM[:, l, which, c, g:g + 1],
                                                in1=RS[th][:], op0=ALU.mult, op1=ALU.mult),
                         reads=[bX[c], bGM, bRS[th]], writes=[bTN[ti]])
                    P.op("act", lambda e, c=c, ti=ti, tok=tok:
                         e.activation(out=XN[:, c, tok], in_=TN[ti][:], func=AF.Identity,
                                      bias=modcol(l, 3 * which, c, g), scale=1.0),
                         reads=[bTN[ti], bMOD], writes=[bXN[c]])
            if ditems:
                mod_window_consume(ditems, last=dlast)
            P.barrier()

    def rg_mixer(g, modl=None):
        S = 4 if g == 0 else 1
        L = T // S
        l = 0
        ring_state["n"] = 4 if modl is None else 3
        dbk = 6 if modl is None else 0
        if modl is not None:
            bMRh = [P.buf("MRh%d" % i) for i in range(2)]
            mi = [0]

            def mloader(w2d, c0):
                i = mi[0] % 2
                mi[0] += 1
                src = w2d[:, c0:c0 + 128].rearrange("(k p) c -> p k c", p=128)
                dst = RING[3][:, i * 2048:(i + 1) * 2048].rearrange("p (k c) -> p k c", k=16)
                P.dma("pool", lambda e, dst=dst, src=src: e.dma_start(out=dst, in_=src), writes=[bMRh[i]])
                return bMRh[i], dst
            mg = mod_gen(modl, 7, mloader, 128)
        else:
            mg = iter(())

        def modstep(k):
            for _ in range(k):
                next(mg, None)
        with contextlib.ExitStack() as ph:
            YG = sb(ph, "YG", [128, NCH, T], BF16)
            bYG = [P.buf("YG%d" % c) for c in range(NCH)]
            WG = [sb(ph, "WG%d" % i, [128, 2, 2, 128], BF16) for i in range(2)]
            XB = sb(ph, "XB", [128, T], F32)
            XBb = sb(ph, "XBb", [128, T], BF16)
            bXB, bXBb = P.buf("XB"), P.buf("XBb")
            TT_ = [sb(ph, "T%d" % i, [128, T], F32) for i in range(5)]
            bT = [P.buf("T%d" % i) for i in range(5)]
            T1, T2, T3, T4, T5 = TT_
            win = dr["rg_w_in"][0]
            ps01 = PS[:, 0:1024]
            ps23 = PS[:, 1024:2048]
            ps45 = PS[:, 2048:3072]
            ps67 = PS[:, dbk * 512:dbk * 512 + 1024]

            def seg3(ap2d, a, n):
                return rap(ap2d[:, a:a + 1], [[L, S], [1, n]])

            XB2 = [XB, sb(ph, "XB2", [128, T], F32)]
            XBb2 = [XBb, sb(ph, "XBb2", [128, T], BF16)]
            bXB2 = [bXB, P.buf("XB2")]
            bXBb2 = [bXBb, P.buf("XBb2")]
            wslots = {}

            def loads(h):
                if h % 2 == 0:
                    wslots["x"] = load_wcols(win, D + h * 128, 256)
                    wslots["g"] = load_wcols(win, h * 128, 256)
                wi = h % 2
                P.dma("pool", lambda e, wi=wi, h=h: e.dma_start(out=WG[wi][:, 0, :, :],
                                                                in_=dr["rg_w_a"][0, :, h].rearrange("d i j -> i d j")),
                      writes=[bWG[wi]])
                P.dma("pool", lambda e, wi=wi, h=h: e.dma_start(out=WG[wi][:, 1, :, :],
                                                                in_=dr["rg_w_x"][0, :, h].rearrange("d i j -> i d j")),
                      writes=[bWG[wi]])
                return wslots["x"], wslots["g"]

            hw = {}

            def A_(h):
                (sx, wx), _ = hw[h]
                hh = h % 2
                for th in range(2):
                    for kc in range(16):
                        P.op("pe", lambda e, th=th, kc=kc, wx=wx, hh=hh:
                             e.matmul(bank(th), lhsT=wx[:, kc, hh * 128:(hh + 1) * 128], rhs=XN[:, kc, th * 512:(th + 1) * 512],
                                      start=(kc == 0), stop=(kc == 15)),
                             reads=[bRING[sx], bXN[kc]], writes=[bPS[th]], signal=(kc == 15))

            def D_(h):
                _, (sg_, wg_) = hw[h]
                hh = h % 2
                for th in range(2):
                    for kc in range(16):
                        P.op("pe", lambda e, th=th, kc=kc, wg_=wg_, hh=hh:
                             e.matmul(bank(dbk + th), lhsT=wg_[:, kc, hh * 128:(hh + 1) * 128],
                                      rhs=XN[:, kc, th * 512:(th + 1) * 512], start=(kc == 0), stop=(kc == 15)),
                             reads=[bRING[sg_], bXN[kc]], writes=[bPS[dbk + th]], signal=(kc == 15))

            def conv_a(h):
                xb, bxb = XB2[h % 2], bXB2[h % 2]
                P.op("act", lambda e, h=h, xb=xb: e.activation(out=xb[:], in_=ps01, func=AF.Identity,
                                                               scale=pcol("rcw", 2 * 16 + h), bias=pcol("rcb", h)),
                     reads=[bPS[0], bPS[1], bPT], writes=[bxb])
                for (k, dst0, src0, n) in ((0, 2, 0, L - 2), (1, 1, 0, L - 1), (3, 0, 1, L - 1)):
                    P.op("dve", lambda e, h=h, k=k, dst0=dst0, src0=src0, n=n, xb=xb:
                         e.scalar_tensor_tensor(out=seg3(xb, dst0, n), in0=seg3(ps01, src0, n),
                                                scalar=pcol("rcw", k * 16 + h), in1=seg3(xb, dst0, n),
                                                op0=ALU.mult, op1=ALU.add),
                         reads=[bPS[0], bPS[1], bPT, bxb], writes=[bxb])

            def conv_b(h):
                P.op("act", lambda e, h=h: e.activation(out=XBb2[h % 2][:], in_=XB2[h % 2][:], func=AF.Copy),
                     reads=[bXB2[h % 2]], writes=[bXBb2[h % 2]])

            def gelu_(h):
                P.op("act", lambda e, h=h: e.activation(out=YG[:, h, :], in_=ps67, func=AF.Gelu_apprx_tanh),
                     reads=[bPS[dbk], bPS[dbk + 1]], writes=[bYG[h]])

            def B_(h, d):
                wi = h % 2
                xbb, bxbb = XBb2[h % 2], bXBb2[h % 2]
                for th in range(2):
                    P.op("pe", lambda e, th=th, d=d, wi=wi, xbb=xbb:
                         e.matmul(bank(2 + th), lhsT=WG[wi][:, 0, d, :], rhs=xbb[:, th * 512:(th + 1) * 512],
                                  start=True, stop=True),
                         reads=[bWG[wi], bxbb], writes=[bPS[2 + th]])
                for th in range(2):
                    P.op("pe", lambda e, th=th, d=d, wi=wi, xbb=xbb:
                         e.matmul(bank(4 + th), lhsT=WG[wi][:, 1, d, :], rhs=xbb[:, th * 512:(th + 1) * 512],
                                  start=True, stop=True),
                         reads=[bWG[wi], bxbb], writes=[bPS[4 + th]])

            def tanh2(h, d):
                col = d * 16 + h
                P.op("act", lambda e, col=col: e.activation(out=T1[:], in_=ps23, func=AF.Tanh, scale=0.5,
                                                            bias=HB_[:, 0, col:col + 1]),
                     reads=[bPS[2], bPS[3], bHB], writes=[bT[0]])
                P.op("act", lambda e, col=col: e.activation(out=T3[:], in_=ps45, func=AF.Tanh, scale=0.5,
                                                            bias=HB_[:, 1, col:col + 1]),
                     reads=[bPS[4], bPS[5], bHB], writes=[bT[2]])

            def chain(h, d):
                col = d * 16 + h
                xb, bxb = XB2[h % 2], bXB2[h % 2]
                P.op("act", lambda e, col=col: e.activation(out=T2[:], in_=T1[:], func=AF.Exp,
                                                            scale=CL[:, 0, col:col + 1], bias=CL[:, 0, col:col + 1]),
                     reads=[bT[0], bCL], writes=[bT[1]])
                P.op("act", lambda e, col=col: e.activation(out=T1[:], in_=T1[:], func=AF.Exp,
                                                            scale=CL[:, 1, col:col + 1], bias=CL[:, 1, col:col + 1]),
                     reads=[bT[0], bCL], writes=[bT[0]])
                P.op("act", lambda e: e.activation(out=T1[:], in_=T1[:], func=AF.Sqrt, scale=-0.25, bias=Q25_T[:, 0:1]),
                     reads=[bT[0], bEPS], writes=[bT[0]])
                P.op("dve", lambda e, xb=xb: e.scalar_tensor_tensor(out=T3[:], in0=T3[:], scalar=1.0, in1=xb[:],
                                                                    op0=ALU.add, op1=ALU.mult),
                     reads=[bT[2], bxb], writes=[bT[2]])
                P.op("dve", lambda e: e.tensor_tensor(out=T3[:], in0=T3[:], in1=T1[:], op=ALU.mult),
                     reads=[bT[2], bT[0]], writes=[bT[2]])
                HD = T4 if d == 0 else T5
                bHD = bT[3] if d == 0 else bT[4]
                for s_ in range(S):
                    a0 = s_ * L
                    if d == 0:
                        oa, da, ua = HD[:, a0:a0 + L], T2[:, a0:a0 + L], T3[:, a0:a0 + L]
                    else:
                        oa = rap(HD[:, a0 + L - 1:a0 + L], [[-1, L]])
                        da = rap(T2[:, a0 + L - 1:a0 + L], [[-1, L]])
                        ua = rap(T3[:, a0 + L - 1:a0 + L], [[-1, L]])
                    init = 0.0 if g == 0 else pcol("st", col)
                    P.op("dve", lambda e, oa=oa, da=da, ua=ua, init=init:
                         e.tensor_tensor_scan(out=oa, data0=da, data1=ua, initial=init, op0=ALU.mult, op1=ALU.add),
                         reads=[bT[1], bT[2], bPT], writes=[bHD])
                if g == 0:
                    pos = (L - 1) if d == 0 else 0
                    P.op("dve", lambda e, HD=HD, pos=pos, col=col:
                         e.tensor_copy(out=rap(NS[:, col:col + 1], [[32, S]]), in_=rap(HD[:, pos:pos + 1], [[L, S]])),
                         reads=[bHD], writes=[bNS])

            hw[0] = loads(0)
            A_(0)
            conv_a(0)
            conv_b(0)
            D_(0)
            for h in range(NCH):
                nx = h + 1 < NCH
                if dbk == 0:
                    gelu_(h)
                if nx:
                    hw[h + 1] = loads(h + 1)
                    A_(h + 1)
                modstep(2)
                if dbk != 0:
                    gelu_(h)
                B_(h, 0)
                tanh2(h, 0)
                B_(h, 1)
                modstep(2)
                if nx:
                    conv_a(h + 1)
                    D_(h + 1)
                modstep(2)
                chain(h, 0)
                if nx:
                    conv_b(h + 1)
                tanh2(h, 1)
                chain(h, 1)
                P.op("dve", lambda e: e.tensor_tensor(out=T4[:], in0=T4[:], in1=T5[:], op=ALU.add),
                     reads=[bT[3], bT[4]], writes=[bT[3]])
                P.op("dve", lambda e, h=h: e.tensor_tensor(out=YG[:, h, :], in0=T4[:], in1=YG[:, h, :], op=ALU.mult),
                     reads=[bT[3], bYG[h]], writes=[bYG[h]])
            if modl is not None:
                for _ in mg:
                    pass
                mod_finalize(modl, 7)
            wout = dr["rg_w_out"][0]
            k = 0
            for oc in range(NCH):
                if oc % 2 == 0:
                    so, wo = load_wcols(wout, oc * 128, 256)
                for th in range(2):
                    pb = k % 2
                    k += 1
                    tok = slice(th * 512, (th + 1) * 512)
                    for kc in range(16):
                        P.op("pe", lambda e, kc=kc, wo=wo, oc=oc, pb=pb, tok=tok:
                             e.matmul(bank(pb), lhsT=wo[:, kc, (oc % 2) * 128:(oc % 2 + 1) * 128], rhs=YG[:, kc, tok],
                                      start=(kc == 0), stop=(kc == 15)),
                             reads=[bRING[so], bYG[kc]], writes=[bPS[pb]], signal=(kc == 15))
                    P.op("dve", lambda e, oc=oc, pb=pb, tok=tok:
                         e.scalar_tensor_tensor(out=X[:, oc, tok], in0=bank(pb), scalar=modcol(l, 2, oc, g),
                                                in1=X[:, oc, tok], op0=ALU.mult, op1=ALU.add),
                         reads=[bPS[pb], bMOD, bX[oc]], writes=[bX[oc]])
            P.barrier()
        if g == 0:
            with contextlib.ExitStack() as ph:
                NSo = sb(ph, "NSo", [128, 128], F32)
                bNSo = P.buf("NSo")
                P.op("pe", lambda e: e.transpose(bank(0, 128), NS[:], IDF[:]), reads=[bNS, bIDF], writes=[bPS[0]])
                P.op("dve", lambda e: e.tensor_copy(out=NSo[:], in_=bank(0, 128)), reads=[bPS[0]], writes=[bNSo])
                P.dma("sp", lambda e: e.dma_start(out=nsd.rearrange("r (h p) -> (r h) p", p=128), in_=NSo[:]),
                      reads=[bNSo])
                P.barrier()

    def cm_mixer(g):
        l = 1
        ring_state["n"] = 4
        win = dr["cm_w_in"][0]
        with contextlib.ExitStack() as ph:
            bV = [P.buf("V%d" % c) for c in range(NCH)]
            WST = sb(ph, "WST", [128, 8, 128], BF16)
            BT = sb(ph, "BT", [128, NCH, 128], F32)
            bWST, bWSF, bBT = P.buf("WST"), P.buf("WSF"), P.buf("BT")
            ROW = sb(ph, "ROW", [33, 1024], F32)
            bROW = P.buf("ROW")
            BH = sb(ph, "BH", [33, D], BF16)
            bBH = P.buf("BH")
            ST1 = sb(ph, "ST1", [128, 2, 8, 8], F32)
            bST = P.buf("ST1")
            STT = sb(ph, "STT", [128, 6, 8], F32)
            bSTT = P.buf("STT")
            GS = [sb(ph, "GS%d" % i, [128, 512], F32) for i in range(2)]
            bGS = [P.buf("GS%d" % i) for i in range(2)]
            JK = sb(ph, "JK", [128, 512], BF16)
            bJK = P.buf("JK")
            SCt = [sb(ph, "SCt%d" % i, [128, 512], F32) for i in range(2)]
            UGt = [sb(ph, "UGt%d" % i, [128, 512], F32) for i in range(2)]
            bSCt = [P.buf("SCt%d" % i) for i in range(2)]
            bUGt = [P.buf("UGt%d" % i) for i in range(2)]
            CMS = float(os.environ.get("CM_STOP", "9"))
            if CMS <= 0:
                return
            with contextlib.ExitStack() as ph2:
                WS0 = sb(ph2, "WS0", [128, 8, 128], F32)
                WSF = sb(ph2, "WSF", [128, 8, 128], F32)
                LNB = sb(ph2, "LNB", [128, D], F32)
                bWS0, bLNB = P.buf("WS0"), P.buf("LNB")
                P.dma("sp", lambda e: e.dma_start(out=WS0[:], in_=dr["cm_w_s"][0].rearrange("g p q -> p g q")), writes=[bWS0])
                P.dma("sp", lambda e: e.dma_start(out=LNB[:], in_=dr["cm_ln_b"][0:1, :].partition_broadcast(128)),
                      writes=[bLNB])
                TB = sb(ph2, "TB", [33, D], F32)
                bTB = P.buf("TB")
                P.op("dve", lambda e: e.memset(BH[:], 0.0), writes=[bBH])
                P.dma("sp", lambda e: e.dma_start(out=TB[0:1, :], in_=dr["cm_b_in"][0:1, D:2 * D]), writes=[bTB])
                P.dma("sp", lambda e: e.dma_start(out=TB[32:33, :], in_=dr["cm_b_in"][0:1, D:2 * D]), writes=[bTB])
                P.op("dve", lambda e: e.tensor_copy(out=BH[0:1, :], in_=TB[0:1, :]), reads=[bTB, bBH], writes=[bBH])
                P.op("dve", lambda e: e.tensor_copy(out=BH[32:33, :], in_=TB[32:33, :]), reads=[bTB, bBH], writes=[bBH])
                P.op("dve", lambda e: e.tensor_tensor(out=TB[32:33, :], in0=TB[32:33, :], in1=BH[32:33, :], op=ALU.subtract),
                     reads=[bTB, bBH], writes=[bTB])
                P.op("dve", lambda e: e.tensor_copy(out=BH[32:33, :], in_=TB[32:33, :]), reads=[bTB, bBH], writes=[bBH])
                P.dma("sp", lambda e: e.dma_start(out=ROW[32:33, 0:1024], in_=dr["cm_b_s"][0:1].rearrange("a g p -> a (g p)")),
                      writes=[bROW])
                if CMS <= 0.3:
                    P.barrier()
                    return
                for gi in range(8):
                    pb = gi % 2
                    P.op("pe", lambda e, gi=gi, pb=pb: e.transpose(bank(pb, 128), WS0[:, gi, :], IDF[:]),
                         reads=[bWS0, bIDF], writes=[bPS[pb]])
                    if CMS <= 0.4:
                        continue
                    P.op("dve", lambda e, gi=gi, pb=pb: e.tensor_copy(out=WSF[:, gi, :], in_=bank(pb, 128)),
                         reads=[bPS[pb]], writes=[bWSF])
                    if CMS <= 0.5:
                        continue
                    P.op("dve", lambda e, gi=gi: e.tensor_copy(out=WST[:, gi, :], in_=WSF[:, gi, :]),
                         reads=[bWSF], writes=[bWST])
                if CMS <= 0.6:
                    P.barrier()
                    return
                for c in range(NCH):
                    gi = c // 2
                    pb = 2 + c % 2
                    P.op("pe", lambda e, c=c, gi=gi, pb=pb:
                         e.matmul(bank(pb, 128), lhsT=LNB[:, c * 128:(c + 1) * 128], rhs=WSF[:, gi, :], start=True, stop=False),
                         reads=[bLNB, bWSF], writes=[bPS[pb]], signal=False)
                    P.op("pe", lambda e, c=c, gi=gi, pb=pb:
                         e.matmul(bank(pb, 128), lhsT=ONF[32:33, :], rhs=ROW[32:33, gi * 128:(gi + 1) * 128], start=False, stop=True),
                         reads=[bONF, bROW], writes=[bPS[pb]])
                    P.op("act", lambda e, c=c, pb=pb: e.activation(out=BT[:, c, :], in_=bank(pb, 128), func=AF.Copy),
                         reads=[bPS[pb]], writes=[bBT])
                P.barrier()
            if CMS <= 1:
                return
            V = sb(ph, "V", [128, 8, D], BF16)
            k = 0
            for vp in range(4):
                pair = vp % 2
                wv = load_pair(win, D + vp * 512, pair)
                rbs = [bRING[2 * pair], bRING[2 * pair + 1]]
                for tt in range(8):
                    pb = k % 2
                    k += 1
                    for kc in range(16):
                        P.op("pe", lambda e, kc=kc, tt=tt, wv=wv, pb=pb:
                             e.matmul(bank(pb), lhsT=XN[:, kc, tt * 128:(tt + 1) * 128], rhs=wv[:, kc, :],
                                      start=(kc == 0), stop=False),
                             reads=rbs + [bXN[kc]], writes=[bPS[pb]], signal=False)
                    P.op("pe", lambda e, vp=vp, pb=pb:
                         e.matmul(bank(pb), lhsT=ONB[0:33, :], rhs=BH[0:33, vp * 512:(vp + 1) * 512], start=False, stop=True),
                         reads=[bONB, bBH], writes=[bPS[pb]])
                    P.op("act", lambda e, pb=pb, tt=tt, vp=vp:
                         e.activation(out=GS[pb][:], in_=bank(pb), func=AF.Gelu_apprx_tanh,
                                      accum_out=ST1[:, 0, tt, vp:vp + 1]),
                         reads=[bPS[pb]], writes=[bGS[pb], bST])
                    P.op("act", lambda e, pb=pb, tt=tt, vp=vp:
                         e.activation(out=JK[:], in_=GS[pb][:], func=AF.Square, accum_out=ST1[:, 1, tt, vp:vp + 1]),
                         reads=[bGS[pb]], writes=[bJK, bST])
                    P.op("dve", lambda e, pb=pb, tt=tt, vp=vp:
                         e.tensor_copy(out=V[:, tt, vp * 512:(vp + 1) * 512], in_=GS[pb][:]),
                         reads=[bGS[pb]], writes=[bV[4 * vp + q] for q in range(4)])
            if CMS <= 2:
                P.barrier()
                return
            MU, EX2, VAR, RSTD, NB, TMPs = (STT[:, i, :] for i in range(6))
            P.op("dve", lambda e: e.tensor_reduce(out=MU, in_=ST1[:, 0, :, 0:4], axis=AX.X, op=ALU.add), reads=[bST], writes=[bSTT])
            P.op("dve", lambda e: e.tensor_reduce(out=EX2, in_=ST1[:, 1, :, 0:4], axis=AX.X, op=ALU.add), reads=[bST], writes=[bSTT])
            P.op("dve", lambda e: e.tensor_scalar(out=MU, in0=MU, scalar1=1.0 / D, scalar2=None, op0=ALU.mult),
                 reads=[bSTT], writes=[bSTT])
            P.op("dve", lambda e: e.tensor_tensor(out=TMPs, in0=MU, in1=MU, op=ALU.mult), reads=[bSTT], writes=[bSTT])
            P.op("dve", lambda e: e.scalar_tensor_tensor(out=VAR, in0=EX2, scalar=1.0 / D, in1=TMPs, op0=ALU.mult,
                                                         op1=ALU.subtract), reads=[bSTT], writes=[bSTT])
            P.op("act", lambda e: e.activation(out=RSTD, in_=VAR, func=AF.Sqrt, scale=1.0, bias=LNE_T[:, 0:1]),
                 reads=[bSTT, bEPS], writes=[bSTT])
            P.op("dve", lambda e: e.reciprocal(out=RSTD, in_=RSTD), reads=[bSTT], writes=[bSTT])
            P.op("dve", lambda e: e.scalar_tensor_tensor(out=NB, in0=MU, scalar=-1.0, in1=RSTD, op0=ALU.mult, op1=ALU.mult),
                 reads=[bSTT], writes=[bSTT])
            for tt in range(8):
                P.op("dve", lambda e, tt=tt: e.tensor_scalar(out=V[:, tt, :], in0=V[:, tt, :], scalar1=STT[:, 3, tt:tt + 1],
                                                             scalar2=STT[:, 4, tt:tt + 1], op0=ALU.mult, op1=ALU.add),
                     reads=bV + [bSTT], writes=bV)
            if CMS <= 3:
                P.barrier()
                return
            for c in range(NCH):
                gi = c // 2
                if c % 2 == 0:
                    su, wu = load_wcols(win, c * 128, 256)
                for tt in range(8):
                    pb = 2 + tt // 4
                    P.op("pe", lambda e, c=c, tt=tt, gi=gi, pb=pb:
                         e.matmul(bank(pb)[:, (tt % 4) * 128:(tt % 4 + 1) * 128], lhsT=V[:, tt, c * 128:(c + 1) * 128],
                                  rhs=WST[:, gi, :], start=True, stop=True),
                         reads=[bV[c], bWST], writes=[bPS[pb]], signal=(tt % 4 == 3))
                for th in range(2):
                    for kc in range(16):
                        P.op("pe", lambda e, th=th, kc=kc, wu=wu, c=c:
                             e.matmul(bank(4 + th), lhsT=wu[:, kc, (c % 2) * 128:(c % 2 + 1) * 128],
                                      rhs=XN[:, kc, th * 512:(th + 1) * 512], start=(kc == 0), stop=(kc == 15)),
                             reads=[bRING[su], bXN[kc]], writes=[bPS[4 + th]], signal=(kc == 15))
                for th in range(2):
                    P.op("dve", lambda e, th=th, c=c:
                         e.scalar_tensor_tensor(out=SCt[th][:].rearrange("p (a b) -> p a b", a=4),
                                                in0=bank(2 + th).rearrange("p (a b) -> p a b", a=4),
                                                scalar=pcol("clg", c),
                                                in1=rap(BT[:, c, 0:1], [[0, 4], [1, 128]]),
                                                op0=ALU.mult, op1=ALU.add),
                         reads=[bPS[2 + th], bPT, bBT], writes=[bSCt[th]])
                    P.op("act", lambda e, th=th, c=c:
                         e.activation(out=UGt[th][:], in_=bank(4 + th), func=AF.Gelu_apprx_tanh, bias=pcol("cbi", c), scale=1.0),
                         reads=[bPS[4 + th], bPT], writes=[bUGt[th]])
                    P.op("dve", lambda e, th=th, c=c:
                         e.tensor_tensor(out=V[:, 4 * th:4 * th + 4, c * 128:(c + 1) * 128],
                                         in0=UGt[th][:].rearrange("p (a b) -> p a b", a=4),
                                         in1=SCt[th][:].rearrange("p (a b) -> p a b", a=4), op=ALU.mult),
                         reads=[bUGt[th], bSCt[th]], writes=[bV[c]])
            if CMS <= 4:
                P.barrier()
                return
            wout = dr["cm_w_out"][0]
            k = 0
            for oc in range(NCH):
                if oc % 2 == 0:
                    so, wo = load_wcols(wout, oc * 128, 256)
                for th in range(2):
                    pb = 6 + k % 2
                    k += 1
                    tok = slice(th * 512, (th + 1) * 512)
                    for kc in range(16):
                        P.op("pe", lambda e, kc=kc, wo=wo, oc=oc, pb=pb, th=th:
                             e.matmul(bank(pb), lhsT=wo[:, kc, (oc % 2) * 128:(oc % 2 + 1) * 128],
                                      rhs=V[:, 4 * th:4 * th + 4, kc * 128:(kc + 1) * 128],
                                      start=(kc == 0), stop=(kc == 15)),
                             reads=[bRING[so], bV[kc]], writes=[bPS[pb]], signal=(kc == 15))
                    P.op("dve", lambda e, oc=oc, pb=pb, tok=tok:
                         e.scalar_tensor_tensor(out=X[:, oc, tok], in0=bank(pb), scalar=modcol(l, 2, oc, g),
                                                in1=X[:, oc, tok], op0=ALU.mult, op1=ALU.add),
                         reads=[bPS[pb], bMOD, bX[oc]], writes=[bX[oc]])
            P.barrier()

    def ffn(l, g):
        S = 4 if g == 0 else 1
        L = T // S
        wup = dr["ffn_w_up"][l]
        wdn = dr["ffn_w_down"][l]
        NFB = DFF // 256
        with contextlib.ExitStack() as ph:
            DR = [sb(ph, "DR%d" % i, [128, 4096], BF16) for i in range(4)]
            bDR = [P.buf("DR%d" % i) for i in range(4)]
            HBf = [sb(ph, "HBf%d" % i, [128, 4, T], BF16) for i in range(2)]
            bHBf = [P.buf("HBf%d" % i) for i in range(2)]
            GC = [sb(ph, "GC%d" % i, [128, T], F32) for i in range(2)]
            VC = [sb(ph, "VC%d" % i, [128, T], F32) for i in range(2)]
            bGC = [P.buf("GC%d" % i) for i in range(2)]
            bVC = [P.buf("VC%d" % i) for i in range(2)]
            ring_state["n"] = 4
            ps01 = PS[:, 0:1024]
            ps23 = PS[:, 1024:2048]

            def seg3(ap2d, a, n):
                return rap(ap2d[:, a:a + 1], [[L, S], [1, n]])

            def conv_evac(psv, pbs, dst, bdst, fc):
                P.op("act", lambda e: e.activation(out=dst[:], in_=psv, func=AF.Identity,
                                                   scale=pcol("fcw", (l * 3 + 1) * 88 + fc), bias=pcol("fcb", l * 88 + fc)),
                     reads=pbs + [bPT], writes=[bdst])
                for (kk, dst0, src0) in ((0, 1, 0), (2, 0, 1)):
                    P.op("dve", lambda e, kk=kk, dst0=dst0, src0=src0:
                         e.scalar_tensor_tensor(out=seg3(dst, dst0, L - 1), in0=seg3(psv, src0, L - 1),
                                                scalar=pcol("fcw", (l * 3 + kk) * 88 + fc), in1=seg3(dst, dst0, L - 1),
                                                op0=ALU.mult, op1=ALU.add),
                         reads=pbs + [bPT, bdst], writes=[bdst])

            def up_gen(fb):
                sg_, wg_ = load_wcols(wup, fb * 256, 256)
                sv_, wv_ = load_wcols(wup, DFF + fb * 256, 256)
                src = wdn[fb * 256:(fb + 1) * 256, :].rearrange("(k p) c -> p k c", p=128)
                di = fb % 4
                ddst = DR[di][:, 0:4096].rearrange("p (k c) -> p k c", k=2)
                P.dma("pool", lambda e, ddst=ddst, src=src: e.dma_start(out=ddst, in_=src), writes=[bDR[di]])
                hb = (fb // 2) % 2
                for j in range(2):
                    fc = 2 * fb + j
                    ji = j % 2
                    hk = (fb % 2) * 2 + j
                    for (w_, s_, b0, isg) in ((wg_, sg_, 0, True), (wv_, sv_, 2, False)):
                        for th in range(2):
                            for kc in range(16):
                                P.op("pe", lambda e, th=th, kc=kc, w_=w_, j=j, b0=b0:
                                     e.matmul(bank(b0 + th), lhsT=w_[:, kc, j * 128:(j + 1) * 128],
                                              rhs=XN[:, kc, th * 512:(th + 1) * 512], start=(kc == 0), stop=(kc == 15)),
                                     reads=[bRING[s_], bXN[kc]], writes=[bPS[b0 + th]], signal=(kc == 15))
                            if th == 1:
                                if isg:
                                    conv_evac(ps01, [bPS[0], bPS[1]], GC[ji], bGC[ji], fc)
                                else:
                                    conv_evac(ps23, [bPS[2], bPS[3]], VC[ji], bVC[ji], NFC + fc)
                                    P.op("act", lambda e, ji=ji: e.activation(out=GC[ji][:], in_=GC[ji][:], func=AF.Silu),
                                         reads=[bGC[ji]], writes=[bGC[ji]])
                                    P.op("dve", lambda e, ji=ji, hb=hb, hk=hk:
                                         e.tensor_tensor(out=HBf[hb][:, hk, :], in0=GC[ji][:], in1=VC[ji][:], op=ALU.mult),
                                         reads=[bGC[ji], bVC[ji]], writes=[bHBf[hb]])
                            yield

            kd = [0]

            def down_gen(pi):
                hb = pi % 2
                n = 0
                for oc in range(NCH):
                    for th in range(2):
                        pb = 4 + kd[0] % 4
                        kd[0] += 1
                        tok = slice(th * 512, (th + 1) * 512)
                        for hk in range(4):
                            di = (2 * pi + hk // 2) % 4
                            wd_ = DR[di][:, 0:4096].rearrange("p (k c) -> p k c", k=2)
                            P.op("pe", lambda e, hk=hk, wd_=wd_, oc=oc, pb=pb, tok=tok, hb=hb:
                                 e.matmul(bank(pb), lhsT=wd_[:, hk % 2, oc * 128:(oc + 1) * 128], rhs=HBf[hb][:, hk, tok],
                                          start=(hk == 0), stop=(hk == 3)),
                                 reads=[bDR[di], bHBf[hb]], writes=[bPS[pb]], signal=(hk == 3))
                        P.op("dve", lambda e, oc=oc, pb=pb, tok=tok:
                             e.scalar_tensor_tensor(out=X[:, oc, tok], in0=bank(pb), scalar=modcol(l, 5, oc, g),
                                                    in1=X[:, oc, tok], op0=ALU.mult, op1=ALU.add),
                             reads=[bPS[pb], bMOD, bX[oc]], writes=[bX[oc]])
                        n += 1
                        if n % 2 == 0:
                            yield

            def drain(gen):
                for _ in gen:
                    pass

            def chain2(fb0):
                for fb in (fb0, fb0 + 1):
                    if fb < NFB:
                        for _ in up_gen(fb):
                            yield

            drain(chain2(0))
            for pi in range(NFB // 2):
                ug = chain2(2 * pi + 2)
                dg = down_gen(pi)
                while True:
                    a_ = next(ug, "end")
                    b_ = next(dg, "end")
                    if a_ == "end" and b_ == "end":
                        break
            P.barrier()

    def final(g):
        dst = yp if g == 0 else ys
        with contextlib.ExitStack() as ph:
            FGB = sb(ph, "FGB", [128, D], F32)
            bFGB = P.buf("FGB")
            OST = [sb(ph, "OST%d" % i, [128, D], F32) for i in range(2)]
            bOST = [P.buf("OST%d" % i) for i in range(2)]
            SSQ = sb(ph, "SSQ", [128, 8, 4], F32)
            bSSQ = P.buf("SSQ")
            RSTD = sb(ph, "RSTDF", [128, 8], F32)
            bRSTD = P.buf("RSTDF")
            JF = sb(ph, "JF", [128, 512], BF16)
            bJF = P.buf("JF")
            P.dma("sp", lambda e: e.dma_start(out=FGB[:], in_=dr["final_g"].rearrange("(a n) -> a n", a=1).partition_broadcast(128)),
                  writes=[bFGB])
            for tt in range(8):
                oi = tt % 2
                base = (tt % 2) * 4
                for q in range(4):
                    pb = base + q
                    for j in range(4):
                        c = q * 4 + j
                        P.op("pe", lambda e, c=c, pb=pb, j=j, tt=tt:
                             e.transpose(bank(pb)[:, j * 128:(j + 1) * 128], X[:, c, tt * 128:(tt + 1) * 128], IDF[:]),
                             reads=[bX[c], bIDF], writes=[bPS[pb]], signal=(j == 3))
                    P.op("act", lambda e, pb=pb, tt=tt, q=q:
                         e.activation(out=JF[:], in_=bank(pb), func=AF.Square, accum_out=SSQ[:, tt, q:q + 1]),
                         reads=[bPS[pb]], writes=[bJF, bSSQ])
                P.op("dve", lambda e, tt=tt: e.tensor_reduce(out=RSTD[:, tt:tt + 1], in_=SSQ[:, tt, :], axis=AX.X, op=ALU.add),
                     reads=[bSSQ], writes=[bRSTD])
                P.op("act", lambda e, tt=tt: e.activation(out=RSTD[:, tt:tt + 1], in_=RSTD[:, tt:tt + 1], func=AF.Sqrt,
                                                          scale=1.0 / D, bias=EPS_T[:, 0:1]),
                     reads=[bRSTD, bEPS], writes=[bRSTD])
                P.op("dve", lambda e, tt=tt: e.reciprocal(out=RSTD[:, tt:tt + 1], in_=RSTD[:, tt:tt + 1]),
                     reads=[bRSTD], writes=[bRSTD])
                for q in range(4):
                    pb = base + q
                    P.op("dve", lambda e, oi=oi, q=q, pb=pb, tt=tt:
                         e.scalar_tensor_tensor(out=OST[oi][:, q * 512:(q + 1) * 512], in0=bank(pb),
                                                scalar=RSTD[:, tt:tt + 1], in1=FGB[:, q * 512:(q + 1) * 512],
                                                op0=ALU.mult, op1=ALU.mult),
                         reads=[bPS[pb], bRSTD, bFGB], writes=[bOST[oi]])
                P.dma("sp", lambda e, oi=oi, tt=tt: e.dma_start(out=dst[tt * 128:(tt + 1) * 128, :], in_=OST[oi][:]),
                      reads=[bOST[oi]])
            P.barrier()

    EPS_T = sb(es, "EPS_T", [128, 1], F32)
    Q25_T = sb(es, "Q25_T", [128, 1], F32)
    LNE_T = sb(es, "LNE_T", [128, 1], F32)
    bEPS = P.buf("EPSC")
    P.op("pool", lambda e: e.memset(EPS_T[:], EPS), writes=[bEPS])
    P.op("pool", lambda e: e.memset(Q25_T[:], 0.25), writes=[bEPS])
    P.op("pool", lambda e: e.memset(LNE_T[:], LN_EPS), writes=[bEPS])
    P.barrier()

    def want(stage):
        return (stage < stop_after) if only is None else (stage in only)

    for g in groups:
        first = (g == groups[0])
        load_group(g, defer=(list(range(NSET1, NSET1 + 5)) if first else None))
        if want(0):
            norm_mod(0, 0, g, nxt="rg", defer=(list(range(NSET1 + 5, NSET1 + 10)) if first else None))
            rg_mixer(g)
        if want(1):
            norm_mod(0, 1, g, nxt="ffn", defer=(list(range(NSET1 + 10, 48)) if first else None), dlast=first)
            ffn(0, g)
        if want(2):
            norm_mod(1, 0, g, nxt="cm")
            cm_mixer(g)
        if want(3):
            norm_mod(1, 1, g, nxt="ffn")
            ffn(1, g)
        final(g)

    P.emit()
    P.close()
    es.close()
    return nc


_NC_CACHE = {}


def _get_nc(stop_after=99):
    if stop_after not in _NC_CACHE:
        _NC_CACHE[stop_after] = build(stop_after)
    return _NC_CACHE[stop_after]


def make_in_maps(inputs):
    f = lambda a: np.ascontiguousarray(np.asarray(a, dtype=np.float32))
    shared = {k: f(inputs[k]) for k in IN_SHAPES if k not in ("xp", "xs", "st", "cond")}
    xp = f(inputs["x_prompt"])
    xs = f(inputs["x_sample"])
    st = f(inputs["state_rglru"])
    c = f(inputs["c"])
    cctx = f(inputs["c_ctx"])
    maps = []
    for i in range(NCORES):
        m = dict(shared)
        m["xp"] = np.ascontiguousarray(xp[4 * i:4 * i + 4].reshape(T, D))
        m["xs"] = np.ascontiguousarray(xs[i])
        m["st"] = np.ascontiguousarray(st[i, 0])
        m["cond"] = np.ascontiguousarray(np.stack([cctx, c[i]], axis=0))
        maps.append(m)
    return maps


def kernel(**inputs):
    nc = _get_nc()
    maps = make_in_maps(inputs)
    res = run_bass_kernel_spmd(nc, maps, core_ids=list(range(NCORES)))
    outs = res.results
    y_prompt = np.concatenate([np.asarray(r["yp"], dtype=np.float32).reshape(4, 256, D) for r in outs], axis=0)
    y_sample = np.stack([np.asarray(r["ys"], dtype=np.float32) for r in outs], axis=0)
    new_state = np.concatenate([np.asarray(r["ns"], dtype=np.float32).reshape(4, 1, 2, D) for r in outs], axis=0)
    return (y_prompt, y_sample, new_state)
```

```python
import contextlib
import math
import os
import numpy as np
import concourse.bass as bass
import concourse.mybir as mybir
from concourse.bass_utils import run_bass_kernel_spmd

F32 = mybir.dt.float32
BF16 = mybir.dt.bfloat16
I32 = mybir.dt.int32
AF = mybir.ActivationFunctionType
ALU = mybir.AluOpType
AX = mybir.AxisListType

ENGS = ("sp", "act", "pe", "dve", "pool")
NCORES = 8
D = 2048
NCH = 16
T = 1024
DFF = 5632
NFC = DFF // 128
EPS = 1e-6
LN_EPS = 1e-5


class Buf:
    __slots__ = ("name", "w", "r", "sem", "semval")

    def __init__(self, name):
        self.name = name
        self.w = None
        self.r = []
        self.sem = None
        self.semval = 0


class Prog:
    def __init__(self, nc):
        self.nc = nc
        self.ops = {e: [] for e in ENGS}
        self.cnt = {e: 0 for e in ENGS}
        self.seen = {e: {} for e in ENGS}
        self.esem = {}
        self._ctx = []
        for e in ("act", "pe", "dve", "pool"):
            cm = nc.semaphore("s_" + e)
            self.esem[e] = cm.__enter__()
            self._ctx.append(cm)
        self.bufs = {}
        self.dma_pending = {}

    def buf(self, name):
        b = self.bufs.get(name)
        if b is None:
            b = Buf(name)
            self.bufs[name] = b
        return b

    def _dma_sem(self, b):
        if b.sem is None:
            cm = self.nc.semaphore("d_" + b.name)
            b.sem = cm.__enter__()
            self._ctx.append(cm)
        return b.sem

    def _collect(self, eng, reads, writes):
        best = {}
        def add(t):
            kind, key, val = t
            if kind == "e" and key == "pe" and eng == "pe":
                return
            k = (kind, key)
            if best.get(k, 0) < val:
                best[k] = val
        for b in reads:
            if b.w is not None:
                add(b.w)
        for b in writes:
            if b.w is not None:
                add(b.w)
            for t in b.r:
                add(t)
        waits = []
        seen = self.seen[eng]
        for k, val in best.items():
            if seen.get(k, 0) >= val:
                continue
            seen[k] = val
            kind, key = k
            sem = self.esem[key] if kind == "e" else key.sem
            waits.append((sem, val))
        return waits

    def op(self, eng, fn, reads=(), writes=(), signal=True):
        waits = self._collect(eng, reads, writes)
        if signal:
            self.cnt[eng] += 1
            t = ("e", eng, self.cnt[eng])
        else:
            t = ("e", eng, self.cnt[eng] + 1)
        self.ops[eng].append((fn, waits, signal, None))
        for b in writes:
            b.w = t
            b.r = []
        for b in reads:
            b.r.append(t)
        return t

    def dma(self, eng, fn, reads=(), writes=(), semb=None):
        if semb is None:
            semb = writes[0] if writes else reads[0]
        sem = self._dma_sem(semb)
        waits = self._collect(eng, reads, writes)
        semb.semval += 16
        t = ("d", semb, semb.semval)
        self.dma_pending[semb] = semb.semval
        self.ops[eng].append((fn, waits, False, (sem, 16)))
        for b in writes:
            b.w = t
            b.r = []
        for b in reads:
            b.r.append(t)
        return t

    def barrier(self, engs=ENGS):
        cur = {f: self.cnt[f] for f in ("act", "pe", "dve", "pool")}
        for e in engs:
            waits = []
            seen = self.seen[e]
            for f, v in cur.items():
                if f == e or v == 0:
                    continue
                k = ("e", f)
                if seen.get(k, 0) >= v:
                    continue
                seen[k] = v
                waits.append((self.esem[f], v))
            for b, val in self.dma_pending.items():
                k = ("d", b)
                if seen.get(k, 0) >= val:
                    continue
                seen[k] = val
                waits.append((b.sem, val))
            self.ops[e].append((None, waits, False, None))
        self.dma_pending = {}

    def emit(self):
        nc = self.nc
        ops = self.ops
        esem = self.esem

        def run(engname, e):
            for fn, waits, signal, dinc in ops[engname]:
                for sem, val in waits:
                    e.wait_ge(sem, val)
                if fn is None:
                    continue
                ins = fn(e)
                if dinc is not None:
                    ins.then_inc(dinc[0], dinc[1])
                elif signal:
                    ins.then_inc(esem[engname], 1)

        with nc.Block() as block:
            @block.sync
            def _(e):
                run("sp", e)

            @block.scalar
            def _(e):
                run("act", e)

            @block.tensor
            def _(e):
                run("pe", e)

            @block.vector
            def _(e):
                run("dve", e)

            @block.gpsimd
            def _(e):
                run("pool", e)

    def close(self):
        for cm in reversed(self._ctx):
            cm.__exit__(None, None, None)


def rap(base, dims):
    return bass.AP(base.tensor, base.offset, [list(base.ap[0])] + [list(d) for d in dims])


PARAM_SEGS = [
    ("cond", "cond", "g (n p) -> (g n) p", 32),
    ("st", "st", "d (n p) -> (d n) p", 32),
    ("n1g", "norm1_g", "l (n p) -> (l n) p", 32),
    ("n2g", "norm2_g", "l (n p) -> (l n) p", 32),
    ("fg", "final_g", "(n p) -> n p", 16),
    ("bada", "b_ada", "l (n p) -> (l n) p", 192),
    ("rcw", "rg_conv_w", "a k (n p) -> (a k n) p", 64),
    ("rcb", "rg_conv_b", "a (n p) -> (a n) p", 16),
    ("rba", "rg_b_a", "a d (n p) -> (a d n) p", 32),
    ("rbx", "rg_b_x", "a d (n p) -> (a d n) p", 32),
    ("lam", "rg_lam", "a d (n p) -> (a d n) p", 32),
    ("cbi", "cm_b_in", "a (n p) -> (a n) p", 32),
    ("clg", "cm_ln_g", "a (n p) -> (a n) p", 16),
    ("fcw", "ffn_conv_w", "l k (n p) -> (l k n) p", 528),
    ("fcb", "ffn_conv_b", "l (n p) -> (l n) p", 176),
]
NPROWS = sum(s[3] for s in PARAM_SEGS)
NPBLK = (NPROWS + 127) // 128

IN_SHAPES = {
    "xp": [T, D], "xs": [T, D], "st": [2, D], "cond": [2, D],
    "norm1_g": [2, D], "norm2_g": [2, D], "w_ada": [2, D, 6 * D], "b_ada": [2, 6 * D],
    "rg_w_in": [1, D, 2 * D], "rg_conv_w": [1, 4, D], "rg_conv_b": [1, D],
    "rg_w_a": [1, 2, 16, 128, 128], "rg_b_a": [1, 2, D], "rg_w_x": [1, 2, 16, 128, 128],
    "rg_b_x": [1, 2, D], "rg_lam": [1, 2, D], "rg_w_out": [1, D, D],
    "cm_w_in": [1, D, 2 * D], "cm_b_in": [1, 2 * D], "cm_ln_g": [1, D], "cm_ln_b": [1, D],
    "cm_w_s": [1, 8, 128, 128], "cm_b_s": [1, 8, 128], "cm_w_out": [1, D, D],
    "ffn_w_up": [2, D, 2 * DFF], "ffn_conv_w": [2, 3, 2 * DFF], "ffn_conv_b": [2, 2 * DFF],
    "ffn_w_down": [2, DFF, D], "final_g": [D],
}


def build(stop_after=99, groups=(0, 1), only=None):
    nc = bass.Bass("TRN2", target_bir_lowering=False)
    dr = {}
    for name, shp in IN_SHAPES.items():
        dr[name] = nc.dram_tensor(name, list(shp), F32, kind="ExternalInput").ap()
    yp = nc.dram_tensor("yp", [T, D], F32, kind="ExternalOutput").ap()
    ys = nc.dram_tensor("ys", [T, D], F32, kind="ExternalOutput").ap()
    nsd = nc.dram_tensor("ns", [8, D], F32, kind="ExternalOutput").ap()

    P = Prog(nc)
    es = contextlib.ExitStack()
    uid = [0]

    def sb(stack, name, shape, dt):
        uid[0] += 1
        return stack.enter_context(nc.sbuf_tensor("%s_%d" % (name, uid[0]), list(shape), dt))

    X = sb(es, "X", [128, NCH, T], F32)
    XN = sb(es, "XN", [128, NCH, T], BF16)
    PT = sb(es, "PT", [128, NPBLK * 128], F32)
    MOD = sb(es, "MOD", [128, 2, 96, 2], F32)
    GM = sb(es, "GM", [128, 2, 2, NCH, 2], F32)
    CL = sb(es, "CL", [128, 2, 32], F32)
    HB_ = sb(es, "HBt", [128, 2, 32], F32)
    TR = sb(es, "TR", [128, 8, 16], F32)
    TC = sb(es, "TC", [128, 8, 64], F32)
    NS = sb(es, "NS", [128, 128], F32)
    IDF = sb(es, "IDF", [128, 128], F32)
    ONB = sb(es, "ONB", [128, 128], BF16)
    ONF = sb(es, "ONF", [128, 128], F32)
    SC = sb(es, "SC", [128, NCH, 2], BF16)
    NRING = 4
    RINGT = sb(es, "RINGT", [128, 4 * 4096], BF16)
    RING = [RINGT[:, i * 4096:(i + 1) * 4096] for i in range(4)]
    PS = es.enter_context(nc.psum_tensor("PS", [128, 8 * 512], F32))

    bX = [P.buf("X%d" % c) for c in range(NCH)]
    bXN = [P.buf("XN%d" % c) for c in range(NCH)]
    bPT, bMOD, bGM, bCL, bHB, bTR, bTC, bNS = (P.buf(n) for n in ("PT", "MOD", "GM", "CL", "HB", "TR", "TC", "NS"))
    bIDF, bONB, bONF, bSC = (P.buf(n) for n in ("IDF", "ONB", "ONF", "SC"))
    bWG = [P.buf("WG%d" % i) for i in range(2)]
    bRING = [P.buf("RING%d" % i) for i in range(NRING)]
    bPS = [P.buf("PS%d" % i) for i in range(8)]

    def bank(b, n=512):
        return PS[:, b * 512:b * 512 + n]

    poff = {}
    o = 0
    for name, _, _, rows in PARAM_SEGS:
        poff[name] = o
        o += rows

    def pcol(name, idx):
        return PT[:, poff[name] + idx:poff[name] + idx + 1]

    ring_state = {"i": 0, "n": 4}

    def ring_load(src_ap, view):
        s = ring_state["i"] % ring_state["n"]
        ring_state["i"] += 1
        dst = view(RING[s])
        P.dma("pool", lambda e, dst=dst, src=src_ap: e.dma_start(out=dst, in_=src), writes=[bRING[s]])
        return s

    def v_k16(tile, ncols):
        return tile[:, 0:16 * ncols].rearrange("p (k c) -> p k c", k=16)

    prefetched = {}

    def load_wcols(w2d, c0, ncols=256):
        key = (w2d.tensor.name, w2d.offset, c0, ncols)
        if key in prefetched:
            return prefetched.pop(key)
        src = w2d[:, c0:c0 + ncols].rearrange("(k p) c -> p k c", p=128)
        s = ring_load(src, lambda t: v_k16(t, ncols))
        return s, v_k16(RING[s], ncols)

    pair_pre = {}

    def load_pair(w2d, c0, pair):
        key = (w2d.tensor.name, w2d.offset, c0, pair)
        dst = RINGT[:, pair * 8192:(pair + 1) * 8192].rearrange("p (k c) -> p k c", k=16)
        if key in pair_pre:
            pair_pre.pop(key)
            return dst
        src = w2d[:, c0:c0 + 512].rearrange("(k p) c -> p k c", p=128)
        P.dma("pool", lambda e, dst=dst, src=src: e.dma_start(out=dst, in_=src),
              writes=[bRING[2 * pair], bRING[2 * pair + 1]])
        return dst

    def prefetch_pair(w2d, c0, pair):
        load_pair(w2d, c0, pair)
        pair_pre[(w2d.tensor.name, w2d.offset, c0, pair)] = True

    def prefetch_wcols(w2d, c0, ncols=256):
        key = (w2d.tensor.name, w2d.offset, c0, ncols)
        assert key not in prefetched
        r_ = load_wcols(w2d, c0, ncols)
        prefetched[key] = r_

    NDEF = 15
    NSET1 = 48 - NDEF

    def mod_window_issue(stack, blks):
        items = []
        for i, blk in enumerate(blks):
            t = sb(stack, "DW%d" % i, [128, 4096], BF16)
            b = P.buf("DW%d" % i)
            src = dr["w_ada"][1][:, blk * 256:(blk + 1) * 256].rearrange("(k p) c -> p k c", p=128)
            dst = v_k16(t, 256)
            P.dma("pool", lambda e, dst=dst, src=src: e.dma_start(out=dst, in_=src), writes=[b])
            items.append((b, dst, blk))
        return items

    def mod_window_consume(items, last=False):
        for (b, wv, blk) in items:
            for n2 in range(2):
                n = blk * 2 + n2
                for kc in range(16):
                    P.op("pe", lambda e, wv=wv, n2=n2, kc=kc, n=n:
                         e.matmul(bank(6)[:, 2 * n:2 * n + 2], lhsT=wv[:, kc, n2 * 128:(n2 + 1) * 128],
                                  rhs=SC[:, kc, :], start=(kc == 0), stop=(kc == 15)),
                         reads=[b, bSC], writes=[bPS[6]], signal=(kc == 15))
        n0 = items[0][2] * 2
        cnt = len(items) * 2
        for g in range(2):
            P.op("dve", lambda e, g=g, n0=n0, cnt=cnt:
                 e.tensor_copy(out=MOD[:, 1, n0:n0 + cnt, g], in_=rap(bank(6)[:, 2 * n0 + g:2 * n0 + g + 1], [[2, cnt]])),
                 reads=[bPS[6]], writes=[bMOD])
        if last:
            for g in range(2):
                P.op("dve", lambda e, g=g:
                     e.tensor_tensor(out=MOD[:, 1, :, g], in0=MOD[:, 1, :, g],
                                     in1=PT[:, poff["bada"] + 96:poff["bada"] + 192], op=ALU.add),
                     reads=[bMOD, bPT], writes=[bMOD])
            mod_gm(1)

    def mod_gen(l, pb, loader=None, ncols=256, nblk=None):
        for blk in range(6 * D // ncols if nblk is None else nblk):
            if loader is None:
                s, wv = load_wcols(dr["w_ada"][l], blk * ncols, ncols)
                rb = bRING[s]
            else:
                rb, wv = loader(dr["w_ada"][l], blk * ncols)
            for n2 in range(ncols // 128):
                n = blk * (ncols // 128) + n2
                for kc in range(16):
                    P.op("pe", lambda e, wv=wv, n2=n2, kc=kc, n=n, pb=pb:
                         e.matmul(bank(pb)[:, 2 * n:2 * n + 2], lhsT=wv[:, kc, n2 * 128:(n2 + 1) * 128],
                                  rhs=SC[:, kc, :], start=(kc == 0), stop=(kc == 15)),
                         reads=[rb, bSC], writes=[bPS[pb]], signal=(kc == 15))
            yield

    def mod_finalize(l, pb):
        for g in range(2):
            P.op("dve", lambda e, l=l, g=g, pb=pb:
                 e.tensor_tensor(out=MOD[:, l, :, g], in0=rap(bank(pb)[:, g:g + 1], [[2, 96]]),
                                 in1=PT[:, poff["bada"] + l * 96:poff["bada"] + (l + 1) * 96], op=ALU.add),
                 reads=[bPS[pb], bPT], writes=[bMOD])
        mod_gm(l)

    def mod_gm(l):
        for which in range(2):
            nm = "n1g" if which == 0 else "n2g"
            for g in range(2):
                P.op("dve", lambda e, l=l, which=which, g=g, nm=nm:
                     e.scalar_tensor_tensor(out=GM[:, l, which, :, g],
                                            in0=MOD[:, l, (1 + 3 * which) * 16:(2 + 3 * which) * 16, g],
                                            scalar=1.0,
                                            in1=PT[:, poff[nm] + l * 16:poff[nm] + (l + 1) * 16],
                                            op0=ALU.add, op1=ALU.mult),
                     reads=[bMOD, bPT], writes=[bGM])

    with contextlib.ExitStack() as ph:
        STG = sb(ph, "STG", [128, NPBLK, 128], F32)
        bSTG = P.buf("STG")
        TMPA = sb(ph, "TMPA", [128, 512], F32)
        bTMPA = P.buf("TMPA")

        P.op("pool", lambda e: e.memset(IDF[:], 0.0), writes=[bIDF])
        P.op("pool", lambda e: e.affine_select(out=IDF[:], in_=IDF[:], compare_op=ALU.not_equal, fill=1.0,
                                               base=0, pattern=[[-1, 128]], channel_multiplier=1),
             reads=[bIDF], writes=[bIDF])
        P.op("pool", lambda e: e.memset(ONB[:], 1.0), writes=[bONB])
        P.op("pool", lambda e: e.memset(ONF[:], 1.0), writes=[bONF])
        bSTG0 = P.buf("STG0")
        P.op("dve", lambda e: e.memset(STG[:], 0.0), writes=[bSTG, bSTG0])
        P.op("dve", lambda e: e.memset(NS[:], 0.0), writes=[bNS])

        r = 0
        for name, dname, pat, rows in PARAM_SEGS:
            seg = dr[dname].rearrange(pat, p=128)
            a = 0
            while a < rows:
                blk, r0 = divmod(r + a, 128)
                n = min(rows - a, 128 - r0)
                P.dma("sp", lambda e, blk=blk, r0=r0, n=n, seg=seg, a=a:
                      e.dma_start(out=STG[r0:r0 + n, blk, :], in_=seg[a:a + n, :]), writes=[bSTG0 if blk == 0 else bSTG])
                a += n
            r += rows
        def pt_block(blk):
            b = blk % 2
            P.op("pe", lambda e, blk=blk, b=b: e.transpose(bank(b, 128), STG[:, blk, :], IDF[:]),
                 reads=[bSTG0 if blk == 0 else bSTG, bIDF], writes=[bPS[b]])
            P.op("dve", lambda e, blk=blk, b=b: e.tensor_copy(out=PT[:, blk * 128:(blk + 1) * 128], in_=bank(b, 128)),
                 reads=[bPS[b]], writes=[bPT])
        pt_block(0)

        for g in range(2):
            P.op("act", lambda e, g=g: e.activation(out=SC[:, :, g], in_=PT[:, poff["cond"] + g * 16:poff["cond"] + g * 16 + 16],
                                                    func=AF.Silu), reads=[bPT], writes=[bSC])

        ring_state["n"] = 4
        for _ in mod_gen(0, 2):
            pass
        for blk in range(1, NPBLK):
            pt_block(blk)
        mod_finalize(0, 2)
        for _ in mod_gen(1, 3, nblk=NSET1):
            pass
        for g in range(2):
            P.op("dve", lambda e, g=g:
                 e.tensor_copy(out=MOD[:, 1, 0:2 * NSET1, g], in_=rap(bank(3)[:, g:g + 1], [[2, 2 * NSET1]])),
                 reads=[bPS[3]], writes=[bMOD])
        lam = PT[:, poff["lam"]:poff["lam"] + 32]
        P.op("act", lambda e: e.activation(out=TMPA[:, 0:32], in_=lam, func=AF.Exp, scale=-1.0), reads=[bPT], writes=[bTMPA])
        P.op("act", lambda e: e.activation(out=TMPA[:, 32:64], in_=TMPA[:, 0:32], func=AF.Ln, bias=1.0, scale=1.0),
             reads=[bTMPA], writes=[bTMPA])
        P.op("dve", lambda e: e.tensor_scalar(out=CL[:, 0, :], in0=TMPA[:, 32:64], scalar1=-4.0, scalar2=None, op0=ALU.mult),
             reads=[bTMPA], writes=[bCL])
        P.op("dve", lambda e: e.tensor_scalar(out=CL[:, 1, :], in0=TMPA[:, 32:64], scalar1=-8.0, scalar2=None, op0=ALU.mult),
             reads=[bTMPA], writes=[bCL])
        P.op("dve", lambda e: e.tensor_scalar(out=HB_[:, 0, :], in0=PT[:, poff["rba"]:poff["rba"] + 32], scalar1=0.5,
                                              scalar2=None, op0=ALU.mult), reads=[bPT], writes=[bHB])
        P.op("dve", lambda e: e.tensor_scalar(out=HB_[:, 1, :], in0=PT[:, poff["rbx"]:poff["rbx"] + 32], scalar1=0.5,
                                              scalar2=None, op0=ALU.mult), reads=[bPT], writes=[bHB])

        P.barrier()
    with contextlib.ExitStack() as ph:
        ARG2 = sb(ph, "ARG2", [128, 640], F32)
        KI = sb(ph, "KI", [128, 640], I32)
        KF = sb(ph, "KF", [128, 640], F32)
        bARG2, bKI, bKF = P.buf("ARG2"), P.buf("KI"), P.buf("KF")
        OM2 = sb(ph, "OM2", [128, 4], F32)
        POS2 = sb(ph, "POS2", [128, 80], F32)
        TI2 = sb(ph, "TI2", [128, 96], I32)
        bOM2, bPOS2, bTI2 = P.buf("OM2"), P.buf("POS2"), P.buf("TI2")
        P.op("pool", lambda e: e.iota(out=TI2[:, 0:4], pattern=[[128, 4]], base=0, channel_multiplier=1), writes=[bTI2])
        P.op("pool", lambda e: e.iota(out=TI2[:, 16:32], pattern=[[1, 16]], base=0, channel_multiplier=0),
             reads=[bTI2], writes=[bTI2])
        P.op("pool", lambda e: e.iota(out=TI2[:, 32:96], pattern=[[1, 64]], base=0, channel_multiplier=0),
             reads=[bTI2], writes=[bTI2])
        P.op("dve", lambda e: e.tensor_copy(out=OM2[:], in_=TI2[:, 0:4]), reads=[bTI2], writes=[bOM2])
        P.op("dve", lambda e: e.tensor_copy(out=POS2[:], in_=TI2[:, 16:96]), reads=[bTI2], writes=[bPOS2])
        P.op("act", lambda e: e.activation(out=OM2[:], in_=OM2[:], func=AF.Exp, scale=-math.log(10000.0) / 512.0),
             reads=[bOM2], writes=[bOM2])
        A4 = ARG2[:].rearrange("p (a c n) -> p a c n", a=2, c=4)
        for cc in range(4):
            P.op("dve", lambda e, cc=cc: e.tensor_scalar(out=A4[:, 0, cc, :], in0=POS2[:], scalar1=OM2[:, cc:cc + 1],
                                                         scalar2=None, op0=ALU.mult),
                 reads=[bPOS2, bOM2], writes=[bARG2])
        P.op("dve", lambda e: e.tensor_scalar(out=A4[:, 1, :, :], in0=A4[:, 0, :, :], scalar1=math.pi / 2, scalar2=None,
                                              op0=ALU.add), reads=[bARG2], writes=[bARG2])
        P.op("dve", lambda e: e.tensor_scalar(out=KI[:], in0=ARG2[:], scalar1=1.0 / (2 * math.pi), scalar2=None,
                                              op0=ALU.mult), reads=[bARG2], writes=[bKI])
        P.op("dve", lambda e: e.tensor_copy(out=KF[:], in_=KI[:]), reads=[bKI], writes=[bKF])
        C1 = 6.28125
        C2 = 2 * math.pi - 6.28125
        P.op("dve", lambda e: e.scalar_tensor_tensor(out=ARG2[:], in0=KF[:], scalar=-C1, in1=ARG2[:], op0=ALU.mult,
                                                     op1=ALU.add), reads=[bKF, bARG2], writes=[bARG2])
        P.op("dve", lambda e: e.scalar_tensor_tensor(out=ARG2[:], in0=KF[:], scalar=-C2, in1=ARG2[:], op0=ALU.mult,
                                                     op1=ALU.add), reads=[bKF, bARG2], writes=[bARG2])
        P.op("dve", lambda e: e.tensor_scalar(out=ARG2[:], in0=ARG2[:], scalar1=-math.pi, scalar2=math.pi, op0=ALU.max,
                                              op1=ALU.min), reads=[bARG2], writes=[bARG2])
        P.op("act", lambda e: e.activation(out=ARG2[:], in_=ARG2[:], func=AF.Sin), reads=[bARG2], writes=[bARG2])
        for a in range(2):
            P.op("dve", lambda e, a=a: e.tensor_copy(out=TR[:, a * 4:(a + 1) * 4, :], in_=A4[:, a, :, 0:16]),
                 reads=[bARG2], writes=[bTR])
            P.op("dve", lambda e, a=a: e.tensor_copy(out=TC[:, a * 4:(a + 1) * 4, :], in_=A4[:, a, :, 16:80]),
                 reads=[bARG2], writes=[bTC])
        P.barrier()

    def modcol(l, k6, c, g):
        return MOD[:, l, k6 * 16 + c, g:g + 1]

    def load_group(g, defer=None):
        src = dr["xp"] if g == 0 else dr["xs"]
        with contextlib.ExitStack() as ph:
            STI = [sb(ph, "STI%d" % i, [128, D], F32) for i in range(2)]
            bSTI = [P.buf("STI%d" % i) for i in range(2)]
            ditems = mod_window_issue(ph, defer) if defer else None
            for tt in range(8):
                si = tt % 2
                P.dma("sp", lambda e, si=si, tt=tt: e.dma_start(out=STI[si][:], in_=src[tt * 128:(tt + 1) * 128, :]),
                      writes=[bSTI[si]])
                for q in range(4):
                    pb = (tt * 4 + q) % 4
                    for j in range(4):
                        c = q * 4 + j
                        P.op("pe", lambda e, si=si, c=c, pb=pb, j=j:
                             e.transpose(bank(pb)[:, j * 128:(j + 1) * 128], STI[si][:, c * 128:(c + 1) * 128], IDF[:]),
                             reads=[bSTI[si], bIDF], writes=[bPS[pb]], signal=(j == 3))
                    outap = X[:, q * 4:(q + 1) * 4, tt * 128:(tt + 1) * 128]
                    inap = bank(pb).rearrange("p (j t) -> p j t", j=4)
                    xb_ = [bX[q * 4 + j] for j in range(4)]
                    if g == 0:
                        P.op("act", lambda e, outap=outap, inap=inap: e.activation(out=outap, in_=inap, func=AF.Copy),
                             reads=[bPS[pb]], writes=xb_)
                    else:
                        o4 = rap(X[:, q * 4, tt * 128:tt * 128 + 1], [[T, 4], [64, 2], [1, 64]])
                        i4 = rap(bank(pb)[:, 0:1], [[128, 4], [64, 2], [1, 64]])
                        if q < 2:
                            t4 = rap(TR[:, q * 4, 2 * tt:2 * tt + 1], [[16, 4], [1, 2], [0, 64]])
                            rb = bTR
                        else:
                            t4 = rap(TC[:, (q - 2) * 4, 0:1], [[64, 4], [0, 2], [1, 64]])
                            rb = bTC
                        P.op("dve", lambda e, o4=o4, i4=i4, t4=t4: e.tensor_tensor(out=o4, in0=i4, in1=t4, op=ALU.add),
                             reads=[bPS[pb], rb], writes=xb_)
            if ditems:
                mod_window_consume(ditems)
            P.barrier()

    def rstd_half(ph_bufs, th):
        SQ, bSQ, RS, bRS = ph_bufs
        tok = slice(th * 512, (th + 1) * 512)
        for q in range(4):
            rd = [bX[q * 4 + j] for j in range(4)]
            if q % 2 == 0:
                P.op("act", lambda e, q=q, tok=tok: e.activation(out=SQ[q][:], in_=X[:, q * 4:(q + 1) * 4, tok], func=AF.Square),
                     reads=rd, writes=[bSQ[q]])
            else:
                P.op("dve", lambda e, q=q, tok=tok: e.tensor_tensor(out=SQ[q][:], in0=X[:, q * 4:(q + 1) * 4, tok],
                                                                    in1=X[:, q * 4:(q + 1) * 4, tok], op=ALU.mult),
                     reads=rd, writes=[bSQ[q]])
        for q in (0, 1, 2, 3):
            for j in range(4):
                P.op("pe", lambda e, q=q, j=j:
                     e.matmul(bank(7), lhsT=ONB[:], rhs=SQ[q][:, j, :],
                              start=(q == 0 and j == 0), stop=(q == 3 and j == 3)),
                     reads=[bSQ[q], bONB], writes=[bPS[7]], signal=(j == 3))
        P.op("act", lambda e, th=th: e.activation(out=RS[th][:], in_=bank(7), func=AF.Ln, scale=1.0 / D, bias=EPS_T[:, 0:1]),
             reads=[bPS[7], bEPS], writes=[bRS[th]])
        P.op("act", lambda e, th=th: e.activation(out=RS[th][:], in_=RS[th][:], func=AF.Exp, scale=-0.5),
             reads=[bRS[th]], writes=[bRS[th]])

    def norm_mod(l, which, g, nxt=None, defer=None, dlast=False):
        ring_state["n"] = 4
        if nxt == "rg":
            prefetch_wcols(dr["rg_w_in"][0], D)
            prefetch_wcols(dr["rg_w_in"][0], 0)
        elif nxt == "ffn":
            prefetch_wcols(dr["ffn_w_up"][l], 0)
            prefetch_wcols(dr["ffn_w_up"][l], DFF)
        elif nxt == "cm":
            prefetch_pair(dr["cm_w_in"][0], D, 0)
        with contextlib.ExitStack() as ph:
            SQ = [sb(ph, "SQ%d" % i, [128, 4, 512], BF16) for i in range(4)]
            bSQ = [P.buf("SQ%d" % i) for i in range(4)]
            RS = [sb(ph, "RS%d" % i, [128, 512], F32) for i in range(2)]
            bRS = [P.buf("RS%d" % i) for i in range(2)]
            TN = [sb(ph, "TN%d" % i, [128, 512], F32) for i in range(2)]
            bTN = [P.buf("TN%d" % i) for i in range(2)]
            ditems = mod_window_issue(ph, defer) if defer else None
            for th in range(2):
                tok = slice(th * 512, (th + 1) * 512)
                rstd_half((SQ, bSQ, RS, bRS), th)
                for c in range(NCH):
                    ti = c % 2
                    P.op("dve", lambda e, c=c, ti=ti, tok=tok, th=th:
                         e.scalar_tensor_tensor(out=TN[ti][:], in0=X[:, c, tok], scalar=GM[:, l, which, c, g:g + 1],
                                                in1=RS[th][:], op0=ALU.mult, op1=ALU.mult),
                         reads=[bX[c], bGM, bRS[th]], writes=[bTN[ti]])
                    P.op("act", lambda e, c=c, ti=ti, tok=tok:
                         e.activation(out=XN[:, c, tok], in_=TN[ti][:], func=AF.Identity,
                                      bias=modcol(l, 3 * which, c, g), scale=1.0),
                         reads=[bTN[ti], bMOD], writes=[bXN[c]])
            if ditems:
                mod_window_consume(ditems, last=dlast)
            P.barrier()

    def rg_mixer(g, modl=None):
        S = 4 if g == 0 else 1
        L = T // S
        l = 0
        ring_state["n"] = 4 if modl is None else 3
        dbk = 6 if modl is None else 0
        if modl is not None:
            bMRh = [P.buf("MRh%d" % i) for i in range(2)]
            mi = [0]

            def mloader(w2d, c0):
                i = mi[0] % 2
                mi[0] += 1
                src = w2d[:, c0:c0 + 128].rearrange("(k p) c -> p k c", p=128)
                dst = RING[3][:, i * 2048:(i + 1) * 2048].rearrange("p (k c) -> p k c", k=16)
                P.dma("pool", lambda e, dst=dst, src=src: e.dma_start(out=dst, in_=src), writes=[bMRh[i]])
                return bMRh[i], dst
            mg = mod_gen(modl, 7, mloader, 128)
        else:
            mg = iter(())

        def modstep(k):
            for _ in range(k):
                next(mg, None)
        with contextlib.ExitStack() as ph:
            YG = sb(ph, "YG", [128, NCH, T], BF16)
            bYG = [P.buf("YG%d" % c) for c in range(NCH)]
            WG = [sb(ph, "WG%d" % i, [128, 2, 2, 128], BF16) for i in range(2)]
            XB = sb(ph, "XB", [128, T], F32)
            XBb = sb(ph, "XBb", [128, T], BF16)
            bXB, bXBb = P.buf("XB"), P.buf("XBb")
            TT_ = [sb(ph, "T%d" % i, [128, T], F32) for i in range(5)]
            bT = [P.buf("T%d" % i) for i in range(5)]
            T1, T2, T3, T4, T5 = TT_
            win = dr["rg_w_in"][0]
            ps01 = PS[:, 0:1024]
            ps23 = PS[:, 1024:2048]
            ps45 = PS[:, 2048:3072]
            ps67 = PS[:, dbk * 512:dbk * 512 + 1024]

            def seg3(ap2d, a, n):
                return rap(ap2d[:, a:a + 1], [[L, S], [1, n]])

            XB2 = [XB, sb(ph, "XB2", [128, T], F32)]
            XBb2 = [XBb, sb(ph, "XBb2", [128, T], BF16)]
            bXB2 = [bXB, P.buf("XB2")]
            bXBb2 = [bXBb, P.buf("XBb2")]
            wslots = {}

            def loads(h):
                if h % 2 == 0:
                    wslots["x"] = load_wcols(win, D + h * 128, 256)
                    wslots["g"] = load_wcols(win, h * 128, 256)
                wi = h % 2
                P.dma("pool", lambda e, wi=wi, h=h: e.dma_start(out=WG[wi][:, 0, :, :],
                                                                in_=dr["rg_w_a"][0, :, h].rearrange("d i j -> i d j")),
                      writes=[bWG[wi]])
                P.dma("pool", lambda e, wi=wi, h=h: e.dma_start(out=WG[wi][:, 1, :, :],
                                                                in_=dr["rg_w_x"][0, :, h].rearrange("d i j -> i d j")),
                      writes=[bWG[wi]])
                return wslots["x"], wslots["g"]

            hw = {}

            def A_(h):
                (sx, wx), _ = hw[h]
                hh = h % 2
                for th in range(2):
                    for kc in range(16):
                        P.op("pe", lambda e, th=th, kc=kc, wx=wx, hh=hh:
                             e.matmul(bank(th), lhsT=wx[:, kc, hh * 128:(hh + 1) * 128], rhs=XN[:, kc, th * 512:(th + 1) * 512],
                                      start=(kc == 0), stop=(kc == 15)),
                             reads=[bRING[sx], bXN[kc]], writes=[bPS[th]], signal=(kc == 15))

            def D_(h):
                _, (sg_, wg_) = hw[h]
                hh = h % 2
                for th in range(2):
                    for kc in range(16):
                        P.op("pe", lambda e, th=th, kc=kc, wg_=wg_, hh=hh:
                             e.matmul(bank(dbk + th), lhsT=wg_[:, kc, hh * 128:(hh + 1) * 128],
                                      rhs=XN[:, kc, th * 512:(th + 1) * 512], start=(kc == 0), stop=(kc == 15)),
                             reads=[bRING[sg_], bXN[kc]], writes=[bPS[dbk + th]], signal=(kc == 15))

            def conv_a(h):
                xb, bxb = XB2[h % 2], bXB2[h % 2]
                P.op("act", lambda e, h=h, xb=xb: e.activation(out=xb[:], in_=ps01, func=AF.Identity,
                                                               scale=pcol("rcw", 2 * 16 + h), bias=pcol("rcb", h)),
                     reads=[bPS[0], bPS[1], bPT], writes=[bxb])
                for (k, dst0, src0, n) in ((0, 2, 0, L - 2), (1, 1, 0, L - 1), (3, 0, 1, L - 1)):
                    P.op("dve", lambda e, h=h, k=k, dst0=dst0, src0=src0, n=n, xb=xb:
                         e.scalar_tensor_tensor(out=seg3(xb, dst0, n), in0=seg3(ps01, src0, n),
                                                scalar=pcol("rcw", k * 16 + h), in1=seg3(xb, dst0, n),
                                                op0=ALU.mult, op1=ALU.add),
                         reads=[bPS[0], bPS[1], bPT, bxb], writes=[bxb])

            def conv_b(h):
                P.op("act", lambda e, h=h: e.activation(out=XBb2[h % 2][:], in_=XB2[h % 2][:], func=AF.Copy),
                     reads=[bXB2[h % 2]], writes=[bXBb2[h % 2]])

            def gelu_(h):
                P.op("act", lambda e, h=h: e.activation(out=YG[:, h, :], in_=ps67, func=AF.Gelu_apprx_tanh),
                     reads=[bPS[dbk], bPS[dbk + 1]], writes=[bYG[h]])

            def B_(h, d):
                wi = h % 2
                xbb, bxbb = XBb2[h % 2], bXBb2[h % 2]
                for th in range(2):
                    P.op("pe", lambda e, th=th, d=d, wi=wi, xbb=xbb:
                         e.matmul(bank(2 + th), lhsT=WG[wi][:, 0, d, :], rhs=xbb[:, th * 512:(th + 1) * 512],
                                  start=True, stop=True),
                         reads=[bWG[wi], bxbb], writes=[bPS[2 + th]])
                for th in range(2):
                    P.op("pe", lambda e, th=th, d=d, wi=wi, xbb=xbb:
                         e.matmul(bank(4 + th), lhsT=WG[wi][:, 1, d, :], rhs=xbb[:, th * 512:(th + 1) * 512],
                                  start=True, stop=True),
                         reads=[bWG[wi], bxbb], writes=[bPS[4 + th]])

            def tanh2(h, d):
                col = d * 16 + h
                P.op("act", lambda e, col=col: e.activation(out=T1[:], in_=ps23, func=AF.Tanh, scale=0.5,
                                                            bias=HB_[:, 0, col:col + 1]),
                     reads=[bPS[2], bPS[3], bHB], writes=[bT[0]])
                P.op("act", lambda e, col=col: e.activation(out=T3[:], in_=ps45, func=AF.Tanh, scale=0.5,
                                                            bias=HB_[:, 1, col:col + 1]),
                     reads=[bPS[4], bPS[5], bHB], writes=[bT[2]])

            def chain(h, d):
                col = d * 16 + h
                xb, bxb = XB2[h % 2], bXB2[h % 2]
                P.op("act", lambda e, col=col: e.activation(out=T2[:], in_=T1[:], func=AF.Exp,
                                                            scale=CL[:, 0, col:col + 1], bias=CL[:, 0, col:col + 1]),
                     reads=[bT[0], bCL], writes=[bT[1]])
                P.op("act", lambda e, col=col: e.activation(out=T1[:], in_=T1[:], func=AF.Exp,
                                                            scale=CL[:, 1, col:col + 1], bias=CL[:, 1, col:col + 1]),
                     reads=[bT[0], bCL], writes=[bT[0]])
                P.op("act", lambda e: e.activation(out=T1[:], in_=T1[:], func=AF.Sqrt, scale=-0.25, bias=Q25_T[:, 0:1]),
                     reads=[bT[0], bEPS], writes=[bT[0]])
                P.op("dve", lambda e, xb=xb: e.scalar_tensor_tensor(out=T3[:], in0=T3[:], scalar=1.0, in1=xb[:],
                                                                    op0=ALU.add, op1=ALU.mult),
                     reads=[bT[2], bxb], writes=[bT[2]])
                P.op("dve", lambda e: e.tensor_tensor(out=T3[:], in0=T3[:], in1=T1[:], op=ALU.mult),
                     reads=[bT[2], bT[0]], writes=[bT[2]])
                HD = T4 if d == 0 else T5
                bHD = bT[3] if d == 0 else bT[4]
                for s_ in range(S):
                    a0 = s_ * L
                    if d == 0:
                        oa, da, ua = HD[:, a0:a0 + L], T2[:, a0:a0 + L], T3[:, a0:a0 + L]
                    else:
                        oa = rap(HD[:, a0 + L - 1:a0 + L], [[-1, L]])
                        da = rap(T2[:, a0 + L - 1:a0 + L], [[-1, L]])
                        ua = rap(T3[:, a0 + L - 1:a0 + L], [[-1, L]])
                    init = 0.0 if g == 0 else pcol("st", col)
                    P.op("dve", lambda e, oa=oa, da=da, ua=ua, init=init:
                         e.tensor_tensor_scan(out=oa, data0=da, data1=ua, initial=init, op0=ALU.mult, op1=ALU.add),
                         reads=[bT[1], bT[2], bPT], writes=[bHD])
                if g == 0:
                    pos = (L - 1) if d == 0 else 0
                    P.op("dve", lambda e, HD=HD, pos=pos, col=col:
                         e.tensor_copy(out=rap(NS[:, col:col + 1], [[32, S]]), in_=rap(HD[:, pos:pos + 1], [[L, S]])),
                         reads=[bHD], writes=[bNS])

            hw[0] = loads(0)
            A_(0)
            conv_a(0)
            conv_b(0)
            D_(0)
            for h in range(NCH):
                nx = h + 1 < NCH
                if dbk == 0:
                    gelu_(h)
                if nx:
                    hw[h + 1] = loads(h + 1)
                    A_(h + 1)
                modstep(2)
                if dbk != 0:
                    gelu_(h)
                B_(h, 0)
                tanh2(h, 0)
                B_(h, 1)
                modstep(2)
                if nx:
                    conv_a(h + 1)
                    D_(h + 1)
                modstep(2)
                chain(h, 0)
                if nx:
                    conv_b(h + 1)
                tanh2(h, 1)
                chain(h, 1)
                P.op("dve", lambda e: e.tensor_tensor(out=T4[:], in0=T4[:], in1=T5[:], op=ALU.add),
                     reads=[bT[3], bT[4]], writes=[bT[3]])
                P.op("dve", lambda e, h=h: e.tensor_tensor(out=YG[:, h, :], in0=T4[:], in1=YG[:, h, :], op=ALU.mult),
                     reads=[bT[3], bYG[h]], writes=[bYG[h]])
            if modl is not None:
                for _ in mg:
                    pass
                mod_finalize(modl, 7)
            wout = dr["rg_w_out"][0]
            k = 0
            for oc in range(NCH):
                if oc % 2 == 0:
                    so, wo = load_wcols(wout, oc * 128, 256)
                for th in range(2):
                    pb = k % 4
                    k += 1
                    tok = slice(th * 512, (th + 1) * 512)
                    for kc in range(16):
                        P.op("pe", lambda e, kc=kc, wo=wo, oc=oc, pb=pb, tok=tok:
                             e.matmul(bank(pb), lhsT=wo[:, kc, (oc % 2) * 128:(oc % 2 + 1) * 128], rhs=YG[:, kc, tok],
                                      start=(kc == 0), stop=(kc == 15)),
                             reads=[bRING[so], bYG[kc]], writes=[bPS[pb]], signal=(kc == 15))
                    P.op("dve", lambda e, oc=oc, pb=pb, tok=tok:
                         e.scalar_tensor_tensor(out=X[:, oc, tok], in0=bank(pb), scalar=modcol(l, 2, oc, g),
                                                in1=X[:, oc, tok], op0=ALU.mult, op1=ALU.add),
                         reads=[bPS[pb], bMOD, bX[oc]], writes=[bX[oc]])
            P.barrier()
        if g == 0:
            with contextlib.ExitStack() as ph:
                NSo = sb(ph, "NSo", [128, 128], F32)
                bNSo = P.buf("NSo")
                P.op("pe", lambda e: e.transpose(bank(0, 128), NS[:], IDF[:]), reads=[bNS, bIDF], writes=[bPS[0]])
                P.op("dve", lambda e: e.tensor_copy(out=NSo[:], in_=bank(0, 128)), reads=[bPS[0]], writes=[bNSo])
                P.dma("sp", lambda e: e.dma_start(out=nsd.rearrange("r (h p) -> (r h) p", p=128), in_=NSo[:]),
                      reads=[bNSo])
                P.barrier()

    def cm_mixer(g):
        l = 1
        ring_state["n"] = 4
        win = dr["cm_w_in"][0]
        with contextlib.ExitStack() as ph:
            bV = [P.buf("V%d" % c) for c in range(NCH)]
            WST = sb(ph, "WST", [128, 8, 128], BF16)
            BT = sb(ph, "BT", [128, NCH, 128], F32)
            bWST, bWSF, bBT = P.buf("WST"), P.buf("WSF"), P.buf("BT")
            ROW = sb(ph, "ROW", [33, 1024], F32)
            bROW = P.buf("ROW")
            BH = sb(ph, "BH", [33, D], BF16)
            bBH = P.buf("BH")
            ST1 = sb(ph, "ST1", [128, 2, 8, 8], F32)
            bST = P.buf("ST1")
            STT = sb(ph, "STT", [128, 6, 8], F32)
            bSTT = P.buf("STT")
            GS = [sb(ph, "GS%d" % i, [128, 512], F32) for i in range(2)]
            bGS = [P.buf("GS%d" % i) for i in range(2)]
            JK = sb(ph, "JK", [128, 512], BF16)
            bJK = P.buf("JK")
            SCt = [sb(ph, "SCt%d" % i, [128, 512], F32) for i in range(2)]
            UGt = [sb(ph, "UGt%d" % i, [128, 512], F32) for i in range(2)]
            bSCt = [P.buf("SCt%d" % i) for i in range(2)]
            bUGt = [P.buf("UGt%d" % i) for i in range(2)]
            CMS = float(os.environ.get("CM_STOP", "9"))
            if CMS <= 0:
                return
            with contextlib.ExitStack() as ph2:
                WS0 = sb(ph2, "WS0", [128, 8, 128], F32)
                WSF = sb(ph2, "WSF", [128, 8, 128], F32)
                LNB = sb(ph2, "LNB", [128, D], F32)
                bWS0, bLNB = P.buf("WS0"), P.buf("LNB")
                P.dma("sp", lambda e: e.dma_start(out=WS0[:], in_=dr["cm_w_s"][0].rearrange("g p q -> p g q")), writes=[bWS0])
                P.dma("sp", lambda e: e.dma_start(out=LNB[:], in_=dr["cm_ln_b"][0:1, :].partition_broadcast(128)),
                      writes=[bLNB])
                TB = sb(ph2, "TB", [33, D], F32)
                bTB = P.buf("TB")
                P.op("dve", lambda e: e.memset(BH[:], 0.0), writes=[bBH])
                P.dma("sp", lambda e: e.dma_start(out=TB[0:1, :], in_=dr["cm_b_in"][0:1, D:2 * D]), writes=[bTB])
                P.dma("sp", lambda e: e.dma_start(out=TB[32:33, :], in_=dr["cm_b_in"][0:1, D:2 * D]), writes=[bTB])
                P.op("dve", lambda e: e.tensor_copy(out=BH[0:1, :], in_=TB[0:1, :]), reads=[bTB, bBH], writes=[bBH])
                P.op("dve", lambda e: e.tensor_copy(out=BH[32:33, :], in_=TB[32:33, :]), reads=[bTB, bBH], writes=[bBH])
                P.op("dve", lambda e: e.tensor_tensor(out=TB[32:33, :], in0=TB[32:33, :], in1=BH[32:33, :], op=ALU.subtract),
                     reads=[bTB, bBH], writes=[bTB])
                P.op("dve", lambda e: e.tensor_copy(out=BH[32:33, :], in_=TB[32:33, :]), reads=[bTB, bBH], writes=[bBH])
                P.dma("sp", lambda e: e.dma_start(out=ROW[32:33, 0:1024], in_=dr["cm_b_s"][0:1].rearrange("a g p -> a (g p)")),
                      writes=[bROW])
                if CMS <= 0.3:
                    P.barrier()
                    return
                for gi in range(8):
                    pb = gi % 2
                    P.op("pe", lambda e, gi=gi, pb=pb: e.transpose(bank(pb, 128), WS0[:, gi, :], IDF[:]),
                         reads=[bWS0, bIDF], writes=[bPS[pb]])
                    if CMS <= 0.4:
                        continue
                    P.op("dve", lambda e, gi=gi, pb=pb: e.tensor_copy(out=WSF[:, gi, :], in_=bank(pb, 128)),
                         reads=[bPS[pb]], writes=[bWSF])
                    if CMS <= 0.5:
                        continue
                    P.op("dve", lambda e, gi=gi: e.tensor_copy(out=WST[:, gi, :], in_=WSF[:, gi, :]),
                         reads=[bWSF], writes=[bWST])
                if CMS <= 0.6:
                    P.barrier()
                    return
                for c in range(NCH):
                    gi = c // 2
                    pb = 2 + c % 2
                    P.op("pe", lambda e, c=c, gi=gi, pb=pb:
                         e.matmul(bank(pb, 128), lhsT=LNB[:, c * 128:(c + 1) * 128], rhs=WSF[:, gi, :], start=True, stop=False),
                         reads=[bLNB, bWSF], writes=[bPS[pb]], signal=False)
                    P.op("pe", lambda e, c=c, gi=gi, pb=pb:
                         e.matmul(bank(pb, 128), lhsT=ONF[32:33, :], rhs=ROW[32:33, gi * 128:(gi + 1) * 128], start=False, stop=True),
                         reads=[bONF, bROW], writes=[bPS[pb]])
                    P.op("act", lambda e, c=c, pb=pb: e.activation(out=BT[:, c, :], in_=bank(pb, 128), func=AF.Copy),
                         reads=[bPS[pb]], writes=[bBT])
                P.barrier()
            if CMS <= 1:
                return
            V = sb(ph, "V", [128, 8, D], BF16)
            k = 0
            for vp in range(4):
                pair = vp % 2
                wv = load_pair(win, D + vp * 512, pair)
                rbs = [bRING[2 * pair], bRING[2 * pair + 1]]
                for tt in range(8):
                    pb = k % 2
                    k += 1
                    for kc in range(16):
                        P.op("pe", lambda e, kc=kc, tt=tt, wv=wv, pb=pb:
                             e.matmul(bank(pb), lhsT=XN[:, kc, tt * 128:(tt + 1) * 128], rhs=wv[:, kc, :],
                                      start=(kc == 0), stop=False),
                             reads=rbs + [bXN[kc]], writes=[bPS[pb]], signal=False)
                    P.op("pe", lambda e, vp=vp, pb=pb:
                         e.matmul(bank(pb), lhsT=ONB[0:33, :], rhs=BH[0:33, vp * 512:(vp + 1) * 512], start=False, stop=True),
                         reads=[bONB, bBH], writes=[bPS[pb]])
                    P.op("act", lambda e, pb=pb, tt=tt, vp=vp:
                         e.activation(out=GS[pb][:], in_=bank(pb), func=AF.Gelu_apprx_tanh,
                                      accum_out=ST1[:, 0, tt, vp:vp + 1]),
                         reads=[bPS[pb]], writes=[bGS[pb], bST])
                    P.op("act", lambda e, pb=pb, tt=tt, vp=vp:
                         e.activation(out=JK[:], in_=GS[pb][:], func=AF.Square, accum_out=ST1[:, 1, tt, vp:vp + 1]),
                         reads=[bGS[pb]], writes=[bJK, bST])
                    P.op("dve", lambda e, pb=pb, tt=tt, vp=vp:
                         e.tensor_copy(out=V[:, tt, vp * 512:(vp + 1) * 512], in_=GS[pb][:]),
                         reads=[bGS[pb]], writes=[bV[4 * vp + q] for q in range(4)])
            if CMS <= 2:
                P.barrier()
                return
            MU, EX2, VAR, RSTD, NB, TMPs = (STT[:, i, :] for i in range(6))
            P.op("dve", lambda e: e.tensor_reduce(out=MU, in_=ST1[:, 0, :, 0:4], axis=AX.X, op=ALU.add), reads=[bST], writes=[bSTT])
            P.op("dve", lambda e: e.tensor_reduce(out=EX2, in_=ST1[:, 1, :, 0:4], axis=AX.X, op=ALU.add), reads=[bST], writes=[bSTT])
            P.op("dve", lambda e: e.tensor_scalar(out=MU, in0=MU, scalar1=1.0 / D, scalar2=None, op0=ALU.mult),
                 reads=[bSTT], writes=[bSTT])
            P.op("dve", lambda e: e.tensor_tensor(out=TMPs, in0=MU, in1=MU, op=ALU.mult), reads=[bSTT], writes=[bSTT])
            P.op("dve", lambda e: e.scalar_tensor_tensor(out=VAR, in0=EX2, scalar=1.0 / D, in1=TMPs, op0=ALU.mult,
                                                         op1=ALU.subtract), reads=[bSTT], writes=[bSTT])
            P.op("act", lambda e: e.activation(out=RSTD, in_=VAR, func=AF.Sqrt, scale=1.0, bias=LNE_T[:, 0:1]),
                 reads=[bSTT, bEPS], writes=[bSTT])
            P.op("dve", lambda e: e.reciprocal(out=RSTD, in_=RSTD), reads=[bSTT], writes=[bSTT])
            P.op("dve", lambda e: e.scalar_tensor_tensor(out=NB, in0=MU, scalar=-1.0, in1=RSTD, op0=ALU.mult, op1=ALU.mult),
                 reads=[bSTT], writes=[bSTT])
            for tt in range(8):
                P.op("dve", lambda e, tt=tt: e.tensor_scalar(out=V[:, tt, :], in0=V[:, tt, :], scalar1=STT[:, 3, tt:tt + 1],
                                                             scalar2=STT[:, 4, tt:tt + 1], op0=ALU.mult, op1=ALU.add),
                     reads=bV + [bSTT], writes=bV)
            if CMS <= 3:
                P.barrier()
                return
            for c in range(NCH):
                gi = c // 2
                if c % 2 == 0:
                    su, wu = load_wcols(win, c * 128, 256)
                for tt in range(8):
                    pb = 2 + tt // 4
                    P.op("pe", lambda e, c=c, tt=tt, gi=gi, pb=pb:
                         e.matmul(bank(pb)[:, (tt % 4) * 128:(tt % 4 + 1) * 128], lhsT=V[:, tt, c * 128:(c + 1) * 128],
                                  rhs=WST[:, gi, :], start=True, stop=True),
                         reads=[bV[c], bWST], writes=[bPS[pb]], signal=(tt % 4 == 3))
                for th in range(2):
                    for kc in range(16):
                        P.op("pe", lambda e, th=th, kc=kc, wu=wu, c=c:
                             e.matmul(bank(4 + th), lhsT=wu[:, kc, (c % 2) * 128:(c % 2 + 1) * 128],
                                      rhs=XN[:, kc, th * 512:(th + 1) * 512], start=(kc == 0), stop=(kc == 15)),
                             reads=[bRING[su], bXN[kc]], writes=[bPS[4 + th]], signal=(kc == 15))
                for th in range(2):
                    P.op("dve", lambda e, th=th, c=c:
                         e.scalar_tensor_tensor(out=SCt[th][:].rearrange("p (a b) -> p a b", a=4),
                                                in0=bank(2 + th).rearrange("p (a b) -> p a b", a=4),
                                                scalar=pcol("clg", c),
                                                in1=rap(BT[:, c, 0:1], [[0, 4], [1, 128]]),
                                                op0=ALU.mult, op1=ALU.add),
                         reads=[bPS[2 + th], bPT, bBT], writes=[bSCt[th]])
                    P.op("act", lambda e, th=th, c=c:
                         e.activation(out=UGt[th][:], in_=bank(4 + th), func=AF.Gelu_apprx_tanh, bias=pcol("cbi", c), scale=1.0),
                         reads=[bPS[4 + th], bPT], writes=[bUGt[th]])
                    P.op("dve", lambda e, th=th, c=c:
                         e.tensor_tensor(out=V[:, 4 * th:4 * th + 4, c * 128:(c + 1) * 128],
                                         in0=UGt[th][:].rearrange("p (a b) -> p a b", a=4),
                                         in1=SCt[th][:].rearrange("p (a b) -> p a b", a=4), op=ALU.mult),
                         reads=[bUGt[th], bSCt[th]], writes=[bV[c]])
            if CMS <= 4:
                P.barrier()
                return
            wout = dr["cm_w_out"][0]
            k = 0
            for oc in range(NCH):
                if oc % 2 == 0:
                    so, wo = load_wcols(wout, oc * 128, 256)
                for th in range(2):
                    pb = 4 + k % 4
                    k += 1
                    tok = slice(th * 512, (th + 1) * 512)
                    for kc in range(16):
                        P.op("pe", lambda e, kc=kc, wo=wo, oc=oc, pb=pb, th=th:
                             e.matmul(bank(pb), lhsT=wo[:, kc, (oc % 2) * 128:(oc % 2 + 1) * 128],
                                      rhs=V[:, 4 * th:4 * th + 4, kc * 128:(kc + 1) * 128],
                                      start=(kc == 0), stop=(kc == 15)),
                             reads=[bRING[so], bV[kc]], writes=[bPS[pb]], signal=(kc == 15))
                    P.op("dve", lambda e, oc=oc, pb=pb, tok=tok:
                         e.scalar_tensor_tensor(out=X[:, oc, tok], in0=bank(pb), scalar=modcol(l, 2, oc, g),
                                                in1=X[:, oc, tok], op0=ALU.mult, op1=ALU.add),
                         reads=[bPS[pb], bMOD, bX[oc]], writes=[bX[oc]])
            P.barrier()

    def ffn(l, g):
        S = 4 if g == 0 else 1
        L = T // S
        wup = dr["ffn_w_up"][l]
        wdn = dr["ffn_w_down"][l]
        NFB = DFF // 256
        with contextlib.ExitStack() as ph:
            DR = [sb(ph, "DR%d" % i, [128, 4096], BF16) for i in range(4)]
            bDR = [P.buf("DR%d" % i) for i in range(4)]
            HBf = [sb(ph, "HBf%d" % i, [128, 4, T], BF16) for i in range(2)]
            bHBf = [P.buf("HBf%d" % i) for i in range(2)]
            GC = [sb(ph, "GC%d" % i, [128, T], F32) for i in range(2)]
            VC = [sb(ph, "VC%d" % i, [128, T], F32) for i in range(2)]
            bGC = [P.buf("GC%d" % i) for i in range(2)]
            bVC = [P.buf("VC%d" % i) for i in range(2)]
            ring_state["n"] = 4
            ps01 = PS[:, 0:1024]
            ps23 = PS[:, 1024:2048]

            def seg3(ap2d, a, n):
                return rap(ap2d[:, a:a + 1], [[L, S], [1, n]])

            def conv_evac(psv, pbs, dst, bdst, fc):
                P.op("act", lambda e: e.activation(out=dst[:], in_=psv, func=AF.Identity,
                                                   scale=pcol("fcw", (l * 3 + 1) * 88 + fc), bias=pcol("fcb", l * 88 + fc)),
                     reads=pbs + [bPT], writes=[bdst])
                for (kk, dst0, src0) in ((0, 1, 0), (2, 0, 1)):
                    P.op("dve", lambda e, kk=kk, dst0=dst0, src0=src0:
                         e.scalar_tensor_tensor(out=seg3(dst, dst0, L - 1), in0=seg3(psv, src0, L - 1),
                                                scalar=pcol("fcw", (l * 3 + kk) * 88 + fc), in1=seg3(dst, dst0, L - 1),
                                                op0=ALU.mult, op1=ALU.add),
                         reads=pbs + [bPT, bdst], writes=[bdst])

            def up_gen(fb):
                sg_, wg_ = load_wcols(wup, fb * 256, 256)
                sv_, wv_ = load_wcols(wup, DFF + fb * 256, 256)
                src = wdn[fb * 256:(fb + 1) * 256, :].rearrange("(k p) c -> p k c", p=128)
                di = fb % 4
                ddst = DR[di][:, 0:4096].rearrange("p (k c) -> p k c", k=2)
                P.dma("pool", lambda e, ddst=ddst, src=src: e.dma_start(out=ddst, in_=src), writes=[bDR[di]])
                hb = (fb // 2) % 2
                for j in range(2):
                    fc = 2 * fb + j
                    ji = j % 2
                    hk = (fb % 2) * 2 + j
                    for (w_, s_, b0, isg) in ((wg_, sg_, 0, True), (wv_, sv_, 2, False)):
                        for th in range(2):
                            for kc in range(16):
                                P.op("pe", lambda e, th=th, kc=kc, w_=w_, j=j, b0=b0:
                                     e.matmul(bank(b0 + th), lhsT=w_[:, kc, j * 128:(j + 1) * 128],
                                              rhs=XN[:, kc, th * 512:(th + 1) * 512], start=(kc == 0), stop=(kc == 15)),
                                     reads=[bRING[s_], bXN[kc]], writes=[bPS[b0 + th]], signal=(kc == 15))
                            if th == 1:
                                if isg:
                                    conv_evac(ps01, [bPS[0], bPS[1]], GC[ji], bGC[ji], fc)
                                else:
                                    conv_evac(ps23, [bPS[2], bPS[3]], VC[ji], bVC[ji], NFC + fc)
                                    P.op("act", lambda e, ji=ji: e.activation(out=GC[ji][:], in_=GC[ji][:], func=AF.Silu),
                                         reads=[bGC[ji]], writes=[bGC[ji]])
                                    P.op("dve", lambda e, ji=ji, hb=hb, hk=hk:
                                         e.tensor_tensor(out=HBf[hb][:, hk, :], in0=GC[ji][:], in1=VC[ji][:], op=ALU.mult),
                                         reads=[bGC[ji], bVC[ji]], writes=[bHBf[hb]])
                            yield

            kd = [0]

            def down_gen(pi):
                hb = pi % 2
                n = 0
                for oc in range(NCH):
                    for th in range(2):
                        pb = 4 + kd[0] % 4
                        kd[0] += 1
                        tok = slice(th * 512, (th + 1) * 512)
                        for hk in range(4):
                            di = (2 * pi + hk // 2) % 4
                            wd_ = DR[di][:, 0:4096].rearrange("p (k c) -> p k c", k=2)
                            P.op("pe", lambda e, hk=hk, wd_=wd_, oc=oc, pb=pb, tok=tok, hb=hb:
                                 e.matmul(bank(pb), lhsT=wd_[:, hk % 2, oc * 128:(oc + 1) * 128], rhs=HBf[hb][:, hk, tok],
                                          start=(hk == 0), stop=(hk == 3)),
                                 reads=[bDR[di], bHBf[hb]], writes=[bPS[pb]], signal=(hk == 3))
                        P.op("dve", lambda e, oc=oc, pb=pb, tok=tok:
                             e.scalar_tensor_tensor(out=X[:, oc, tok], in0=bank(pb), scalar=modcol(l, 5, oc, g),
                                                    in1=X[:, oc, tok], op0=ALU.mult, op1=ALU.add),
                             reads=[bPS[pb], bMOD, bX[oc]], writes=[bX[oc]])
                        n += 1
                        if n % 2 == 0:
                            yield

            def drain(gen):
                for _ in gen:
                    pass

            def chain2(fb0):
                for fb in (fb0, fb0 + 1):
                    if fb < NFB:
                        for _ in up_gen(fb):
                            yield

            drain(chain2(0))
            for pi in range(NFB // 2):
                ug = chain2(2 * pi + 2)
                dg = down_gen(pi)
                while True:
                    a_ = next(ug, "end")
                    b_ = next(dg, "end")
                    if a_ == "end" and b_ == "end":
                        break
            P.barrier()

    def final(g):
        dst = yp if g == 0 else ys
        with contextlib.ExitStack() as ph:
            FGB = sb(ph, "FGB", [128, D], F32)
            bFGB = P.buf("FGB")
            OST = [sb(ph, "OST%d" % i, [128, D], F32) for i in range(2)]
            bOST = [P.buf("OST%d" % i) for i in range(2)]
            SSQ = sb(ph, "SSQ", [128, 8, 4], F32)
            bSSQ = P.buf("SSQ")
            RSTD = sb(ph, "RSTDF", [128, 8], F32)
            bRSTD = P.buf("RSTDF")
            JF = sb(ph, "JF", [128, 512], BF16)
            bJF = P.buf("JF")
            P.dma("sp", lambda e: e.dma_start(out=FGB[:], in_=dr["final_g"].rearrange("(a n) -> a n", a=1).partition_broadcast(128)),
                  writes=[bFGB])
            for tt in range(8):
                oi = tt % 2
                base = (tt % 2) * 4
                for q in range(4):
                    pb = base + q
                    for j in range(4):
                        c = q * 4 + j
                        P.op("pe", lambda e, c=c, pb=pb, j=j, tt=tt:
                             e.transpose(bank(pb)[:, j * 128:(j + 1) * 128], X[:, c, tt * 128:(tt + 1) * 128], IDF[:]),
                             reads=[bX[c], bIDF], writes=[bPS[pb]], signal=(j == 3))
                    P.op("act", lambda e, pb=pb, tt=tt, q=q:
                         e.activation(out=JF[:], in_=bank(pb), func=AF.Square, accum_out=SSQ[:, tt, q:q + 1]),
                         reads=[bPS[pb]], writes=[bJF, bSSQ])
                P.op("dve", lambda e, tt=tt: e.tensor_reduce(out=RSTD[:, tt:tt + 1], in_=SSQ[:, tt, :], axis=AX.X, op=ALU.add),
                     reads=[bSSQ], writes=[bRSTD])
                P.op("act", lambda e, tt=tt: e.activation(out=RSTD[:, tt:tt + 1], in_=RSTD[:, tt:tt + 1], func=AF.Sqrt,
                                                          scale=1.0 / D, bias=EPS_T[:, 0:1]),
                     reads=[bRSTD, bEPS], writes=[bRSTD])
                P.op("dve", lambda e, tt=tt: e.reciprocal(out=RSTD[:, tt:tt + 1], in_=RSTD[:, tt:tt + 1]),
                     reads=[bRSTD], writes=[bRSTD])
                for q in range(4):
                    pb = base + q
                    P.op("dve", lambda e, oi=oi, q=q, pb=pb, tt=tt:
                         e.scalar_tensor_tensor(out=OST[oi][:, q * 512:(q + 1) * 512], in0=bank(pb),
                                                scalar=RSTD[:, tt:tt + 1], in1=FGB[:, q * 512:(q + 1) * 512],
                                                op0=ALU.mult, op1=ALU.mult),
                         reads=[bPS[pb], bRSTD, bFGB], writes=[bOST[oi]])
                P.dma("sp", lambda e, oi=oi, tt=tt: e.dma_start(out=dst[tt * 128:(tt + 1) * 128, :], in_=OST[oi][:]),
                      reads=[bOST[oi]])
            P.barrier()

    EPS_T = sb(es, "EPS_T", [128, 1], F32)
    Q25_T = sb(es, "Q25_T", [128, 1], F32)
    LNE_T = sb(es, "LNE_T", [128, 1], F32)
    bEPS = P.buf("EPSC")
    P.op("pool", lambda e: e.memset(EPS_T[:], EPS), writes=[bEPS])
    P.op("pool", lambda e: e.memset(Q25_T[:], 0.25), writes=[bEPS])
    P.op("pool", lambda e: e.memset(LNE_T[:], LN_EPS), writes=[bEPS])
    P.barrier()

    def want(stage):
        return (stage < stop_after) if only is None else (stage in only)

    for g in groups:
        first = (g == groups[0])
        load_group(g, defer=(list(range(NSET1, NSET1 + 5)) if first else None))
        if want(0):
            norm_mod(0, 0, g, nxt="rg", defer=(list(range(NSET1 + 5, NSET1 + 10)) if first else None))
            rg_mixer(g)
        if want(1):
            norm_mod(0, 1, g, nxt="ffn", defer=(list(range(NSET1 + 10, 48)) if first else None), dlast=first)
            ffn(0, g)
        if want(2):
            norm_mod(1, 0, g, nxt="cm")
            cm_mixer(g)
        if want(3):
            norm_mod(1, 1, g, nxt="ffn")
            ffn(1, g)
        final(g)

    P.emit()
    P.close()
    es.close()
    return nc


_NC_CACHE = {}


def _get_nc(stop_after=99):
    if stop_after not in _NC_CACHE:
        _NC_CACHE[stop_after] = build(stop_after)
    return _NC_CACHE[stop_after]


def make_in_maps(inputs):
    f = lambda a: np.ascontiguousarray(np.asarray(a, dtype=np.float32))
    shared = {k: f(inputs[k]) for k in IN_SHAPES if k not in ("xp", "xs", "st", "cond")}
    xp = f(inputs["x_prompt"])
    xs = f(inputs["x_sample"])
    st = f(inputs["state_rglru"])
    c = f(inputs["c"])
    cctx = f(inputs["c_ctx"])
    maps = []
    for i in range(NCORES):
        m = dict(shared)
        m["xp"] = np.ascontiguousarray(xp[4 * i:4 * i + 4].reshape(T, D))
        m["xs"] = np.ascontiguousarray(xs[i])
        m["st"] = np.ascontiguousarray(st[i, 0])
        m["cond"] = np.ascontiguousarray(np.stack([cctx, c[i]], axis=0))
        maps.append(m)
    return maps


def kernel(**inputs):
    nc = _get_nc()
    maps = make_in_maps(inputs)
    res = run_bass_kernel_spmd(nc, maps, core_ids=list(range(NCORES)))
    outs = res.results
    y_prompt = np.concatenate([np.asarray(r["yp"], dtype=np.float32).reshape(4, 256, D) for r in outs], axis=0)
    y_sample = np.stack([np.asarray(r["ys"], dtype=np.float32) for r in outs], axis=0)
    new_state = np.concatenate([np.asarray(r["ns"], dtype=np.float32).reshape(4, 1, 2, D) for r in outs], axis=0)
    return (y_prompt, y_sample, new_state)
```

```python
import contextlib
import math
import os
import numpy as np
import concourse.bass as bass
import concourse.mybir as mybir
from concourse.bass_utils import run_bass_kernel_spmd

F32 = mybir.dt.float32
BF16 = mybir.dt.bfloat16
I32 = mybir.dt.int32
AF = mybir.ActivationFunctionType
ALU = mybir.AluOpType
AX = mybir.AxisListType

ENGS = ("sp", "act", "pe", "dve", "pool")
NCORES = 8
D = 2048
NCH = 16
T = 1024
DFF = 5632
NFC = DFF // 128
EPS = 1e-6
LN_EPS = 1e-5


class Buf:
    __slots__ = ("name", "w", "r", "sem", "semval")

    def __init__(self, name):
        self.name = name
        self.w = None
        self.r = []
        self.sem = None
        self.semval = 0


class Prog:
    def __init__(self, nc):
        self.nc = nc
        self.ops = {e: [] for e in ENGS}
        self.cnt = {e: 0 for e in ENGS}
        self.seen = {e: {} for e in ENGS}
        self.esem = {}
        self._ctx = []
        for e in ("act", "pe", "dve", "pool"):
            cm = nc.semaphore("s_" + e)
            self.esem[e] = cm.__enter__()
            self._ctx.append(cm)
        self.bufs = {}
        self.dma_pending = {}

    def buf(self, name):
        b = self.bufs.get(name)
        if b is None:
            b = Buf(name)
            self.bufs[name] = b
        return b

    def _dma_sem(self, b):
        if b.sem is None:
            cm = self.nc.semaphore("d_" + b.name)
            b.sem = cm.__enter__()
            self._ctx.append(cm)
        return b.sem

    def _collect(self, eng, reads, writes):
        best = {}
        def add(t):
            kind, key, val = t
            if kind == "e" and key == "pe" and eng == "pe":
                return
            k = (kind, key)
            if best.get(k, 0) < val:
                best[k] = val
        for b in reads:
            if b.w is not None:
                add(b.w)
        for b in writes:
            if b.w is not None:
                add(b.w)
            for t in b.r:
                add(t)
        waits = []
        seen = self.seen[eng]
        for k, val in best.items():
            if seen.get(k, 0) >= val:
                continue
            seen[k] = val
            kind, key = k
            sem = self.esem[key] if kind == "e" else key.sem
            waits.append((sem, val))
        return waits

    def op(self, eng, fn, reads=(), writes=(), signal=True):
        waits = self._collect(eng, reads, writes)
        if signal:
            self.cnt[eng] += 1
            t = ("e", eng, self.cnt[eng])
        else:
            t = ("e", eng, self.cnt[eng] + 1)
        self.ops[eng].append((fn, waits, signal, None))
        for b in writes:
            b.w = t
            b.r = []
        for b in reads:
            b.r.append(t)
        return t

    def dma(self, eng, fn, reads=(), writes=(), semb=None):
        if semb is None:
            semb = writes[0] if writes else reads[0]
        sem = self._dma_sem(semb)
        waits = self._collect(eng, reads, writes)
        semb.semval += 16
        t = ("d", semb, semb.semval)
        self.dma_pending[semb] = semb.semval
        self.ops[eng].append((fn, waits, False, (sem, 16)))
        for b in writes:
            b.w = t
            b.r = []
        for b in reads:
            b.r.append(t)
        return t

    def barrier(self, engs=ENGS):
        cur = {f: self.cnt[f] for f in ("act", "pe", "dve", "pool")}
        for e in engs:
            waits = []
            seen = self.seen[e]
            for f, v in cur.items():
                if f == e or v == 0:
                    continue
                k = ("e", f)
                if seen.get(k, 0) >= v:
                    continue
                seen[k] = v
                waits.append((self.esem[f], v))
            for b, val in self.dma_pending.items():
                k = ("d", b)
                if seen.get(k, 0) >= val:
                    continue
                seen[k] = val
                waits.append((b.sem, val))
            self.ops[e].append((None, waits, False, None))
        self.dma_pending = {}

    def emit(self):
        nc = self.nc
        ops = self.ops
        esem = self.esem

        def run(engname, e):
            for fn, waits, signal, dinc in ops[engname]:
                for sem, val in waits:
                    e.wait_ge(sem, val)
                if fn is None:
                    continue
                ins = fn(e)
                if dinc is not None:
                    ins.then_inc(dinc[0], dinc[1])
                elif signal:
                    ins.then_inc(esem[engname], 1)

        with nc.Block() as block:
            @block.sync
            def _(e):
                run("sp", e)

            @block.scalar
            def _(e):
                run("act", e)

            @block.tensor
            def _(e):
                run("pe", e)

            @block.vector
            def _(e):
                run("dve", e)

            @block.gpsimd
            def _(e):
                run("pool", e)

    def close(self):
        for cm in reversed(self._ctx):
            cm.__exit__(None, None, None)


def rap(base, dims):
    return bass.AP(base.tensor, base.offset, [list(base.ap[0])] + [list(d) for d in dims])


PARAM_SEGS = [
    ("cond", "cond", "g (n p) -> (g n) p", 32),
    ("st", "st", "d (n p) -> (d n) p", 32),
    ("n1g", "norm1_g", "l (n p) -> (l n) p", 32),
    ("n2g", "norm2_g", "l (n p) -> (l n) p", 32),
    ("fg", "final_g", "(n p) -> n p", 16),
    ("bada", "b_ada", "l (n p) -> (l n) p", 192),
    ("rcw", "rg_conv_w", "a k (n p) -> (a k n) p", 64),
    ("rcb", "rg_conv_b", "a (n p) -> (a n) p", 16),
    ("rba", "rg_b_a", "a d (n p) -> (a d n) p", 32),
    ("rbx", "rg_b_x", "a d (n p) -> (a d n) p", 32),
    ("lam", "rg_lam", "a d (n p) -> (a d n) p", 32),
    ("cbi", "cm_b_in", "a (n p) -> (a n) p", 32),
    ("clg", "cm_ln_g", "a (n p) -> (a n) p", 16),
    ("fcw", "ffn_conv_w", "l k (n p) -> (l k n) p", 528),
    ("fcb", "ffn_conv_b", "l (n p) -> (l n) p", 176),
]
NPROWS = sum(s[3] for s in PARAM_SEGS)
NPBLK = (NPROWS + 127) // 128

IN_SHAPES = {
    "xp": [T, D], "xs": [T, D], "st": [2, D], "cond": [2, D],
    "norm1_g": [2, D], "norm2_g": [2, D], "w_ada": [2, D, 6 * D], "b_ada": [2, 6 * D],
    "rg_w_in": [1, D, 2 * D], "rg_conv_w": [1, 4, D], "rg_conv_b": [1, D],
    "rg_w_a": [1, 2, 16, 128, 128], "rg_b_a": [1, 2, D], "rg_w_x": [1, 2, 16, 128, 128],
    "rg_b_x": [1, 2, D], "rg_lam": [1, 2, D], "rg_w_out": [1, D, D],
    "cm_w_in": [1, D, 2 * D], "cm_b_in": [1, 2 * D], "cm_ln_g": [1, D], "cm_ln_b": [1, D],
    "cm_w_s": [1, 8, 128, 128], "cm_b_s": [1, 8, 128], "cm_w_out": [1, D, D],
    "ffn_w_up": [2, D, 2 * DFF], "ffn_conv_w": [2, 3, 2 * DFF], "ffn_conv_b": [2, 2 * DFF],
    "ffn_w_down": [2, DFF, D], "final_g": [D],
}


def build(stop_after=99, groups=(0, 1), only=None):
    nc = bass.Bass("TRN2", target_bir_lowering=False)
    dr = {}
    for name, shp in IN_SHAPES.items():
        dr[name] = nc.dram_tensor(name, list(shp), F32, kind="ExternalInput").ap()
    yp = nc.dram_tensor("yp", [T, D], F32, kind="ExternalOutput").ap()
    ys = nc.dram_tensor("ys", [T, D], F32, kind="ExternalOutput").ap()
    nsd = nc.dram_tensor("ns", [8, D], F32, kind="ExternalOutput").ap()

    P = Prog(nc)
    es = contextlib.ExitStack()
    uid = [0]

    def sb(stack, name, shape, dt):
        uid[0] += 1
        return stack.enter_context(nc.sbuf_tensor("%s_%d" % (name, uid[0]), list(shape), dt))

    X = sb(es, "X", [128, NCH, T], F32)
    XN = sb(es, "XN", [128, NCH, T], BF16)
    PT = sb(es, "PT", [128, NPBLK * 128], F32)
    MOD = sb(es, "MOD", [128, 2, 96, 2], F32)
    GM = sb(es, "GM", [128, 2, 2, NCH, 2], F32)
    CL = sb(es, "CL", [128, 2, 32], F32)
    HB_ = sb(es, "HBt", [128, 2, 32], F32)
    TR = sb(es, "TR", [128, 8, 16], F32)
    TC = sb(es, "TC", [128, 8, 64], F32)
    NS = sb(es, "NS", [128, 128], F32)
    IDF = sb(es, "IDF", [128, 128], F32)
    ONB = sb(es, "ONB", [128, 128], BF16)
    ONF = sb(es, "ONF", [128, 128], F32)
    SC = sb(es, "SC", [128, NCH, 2], BF16)
    NRING = 4
    RINGT = sb(es, "RINGT", [128, 4 * 4096], BF16)
    RING = [RINGT[:, i * 4096:(i + 1) * 4096] for i in range(4)]
    PS = es.enter_context(nc.psum_tensor("PS", [128, 8 * 512], F32))

    bX = [P.buf("X%d" % c) for c in range(NCH)]
    bXN = [P.buf("XN%d" % c) for c in range(NCH)]
    bPT, bMOD, bGM, bCL, bHB, bTR, bTC, bNS = (P.buf(n) for n in ("PT", "MOD", "GM", "CL", "HB", "TR", "TC", "NS"))
    bIDF, bONB, bONF, bSC = (P.buf(n) for n in ("IDF", "ONB", "ONF", "SC"))
    bWG = [P.buf("WG%d" % i) for i in range(2)]
    bRING = [P.buf("RING%d" % i) for i in range(NRING)]
    bPS = [P.buf("PS%d" % i) for i in range(8)]

    def bank(b, n=512):
        return PS[:, b * 512:b * 512 + n]

    poff = {}
    o = 0
    for name, _, _, rows in PARAM_SEGS:
        poff[name] = o
        o += rows

    def pcol(name, idx):
        return PT[:, poff[name] + idx:poff[name] + idx + 1]

    ring_state = {"i": 0, "n": 4}

    def ring_load(src_ap, view):
        s = ring_state["i"] % ring_state["n"]
        ring_state["i"] += 1
        dst = view(RING[s])
        P.dma("pool", lambda e, dst=dst, src=src_ap: e.dma_start(out=dst, in_=src), writes=[bRING[s]])
        return s

    def v_k16(tile, ncols):
        return tile[:, 0:16 * ncols].rearrange("p (k c) -> p k c", k=16)

    prefetched = {}

    def load_wcols(w2d, c0, ncols=256):
        key = (w2d.tensor.name, w2d.offset, c0, ncols)
        if key in prefetched:
            return prefetched.pop(key)
        src = w2d[:, c0:c0 + ncols].rearrange("(k p) c -> p k c", p=128)
        s = ring_load(src, lambda t: v_k16(t, ncols))
        return s, v_k16(RING[s], ncols)

    pair_pre = {}

    def load_pair(w2d, c0, pair):
        key = (w2d.tensor.name, w2d.offset, c0, pair)
        dst = RINGT[:, pair * 8192:(pair + 1) * 8192].rearrange("p (k c) -> p k c", k=16)
        if key in pair_pre:
            pair_pre.pop(key)
            return dst
        src = w2d[:, c0:c0 + 512].rearrange("(k p) c -> p k c", p=128)
        P.dma("pool", lambda e, dst=dst, src=src: e.dma_start(out=dst, in_=src),
              writes=[bRING[2 * pair], bRING[2 * pair + 1]])
        return dst

    def prefetch_pair(w2d, c0, pair):
        load_pair(w2d, c0, pair)
        pair_pre[(w2d.tensor.name, w2d.offset, c0, pair)] = True

    def prefetch_wcols(w2d, c0, ncols=256):
        key = (w2d.tensor.name, w2d.offset, c0, ncols)
        assert key not in prefetched
        r_ = load_wcols(w2d, c0, ncols)
        prefetched[key] = r_

    NDEF = 15
    NSET1 = 48 - NDEF

    def mod_window_issue(stack, blks):
        items = []
        for i, blk in enumerate(blks):
            t = sb(stack, "DW%d" % i, [128, 4096], BF16)
            b = P.buf("DW%d" % i)
            src = dr["w_ada"][1][:, blk * 256:(blk + 1) * 256].rearrange("(k p) c -> p k c", p=128)
            dst = v_k16(t, 256)
            P.dma("pool", lambda e, dst=dst, src=src: e.dma_start(out=dst, in_=src), writes=[b])
            items.append((b, dst, blk))
        return items

    def mod_window_consume(items, last=False):
        for (b, wv, blk) in items:
            for n2 in range(2):
                n = blk * 2 + n2
                for kc in range(16):
                    P.op("pe", lambda e, wv=wv, n2=n2, kc=kc, n=n:
                         e.matmul(bank(6)[:, 2 * n:2 * n + 2], lhsT=wv[:, kc, n2 * 128:(n2 + 1) * 128],
                                  rhs=SC[:, kc, :], start=(kc == 0), stop=(kc == 15)),
                         reads=[b, bSC], writes=[bPS[6]], signal=(kc == 15))
        n0 = items[0][2] * 2
        cnt = len(items) * 2
        for g in range(2):
            P.op("dve", lambda e, g=g, n0=n0, cnt=cnt:
                 e.tensor_copy(out=MOD[:, 1, n0:n0 + cnt, g], in_=rap(bank(6)[:, 2 * n0 + g:2 * n0 + g + 1], [[2, cnt]])),
                 reads=[bPS[6]], writes=[bMOD])
        if last:
            for g in range(2):
                P.op("dve", lambda e, g=g:
                     e.tensor_tensor(out=MOD[:, 1, :, g], in0=MOD[:, 1, :, g],
                                     in1=PT[:, poff["bada"] + 96:poff["bada"] + 192], op=ALU.add),
                     reads=[bMOD, bPT], writes=[bMOD])
            mod_gm(1)

    def mod_gen(l, pb, loader=None, ncols=256, nblk=None):
        for blk in range(6 * D // ncols if nblk is None else nblk):
            if loader is None:
                s, wv = load_wcols(dr["w_ada"][l], blk * ncols, ncols)
                rb = bRING[s]
            else:
                rb, wv = loader(dr["w_ada"][l], blk * ncols)
            for n2 in range(ncols // 128):
                n = blk * (ncols // 128) + n2
                for kc in range(16):
                    P.op("pe", lambda e, wv=wv, n2=n2, kc=kc, n=n, pb=pb:
                         e.matmul(bank(pb)[:, 2 * n:2 * n + 2], lhsT=wv[:, kc, n2 * 128:(n2 + 1) * 128],
                                  rhs=SC[:, kc, :], start=(kc == 0), stop=(kc == 15)),
                         reads=[rb, bSC], writes=[bPS[pb]], signal=(kc == 15))
            yield

    def mod_finalize(l, pb):
        for g in range(2):
            P.op("dve", lambda e, l=l, g=g, pb=pb:
                 e.tensor_tensor(out=MOD[:, l, :, g], in0=rap(bank(pb)[:, g:g + 1], [[2, 96]]),
                                 in1=PT[:, poff["bada"] + l * 96:poff["bada"] + (l + 1) * 96], op=ALU.add),
                 reads=[bPS[pb], bPT], writes=[bMOD])
        mod_gm(l)

    def mod_gm(l):
        for which in range(2):
            nm = "n1g" if which == 0 else "n2g"
            for g in range(2):
                P.op("dve", lambda e, l=l, which=which, g=g, nm=nm:
                     e.scalar_tensor_tensor(out=GM[:, l, which, :, g],
                                            in0=MOD[:, l, (1 + 3 * which) * 16:(2 + 3 * which) * 16, g],
                                            scalar=1.0,
                                            in1=PT[:, poff[nm] + l * 16:poff[nm] + (l + 1) * 16],
                                            op0=ALU.add, op1=ALU.mult),
                     reads=[bMOD, bPT], writes=[bGM])

    EPS_T = sb(es, "EPS_T", [128, 1], F32)
    Q25_T = sb(es, "Q25_T", [128, 1], F32)
    LNE_T = sb(es, "LNE_T", [128, 1], F32)
    bEPS = P.buf("EPSC")
    P.op("pool", lambda e: e.memset(EPS_T[:], EPS), writes=[bEPS])
    P.op("pool", lambda e: e.memset(Q25_T[:], 0.25), writes=[bEPS])
    P.op("pool", lambda e: e.memset(LNE_T[:], LN_EPS), writes=[bEPS])

    with contextlib.ExitStack() as ph:
        STG = sb(ph, "STG", [128, NPBLK, 128], F32)
        bSTG = P.buf("STG")
        TMPA = sb(ph, "TMPA", [128, 512], F32)
        bTMPA = P.buf("TMPA")

        P.op("pool", lambda e: e.memset(IDF[:], 0.0), writes=[bIDF])
        P.op("pool", lambda e: e.affine_select(out=IDF[:], in_=IDF[:], compare_op=ALU.not_equal, fill=1.0,
                                               base=0, pattern=[[-1, 128]], channel_multiplier=1),
             reads=[bIDF], writes=[bIDF])
        P.op("pool", lambda e: e.memset(ONB[:], 1.0), writes=[bONB])
        P.op("pool", lambda e: e.memset(ONF[:], 1.0), writes=[bONF])
        bSTG0 = P.buf("STG0")
        P.op("dve", lambda e: e.memset(STG[:], 0.0), writes=[bSTG, bSTG0])
        P.op("dve", lambda e: e.memset(NS[:], 0.0), writes=[bNS])

        r = 0
        for name, dname, pat, rows in PARAM_SEGS:
            seg = dr[dname].rearrange(pat, p=128)
            a = 0
            while a < rows:
                blk, r0 = divmod(r + a, 128)
                n = min(rows - a, 128 - r0)
                P.dma("sp", lambda e, blk=blk, r0=r0, n=n, seg=seg, a=a:
                      e.dma_start(out=STG[r0:r0 + n, blk, :], in_=seg[a:a + n, :]), writes=[bSTG0 if blk == 0 else bSTG])
                a += n
            r += rows
        def pt_block(blk):
            b = blk % 2
            P.op("pe", lambda e, blk=blk, b=b: e.transpose(bank(b, 128), STG[:, blk, :], IDF[:]),
                 reads=[bSTG0 if blk == 0 else bSTG, bIDF], writes=[bPS[b]])
            P.op("dve", lambda e, blk=blk, b=b: e.tensor_copy(out=PT[:, blk * 128:(blk + 1) * 128], in_=bank(b, 128)),
                 reads=[bPS[b]], writes=[bPT])
        pt_block(0)

        for g in range(2):
            P.op("act", lambda e, g=g: e.activation(out=SC[:, :, g], in_=PT[:, poff["cond"] + g * 16:poff["cond"] + g * 16 + 16],
                                                    func=AF.Silu), reads=[bPT], writes=[bSC])

        ARG2 = sb(ph, "ARG2", [128, 640], F32)
        KI = sb(ph, "KI", [128, 640], I32)
        KF = sb(ph, "KF", [128, 640], F32)
        bARG2, bKI, bKF = P.buf("ARG2"), P.buf("KI"), P.buf("KF")
        OM2 = sb(ph, "OM2", [128, 4], F32)
        POS2 = sb(ph, "POS2", [128, 80], F32)
        TI2 = sb(ph, "TI2", [128, 96], I32)
        bOM2, bPOS2, bTI2 = P.buf("OM2"), P.buf("POS2"), P.buf("TI2")
        P.op("pool", lambda e: e.iota(out=TI2[:, 0:4], pattern=[[128, 4]], base=0, channel_multiplier=1), writes=[bTI2])
        P.op("pool", lambda e: e.iota(out=TI2[:, 16:32], pattern=[[1, 16]], base=0, channel_multiplier=0),
             reads=[bTI2], writes=[bTI2])
        P.op("pool", lambda e: e.iota(out=TI2[:, 32:96], pattern=[[1, 64]], base=0, channel_multiplier=0),
             reads=[bTI2], writes=[bTI2])
        P.op("dve", lambda e: e.tensor_copy(out=OM2[:], in_=TI2[:, 0:4]), reads=[bTI2], writes=[bOM2])
        P.op("dve", lambda e: e.tensor_copy(out=POS2[:], in_=TI2[:, 16:96]), reads=[bTI2], writes=[bPOS2])
        P.op("act", lambda e: e.activation(out=OM2[:], in_=OM2[:], func=AF.Exp, scale=-math.log(10000.0) / 512.0),
             reads=[bOM2], writes=[bOM2])
        A4 = ARG2[:].rearrange("p (a c n) -> p a c n", a=2, c=4)
        for cc in range(4):
            P.op("dve", lambda e, cc=cc: e.tensor_scalar(out=A4[:, 0, cc, :], in0=POS2[:], scalar1=OM2[:, cc:cc + 1],
                                                         scalar2=None, op0=ALU.mult),
                 reads=[bPOS2, bOM2], writes=[bARG2])
        P.op("dve", lambda e: e.tensor_scalar(out=A4[:, 1, :, :], in0=A4[:, 0, :, :], scalar1=math.pi / 2, scalar2=None,
                                              op0=ALU.add), reads=[bARG2], writes=[bARG2])
        P.op("dve", lambda e: e.tensor_scalar(out=KI[:], in0=ARG2[:], scalar1=1.0 / (2 * math.pi), scalar2=None,
                                              op0=ALU.mult), reads=[bARG2], writes=[bKI])
        P.op("dve", lambda e: e.tensor_copy(out=KF[:], in_=KI[:]), reads=[bKI], writes=[bKF])
        C1 = 6.28125
        C2 = 2 * math.pi - 6.28125
        P.op("dve", lambda e: e.scalar_tensor_tensor(out=ARG2[:], in0=KF[:], scalar=-C1, in1=ARG2[:], op0=ALU.mult,
                                                     op1=ALU.add), reads=[bKF, bARG2], writes=[bARG2])
        P.op("dve", lambda e: e.scalar_tensor_tensor(out=ARG2[:], in0=KF[:], scalar=-C2, in1=ARG2[:], op0=ALU.mult,
                                                     op1=ALU.add), reads=[bKF, bARG2], writes=[bARG2])
        P.op("dve", lambda e: e.tensor_scalar(out=ARG2[:], in0=ARG2[:], scalar1=-math.pi, scalar2=math.pi, op0=ALU.max,
                                              op1=ALU.min), reads=[bARG2], writes=[bARG2])
        P.op("act", lambda e: e.activation(out=ARG2[:], in_=ARG2[:], func=AF.Sin), reads=[bARG2], writes=[bARG2])
        for a in range(2):
            P.op("dve", lambda e, a=a: e.tensor_copy(out=TR[:, a * 4:(a + 1) * 4, :], in_=A4[:, a, :, 0:16]),
                 reads=[bARG2], writes=[bTR])
            P.op("dve", lambda e, a=a: e.tensor_copy(out=TC[:, a * 4:(a + 1) * 4, :], in_=A4[:, a, :, 16:80]),
                 reads=[bARG2], writes=[bTC])
        ring_state["n"] = 4
        for _ in mod_gen(0, 2):
            pass
        for blk in range(1, NPBLK):
            pt_block(blk)
        mod_finalize(0, 2)
        for _ in mod_gen(1, 3, nblk=NSET1):
            pass
        for g in range(2):
            P.op("dve", lambda e, g=g:
                 e.tensor_copy(out=MOD[:, 1, 0:2 * NSET1, g], in_=rap(bank(3)[:, g:g + 1], [[2, 2 * NSET1]])),
                 reads=[bPS[3]], writes=[bMOD])
        lam = PT[:, poff["lam"]:poff["lam"] + 32]
        P.op("act", lambda e: e.activation(out=TMPA[:, 0:32], in_=lam, func=AF.Exp, scale=-1.0), reads=[bPT], writes=[bTMPA])
        P.op("act", lambda e: e.activation(out=TMPA[:, 32:64], in_=TMPA[:, 0:32], func=AF.Ln, bias=1.0, scale=1.0),
             reads=[bTMPA], writes=[bTMPA])
        P.op("dve", lambda e: e.tensor_scalar(out=CL[:, 0, :], in0=TMPA[:, 32:64], scalar1=-4.0, scalar2=None, op0=ALU.mult),
             reads=[bTMPA], writes=[bCL])
        P.op("dve", lambda e: e.tensor_scalar(out=CL[:, 1, :], in0=TMPA[:, 32:64], scalar1=-8.0, scalar2=None, op0=ALU.mult),
             reads=[bTMPA], writes=[bCL])
        P.op("dve", lambda e: e.tensor_scalar(out=HB_[:, 0, :], in0=PT[:, poff["rba"]:poff["rba"] + 32], scalar1=0.5,
                                              scalar2=None, op0=ALU.mult), reads=[bPT], writes=[bHB])
        P.op("dve", lambda e: e.tensor_scalar(out=HB_[:, 1, :], in0=PT[:, poff["rbx"]:poff["rbx"] + 32], scalar1=0.5,
                                              scalar2=None, op0=ALU.mult), reads=[bPT], writes=[bHB])

        P.barrier()

    def modcol(l, k6, c, g):
        return MOD[:, l, k6 * 16 + c, g:g + 1]

    def load_group(g, defer=None):
        src = dr["xp"] if g == 0 else dr["xs"]
        with contextlib.ExitStack() as ph:
            STI = [sb(ph, "STI%d" % i, [128, D], F32) for i in range(2)]
            bSTI = [P.buf("STI%d" % i) for i in range(2)]
            ditems = mod_window_issue(ph, defer) if defer else None
            for tt in range(8):
                si = tt % 2
                P.dma("sp", lambda e, si=si, tt=tt: e.dma_start(out=STI[si][:], in_=src[tt * 128:(tt + 1) * 128, :]),
                      writes=[bSTI[si]])
                for q in range(4):
                    pb = (tt * 4 + q) % 4
                    for j in range(4):
                        c = q * 4 + j
                        P.op("pe", lambda e, si=si, c=c, pb=pb, j=j:
                             e.transpose(bank(pb)[:, j * 128:(j + 1) * 128], STI[si][:, c * 128:(c + 1) * 128], IDF[:]),
                             reads=[bSTI[si], bIDF], writes=[bPS[pb]], signal=(j == 3))
                    outap = X[:, q * 4:(q + 1) * 4, tt * 128:(tt + 1) * 128]
                    inap = bank(pb).rearrange("p (j t) -> p j t", j=4)
                    xb_ = [bX[q * 4 + j] for j in range(4)]
                    if g == 0:
                        P.op("act", lambda e, outap=outap, inap=inap: e.activation(out=outap, in_=inap, func=AF.Copy),
                             reads=[bPS[pb]], writes=xb_)
                    else:
                        o4 = rap(X[:, q * 4, tt * 128:tt * 128 + 1], [[T, 4], [64, 2], [1, 64]])
                        i4 = rap(bank(pb)[:, 0:1], [[128, 4], [64, 2], [1, 64]])
                        if q < 2:
                            t4 = rap(TR[:, q * 4, 2 * tt:2 * tt + 1], [[16, 4], [1, 2], [0, 64]])
                            rb = bTR
                        else:
                            t4 = rap(TC[:, (q - 2) * 4, 0:1], [[64, 4], [0, 2], [1, 64]])
                            rb = bTC
                        P.op("dve", lambda e, o4=o4, i4=i4, t4=t4: e.tensor_tensor(out=o4, in0=i4, in1=t4, op=ALU.add),
                             reads=[bPS[pb], rb], writes=xb_)
            if ditems:
                mod_window_consume(ditems)
            P.barrier()

    def rstd_half(ph_bufs, th):
        SQ, bSQ, RS, bRS = ph_bufs
        tok = slice(th * 512, (th + 1) * 512)
        for q in range(4):
            rd = [bX[q * 4 + j] for j in range(4)]
            if q % 2 == 0:
                P.op("act", lambda e, q=q, tok=tok: e.activation(out=SQ[q][:], in_=X[:, q * 4:(q + 1) * 4, tok], func=AF.Square),
                     reads=rd, writes=[bSQ[q]])
            else:
                P.op("dve", lambda e, q=q, tok=tok: e.tensor_tensor(out=SQ[q][:], in0=X[:, q * 4:(q + 1) * 4, tok],
                                                                    in1=X[:, q * 4:(q + 1) * 4, tok], op=ALU.mult),
                     reads=rd, writes=[bSQ[q]])
        for q in (0, 1, 2, 3):
            for j in range(4):
                P.op("pe", lambda e, q=q, j=j:
                     e.matmul(bank(7), lhsT=ONB[:], rhs=SQ[q][:, j, :],
                              start=(q == 0 and j == 0), stop=(q == 3 and j == 3)),
                     reads=[bSQ[q], bONB], writes=[bPS[7]], signal=(j == 3))
        P.op("act", lambda e, th=th: e.activation(out=RS[th][:], in_=bank(7), func=AF.Ln, scale=1.0 / D, bias=EPS_T[:, 0:1]),
             reads=[bPS[7], bEPS], writes=[bRS[th]])
        P.op("act", lambda e, th=th: e.activation(out=RS[th][:], in_=RS[th][:], func=AF.Exp, scale=-0.5),
             reads=[bRS[th]], writes=[bRS[th]])

    def norm_mod(l, which, g, nxt=None, defer=None, dlast=False):
        ring_state["n"] = 4
        if nxt == "rg":
            prefetch_wcols(dr["rg_w_in"][0], D)
            prefetch_wcols(dr["rg_w_in"][0], 0)
        elif nxt == "ffn":
            prefetch_wcols(dr["ffn_w_up"][l], 0)
            prefetch_wcols(dr["ffn_w_up"][l], DFF)
        elif nxt == "cm":
            prefetch_pair(dr["cm_w_in"][0], D, 0)
        with contextlib.ExitStack() as ph:
            SQ = [sb(ph, "SQ%d" % i, [128, 4, 512], BF16) for i in range(4)]
            bSQ = [P.buf("SQ%d" % i) for i in range(4)]
            RS = [sb(ph, "RS%d" % i, [128, 512], F32) for i in range(2)]
            bRS = [P.buf("RS%d" % i) for i in range(2)]
            TN = [sb(ph, "TN%d" % i, [128, 512], F32) for i in range(2)]
            bTN = [P.buf("TN%d" % i) for i in range(2)]
            ditems = mod_window_issue(ph, defer) if defer else None
            for th in range(2):
                tok = slice(th * 512, (th + 1) * 512)
                rstd_half((SQ, bSQ, RS, bRS), th)
                for c in range(NCH):
                    ti = c % 2
                    P.op("dve", lambda e, c=c, ti=ti, tok=tok, th=th:
                         e.scalar_tensor_tensor(out=TN[ti][:], in0=X[:, c, tok], scalar=GM[:, l, which, c, g:g + 1],
                                                in1=RS[th][:], op0=ALU.mult, op1=ALU.mult),
                         reads=[bX[c], bGM, bRS[th]], writes=[bTN[ti]])
                    P.op("act", lambda e, c=c, ti=ti, tok=tok:
                         e.activation(out=XN[:, c, tok], in_=TN[ti][:], func=AF.Identity,
                                      bias=modcol(l, 3 * which, c, g), scale=1.0),
                         reads=[bTN[ti], bMOD], writes=[bXN[c]])
            if ditems:
                mod_window_consume(ditems, last=dlast)
            P.barrier()

    def rg_mixer(g, modl=None):
        S = 4 if g == 0 else 1
        L = T // S
        l = 0
        ring_state["n"] = 4 if modl is None else 3
        dbk = 6 if modl is None else 0
        if modl is not None:
            bMRh = [P.buf("MRh%d" % i) for i in range(2)]
            mi = [0]

            def mloader(w2d, c0):
                i = mi[0] % 2
                mi[0] += 1
                src = w2d[:, c0:c0 + 128].rearrange("(k p) c -> p k c", p=128)
                dst = RING[3][:, i * 2048:(i + 1) * 2048].rearrange("p (k c) -> p k c", k=16)
                P.dma("pool", lambda e, dst=dst, src=src: e.dma_start(out=dst, in_=src), writes=[bMRh[i]])
                return bMRh[i], dst
            mg = mod_gen(modl, 7, mloader, 128)
        else:
            mg = iter(())

        def modstep(k):
            for _ in range(k):
                next(mg, None)
        with contextlib.ExitStack() as ph:
            YG = sb(ph, "YG", [128, NCH, T], BF16)
            bYG = [P.buf("YG%d" % c) for c in range(NCH)]
            WG = [sb(ph, "WG%d" % i, [128, 2, 2, 128], BF16) for i in range(2)]
            XB = sb(ph, "XB", [128, T], F32)
            XBb = sb(ph, "XBb", [128, T], BF16)
            bXB, bXBb = P.buf("XB"), P.buf("XBb")
            TT_ = [sb(ph, "T%d" % i, [128, T], F32) for i in range(5)]
            bT = [P.buf("T%d" % i) for i in range(5)]
            T1, T2, T3, T4, T5 = TT_
            win = dr["rg_w_in"][0]
            ps01 = PS[:, 0:1024]
            ps23 = PS[:, 1024:2048]
            ps45 = PS[:, 2048:3072]
            ps67 = PS[:, dbk * 512:dbk * 512 + 1024]

            def seg3(ap2d, a, n):
                return rap(ap2d[:, a:a + 1], [[L, S], [1, n]])

            XB2 = [XB, sb(ph, "XB2", [128, T], F32)]
            XBb2 = [XBb, sb(ph, "XBb2", [128, T], BF16)]
            bXB2 = [bXB, P.buf("XB2")]
            bXBb2 = [bXBb, P.buf("XBb2")]
            wslots = {}

            def loads(h):
                if h % 2 == 0:
                    wslots["x"] = load_wcols(win, D + h * 128, 256)
                    wslots["g"] = load_wcols(win, h * 128, 256)
                wi = h % 2
                P.dma("pool", lambda e, wi=wi, h=h: e.dma_start(out=WG[wi][:, 0, :, :],
                                                                in_=dr["rg_w_a"][0, :, h].rearrange("d i j -> i d j")),
                      writes=[bWG[wi]])
                P.dma("pool", lambda e, wi=wi, h=h: e.dma_start(out=WG[wi][:, 1, :, :],
                                                                in_=dr["rg_w_x"][0, :, h].rearrange("d i j -> i d j")),
                      writes=[bWG[wi]])
                return wslots["x"], wslots["g"]

            hw = {}

            def A_(h):
                (sx, wx), _ = hw[h]
                hh = h % 2
                for th in range(2):
                    for kc in range(16):
                        P.op("pe", lambda e, th=th, kc=kc, wx=wx, hh=hh:
                             e.matmul(bank(th), lhsT=wx[:, kc, hh * 128:(hh + 1) * 128], rhs=XN[:, kc, th * 512:(th + 1) * 512],
                                      start=(kc == 0), stop=(kc == 15)),
                             reads=[bRING[sx], bXN[kc]], writes=[bPS[th]], signal=(kc == 15))

            def D_(h):
                _, (sg_, wg_) = hw[h]
                hh = h % 2
                for th in range(2):
                    for kc in range(16):
                        P.op("pe", lambda e, th=th, kc=kc, wg_=wg_, hh=hh:
                             e.matmul(bank(dbk + th), lhsT=wg_[:, kc, hh * 128:(hh + 1) * 128],
                                      rhs=XN[:, kc, th * 512:(th + 1) * 512], start=(kc == 0), stop=(kc == 15)),
                             reads=[bRING[sg_], bXN[kc]], writes=[bPS[dbk + th]], signal=(kc == 15))

            def conv_a(h):
                xb, bxb = XB2[h % 2], bXB2[h % 2]
                P.op("act", lambda e, h=h, xb=xb: e.activation(out=xb[:], in_=ps01, func=AF.Identity,
                                                               scale=pcol("rcw", 2 * 16 + h), bias=pcol("rcb", h)),
                     reads=[bPS[0], bPS[1], bPT], writes=[bxb])
                for (k, dst0, src0, n) in ((0, 2, 0, L - 2), (1, 1, 0, L - 1), (3, 0, 1, L - 1)):
                    P.op("dve", lambda e, h=h, k=k, dst0=dst0, src0=src0, n=n, xb=xb:
                         e.scalar_tensor_tensor(out=seg3(xb, dst0, n), in0=seg3(ps01, src0, n),
                                                scalar=pcol("rcw", k * 16 + h), in1=seg3(xb, dst0, n),
                                                op0=ALU.mult, op1=ALU.add),
                         reads=[bPS[0], bPS[1], bPT, bxb], writes=[bxb])

            def conv_b(h):
                P.op("act", lambda e, h=h: e.activation(out=XBb2[h % 2][:], in_=XB2[h % 2][:], func=AF.Copy),
                     reads=[bXB2[h % 2]], writes=[bXBb2[h % 2]])

            def gelu_(h):
                P.op("act", lambda e, h=h: e.activation(out=YG[:, h, :], in_=ps67, func=AF.Gelu_apprx_tanh),
                     reads=[bPS[dbk], bPS[dbk + 1]], writes=[bYG[h]])

            def B_(h, d):
                wi = h % 2
                xbb, bxbb = XBb2[h % 2], bXBb2[h % 2]
                for th in range(2):
                    P.op("pe", lambda e, th=th, d=d, wi=wi, xbb=xbb:
                         e.matmul(bank(2 + th), lhsT=WG[wi][:, 0, d, :], rhs=xbb[:, th * 512:(th + 1) * 512],
                                  start=True, stop=True),
                         reads=[bWG[wi], bxbb], writes=[bPS[2 + th]])
                for th in range(2):
                    P.op("pe", lambda e, th=th, d=d, wi=wi, xbb=xbb:
                         e.matmul(bank(4 + th), lhsT=WG[wi][:, 1, d, :], rhs=xbb[:, th * 512:(th + 1) * 512],
                                  start=True, stop=True),
                         reads=[bWG[wi], bxbb], writes=[bPS[4 + th]])

            def tanh2(h, d):
                col = d * 16 + h
                P.op("act", lambda e, col=col: e.activation(out=T1[:], in_=ps23, func=AF.Tanh, scale=0.5,
                                                            bias=HB_[:, 0, col:col + 1]),
                     reads=[bPS[2], bPS[3], bHB], writes=[bT[0]])
                P.op("act", lambda e, col=col: e.activation(out=T3[:], in_=ps45, func=AF.Tanh, scale=0.5,
                                                            bias=HB_[:, 1, col:col + 1]),
                     reads=[bPS[4], bPS[5], bHB], writes=[bT[2]])

            def chain(h, d):
                col = d * 16 + h
                xb, bxb = XB2[h % 2], bXB2[h % 2]
                P.op("act", lambda e, col=col: e.activation(out=T2[:], in_=T1[:], func=AF.Exp,
                                                            scale=CL[:, 0, col:col + 1], bias=CL[:, 0, col:col + 1]),
                     reads=[bT[0], bCL], writes=[bT[1]])
                P.op("act", lambda e, col=col: e.activation(out=T1[:], in_=T1[:], func=AF.Exp,
                                                            scale=CL[:, 1, col:col + 1], bias=CL[:, 1, col:col + 1]),
                     reads=[bT[0], bCL], writes=[bT[0]])
                P.op("act", lambda e: e.activation(out=T1[:], in_=T1[:], func=AF.Sqrt, scale=-0.25, bias=Q25_T[:, 0:1]),
                     reads=[bT[0], bEPS], writes=[bT[0]])
                P.op("dve", lambda e, xb=xb: e.scalar_tensor_tensor(out=T3[:], in0=T3[:], scalar=1.0, in1=xb[:],
                                                                    op0=ALU.add, op1=ALU.mult),
                     reads=[bT[2], bxb], writes=[bT[2]])
                P.op("dve", lambda e: e.tensor_tensor(out=T3[:], in0=T3[:], in1=T1[:], op=ALU.mult),
                     reads=[bT[2], bT[0]], writes=[bT[2]])
                HD = T4 if d == 0 else T5
                bHD = bT[3] if d == 0 else bT[4]
                for s_ in range(S):
                    a0 = s_ * L
                    if d == 0:
                        oa, da, ua = HD[:, a0:a0 + L], T2[:, a0:a0 + L], T3[:, a0:a0 + L]
                    else:
                        oa = rap(HD[:, a0 + L - 1:a0 + L], [[-1, L]])
                        da = rap(T2[:, a0 + L - 1:a0 + L], [[-1, L]])
                        ua = rap(T3[:, a0 + L - 1:a0 + L], [[-1, L]])
                    init = 0.0 if g == 0 else pcol("st", col)
                    P.op("dve", lambda e, oa=oa, da=da, ua=ua, init=init:
                         e.tensor_tensor_scan(out=oa, data0=da, data1=ua, initial=init, op0=ALU.mult, op1=ALU.add),
                         reads=[bT[1], bT[2], bPT], writes=[bHD])
                if g == 0:
                    pos = (L - 1) if d == 0 else 0
                    P.op("dve", lambda e, HD=HD, pos=pos, col=col:
                         e.tensor_copy(out=rap(NS[:, col:col + 1], [[32, S]]), in_=rap(HD[:, pos:pos + 1], [[L, S]])),
                         reads=[bHD], writes=[bNS])

            hw[0] = loads(0)
            A_(0)
            conv_a(0)
            conv_b(0)
            D_(0)
            for h in range(NCH):
                nx = h + 1 < NCH
                if dbk == 0:
                    gelu_(h)
                if nx:
                    hw[h + 1] = loads(h + 1)
                    A_(h + 1)
                modstep(2)
                if dbk != 0:
                    gelu_(h)
                B_(h, 0)
                tanh2(h, 0)
                B_(h, 1)
                modstep(2)
                if nx:
                    conv_a(h + 1)
                    D_(h + 1)
                modstep(2)
                chain(h, 0)
                if nx:
                    conv_b(h + 1)
                tanh2(h, 1)
                chain(h, 1)
                P.op("dve", lambda e: e.tensor_tensor(out=T4[:], in0=T4[:], in1=T5[:], op=ALU.add),
                     reads=[bT[3], bT[4]], writes=[bT[3]])
                P.op("dve", lambda e, h=h: e.tensor_tensor(out=YG[:, h, :], in0=T4[:], in1=YG[:, h, :], op=ALU.mult),
                     reads=[bT[3], bYG[h]], writes=[bYG[h]])
            if modl is not None:
                for _ in mg:
                    pass
                mod_finalize(modl, 7)
            wout = dr["rg_w_out"][0]
            k = 0
            for oc in range(NCH):
                if oc % 2 == 0:
                    so, wo = load_wcols(wout, oc * 128, 256)
                for th in range(2):
                    pb = k % 4
                    k += 1
                    tok = slice(th * 512, (th + 1) * 512)
                    for kc in range(16):
                        P.op("pe", lambda e, kc=kc, wo=wo, oc=oc, pb=pb, tok=tok:
                             e.matmul(bank(pb), lhsT=wo[:, kc, (oc % 2) * 128:(oc % 2 + 1) * 128], rhs=YG[:, kc, tok],
                                      start=(kc == 0), stop=(kc == 15)),
                             reads=[bRING[so], bYG[kc]], writes=[bPS[pb]], signal=(kc == 15))
                    P.op("dve", lambda e, oc=oc, pb=pb, tok=tok:
                         e.scalar_tensor_tensor(out=X[:, oc, tok], in0=bank(pb), scalar=modcol(l, 2, oc, g),
                                                in1=X[:, oc, tok], op0=ALU.mult, op1=ALU.add),
                         reads=[bPS[pb], bMOD, bX[oc]], writes=[bX[oc]])
            P.barrier()
        if g == 0:
            with contextlib.ExitStack() as ph:
                NSo = sb(ph, "NSo", [128, 128], F32)
                bNSo = P.buf("NSo")
                P.op("pe", lambda e: e.transpose(bank(0, 128), NS[:], IDF[:]), reads=[bNS, bIDF], writes=[bPS[0]])
                P.op("dve", lambda e: e.tensor_copy(out=NSo[:], in_=bank(0, 128)), reads=[bPS[0]], writes=[bNSo])
                P.dma("sp", lambda e: e.dma_start(out=nsd.rearrange("r (h p) -> (r h) p", p=128), in_=NSo[:]),
                      reads=[bNSo])
                P.barrier()

    def cm_mixer(g):
        l = 1
        ring_state["n"] = 4
        win = dr["cm_w_in"][0]
        with contextlib.ExitStack() as ph:
            bV = [P.buf("V%d" % c) for c in range(NCH)]
            WST = sb(ph, "WST", [128, 8, 128], BF16)
            BT = sb(ph, "BT", [128, NCH, 128], F32)
            bWST, bWSF, bBT = P.buf("WST"), P.buf("WSF"), P.buf("BT")
            ROW = sb(ph, "ROW", [33, 1024], F32)
            bROW = P.buf("ROW")
            BH = sb(ph, "BH", [33, D], BF16)
            bBH = P.buf("BH")
            ST1 = sb(ph, "ST1", [128, 2, 8, 8], F32)
            bST = P.buf("ST1")
            STT = sb(ph, "STT", [128, 6, 8], F32)
            bSTT = P.buf("STT")
            GS = [sb(ph, "GS%d" % i, [128, 512], F32) for i in range(2)]
            bGS = [P.buf("GS%d" % i) for i in range(2)]
            JK = sb(ph, "JK", [128, 512], BF16)
            bJK = P.buf("JK")
            SCt = [sb(ph, "SCt%d" % i, [128, 512], F32) for i in range(2)]
            UGt = [sb(ph, "UGt%d" % i, [128, 512], F32) for i in range(2)]
            bSCt = [P.buf("SCt%d" % i) for i in range(2)]
            bUGt = [P.buf("UGt%d" % i) for i in range(2)]
            CMS = float(os.environ.get("CM_STOP", "9"))
            if CMS <= 0:
                return
            with contextlib.ExitStack() as ph2:
                WS0 = sb(ph2, "WS0", [128, 8, 128], F32)
                WSF = sb(ph2, "WSF", [128, 8, 128], F32)
                LNB = sb(ph2, "LNB", [128, D], F32)
                bWS0, bLNB = P.buf("WS0"), P.buf("LNB")
                P.dma("sp", lambda e: e.dma_start(out=WS0[:], in_=dr["cm_w_s"][0].rearrange("g p q -> p g q")), writes=[bWS0])
                P.dma("sp", lambda e: e.dma_start(out=LNB[:], in_=dr["cm_ln_b"][0:1, :].partition_broadcast(128)),
                      writes=[bLNB])
                TB = sb(ph2, "TB", [33, D], F32)
                bTB = P.buf("TB")
                P.op("dve", lambda e: e.memset(BH[:], 0.0), writes=[bBH])
                P.dma("sp", lambda e: e.dma_start(out=TB[0:1, :], in_=dr["cm_b_in"][0:1, D:2 * D]), writes=[bTB])
                P.dma("sp", lambda e: e.dma_start(out=TB[32:33, :], in_=dr["cm_b_in"][0:1, D:2 * D]), writes=[bTB])
                P.op("dve", lambda e: e.tensor_copy(out=BH[0:1, :], in_=TB[0:1, :]), reads=[bTB, bBH], writes=[bBH])
                P.op("dve", lambda e: e.tensor_copy(out=BH[32:33, :], in_=TB[32:33, :]), reads=[bTB, bBH], writes=[bBH])
                P.op("dve", lambda e: e.tensor_tensor(out=TB[32:33, :], in0=TB[32:33, :], in1=BH[32:33, :], op=ALU.subtract),
                     reads=[bTB, bBH], writes=[bTB])
                P.op("dve", lambda e: e.tensor_copy(out=BH[32:33, :], in_=TB[32:33, :]), reads=[bTB, bBH], writes=[bBH])
                P.dma("sp", lambda e: e.dma_start(out=ROW[32:33, 0:1024], in_=dr["cm_b_s"][0:1].rearrange("a g p -> a (g p)")),
                      writes=[bROW])
                if CMS <= 0.3:
                    P.barrier()
                    return
                for gi in range(8):
                    pb = gi % 2
                    P.op("pe", lambda e, gi=gi, pb=pb: e.transpose(bank(pb, 128), WS0[:, gi, :], IDF[:]),
                         reads=[bWS0, bIDF], writes=[bPS[pb]])
                    if CMS <= 0.4:
                        continue
                    P.op("dve", lambda e, gi=gi, pb=pb: e.tensor_copy(out=WSF[:, gi, :], in_=bank(pb, 128)),
                         reads=[bPS[pb]], writes=[bWSF])
                    if CMS <= 0.5:
                        continue
                    P.op("dve", lambda e, gi=gi: e.tensor_copy(out=WST[:, gi, :], in_=WSF[:, gi, :]),
                         reads=[bWSF], writes=[bWST])
                if CMS <= 0.6:
                    P.barrier()
                    return
                for c in range(NCH):
                    gi = c // 2
                    pb = 2 + c % 2
                    P.op("pe", lambda e, c=c, gi=gi, pb=pb:
                         e.matmul(bank(pb, 128), lhsT=LNB[:, c * 128:(c + 1) * 128], rhs=WSF[:, gi, :], start=True, stop=False),
                         reads=[bLNB, bWSF], writes=[bPS[pb]], signal=False)
                    P.op("pe", lambda e, c=c, gi=gi, pb=pb:
                         e.matmul(bank(pb, 128), lhsT=ONF[32:33, :], rhs=ROW[32:33, gi * 128:(gi + 1) * 128], start=False, stop=True),
                         reads=[bONF, bROW], writes=[bPS[pb]])
                    P.op("act", lambda e, c=c, pb=pb: e.activation(out=BT[:, c, :], in_=bank(pb, 128), func=AF.Copy),
                         reads=[bPS[pb]], writes=[bBT])
                P.barrier()
            if CMS <= 1:
                return
            V = sb(ph, "V", [128, 8, D], BF16)
            k = 0
            for vp in range(4):
                pair = vp % 2
                wv = load_pair(win, D + vp * 512, pair)
                rbs = [bRING[2 * pair], bRING[2 * pair + 1]]
                for tt in range(8):
                    pb = k % 2
                    k += 1
                    for kc in range(16):
                        P.op("pe", lambda e, kc=kc, tt=tt, wv=wv, pb=pb:
                             e.matmul(bank(pb), lhsT=XN[:, kc, tt * 128:(tt + 1) * 128], rhs=wv[:, kc, :],
                                      start=(kc == 0), stop=False),
                             reads=rbs + [bXN[kc]], writes=[bPS[pb]], signal=False)
                    P.op("pe", lambda e, vp=vp, pb=pb:
                         e.matmul(bank(pb), lhsT=ONB[0:33, :], rhs=BH[0:33, vp * 512:(vp + 1) * 512], start=False, stop=True),
                         reads=[bONB, bBH], writes=[bPS[pb]])
                    P.op("act", lambda e, pb=pb, tt=tt, vp=vp:
                         e.activation(out=GS[pb][:], in_=bank(pb), func=AF.Gelu_apprx_tanh,
                                      accum_out=ST1[:, 0, tt, vp:vp + 1]),
                         reads=[bPS[pb]], writes=[bGS[pb], bST])
                    P.op("act", lambda e, pb=pb, tt=tt, vp=vp:
                         e.activation(out=JK[:], in_=GS[pb][:], func=AF.Square, accum_out=ST1[:, 1, tt, vp:vp + 1]),
                         reads=[bGS[pb]], writes=[bJK, bST])
                    P.op("dve", lambda e, pb=pb, tt=tt, vp=vp:
                         e.tensor_copy(out=V[:, tt, vp * 512:(vp + 1) * 512], in_=GS[pb][:]),
                         reads=[bGS[pb]], writes=[bV[4 * vp + q] for q in range(4)])
            if CMS <= 2:
                P.barrier()
                return
            MU, EX2, VAR, RSTD, NB, TMPs = (STT[:, i, :] for i in range(6))
            P.op("dve", lambda e: e.tensor_reduce(out=MU, in_=ST1[:, 0, :, 0:4], axis=AX.X, op=ALU.add), reads=[bST], writes=[bSTT])
            P.op("dve", lambda e: e.tensor_reduce(out=EX2, in_=ST1[:, 1, :, 0:4], axis=AX.X, op=ALU.add), reads=[bST], writes=[bSTT])
            P.op("dve", lambda e: e.tensor_scalar(out=MU, in0=MU, scalar1=1.0 / D, scalar2=None, op0=ALU.mult),
                 reads=[bSTT], writes=[bSTT])
            P.op("dve", lambda e: e.tensor_tensor(out=TMPs, in0=MU, in1=MU, op=ALU.mult), reads=[bSTT], writes=[bSTT])
            P.op("dve", lambda e: e.scalar_tensor_tensor(out=VAR, in0=EX2, scalar=1.0 / D, in1=TMPs, op0=ALU.mult,
                                                         op1=ALU.subtract), reads=[bSTT], writes=[bSTT])
            P.op("act", lambda e: e.activation(out=RSTD, in_=VAR, func=AF.Sqrt, scale=1.0, bias=LNE_T[:, 0:1]),
                 reads=[bSTT, bEPS], writes=[bSTT])
            P.op("dve", lambda e: e.reciprocal(out=RSTD, in_=RSTD), reads=[bSTT], writes=[bSTT])
            P.op("dve", lambda e: e.scalar_tensor_tensor(out=NB, in0=MU, scalar=-1.0, in1=RSTD, op0=ALU.mult, op1=ALU.mult),
                 reads=[bSTT], writes=[bSTT])
            for tt in range(8):
                P.op("dve", lambda e, tt=tt: e.tensor_scalar(out=V[:, tt, :], in0=V[:, tt, :], scalar1=STT[:, 3, tt:tt + 1],
                                                             scalar2=STT[:, 4, tt:tt + 1], op0=ALU.mult, op1=ALU.add),
                     reads=bV + [bSTT], writes=bV)
            if CMS <= 3:
                P.barrier()
                return
            for c in range(NCH):
                gi = c // 2
                if c % 2 == 0:
                    su, wu = load_wcols(win, c * 128, 256)
                for tt in range(8):
                    pb = 2 + tt // 4
                    P.op("pe", lambda e, c=c, tt=tt, gi=gi, pb=pb:
                         e.matmul(bank(pb)[:, (tt % 4) * 128:(tt % 4 + 1) * 128], lhsT=V[:, tt, c * 128:(c + 1) * 128],
                                  rhs=WST[:, gi, :], start=True, stop=True),
                         reads=[bV[c], bWST], writes=[bPS[pb]], signal=(tt % 4 == 3))
                for th in range(2):
                    for kc in range(16):
                        P.op("pe", lambda e, th=th, kc=kc, wu=wu, c=c:
                             e.matmul(bank(4 + th), lhsT=wu[:, kc, (c % 2) * 128:(c % 2 + 1) * 128],
                                      rhs=XN[:, kc, th * 512:(th + 1) * 512], start=(kc == 0), stop=(kc == 15)),
                             reads=[bRING[su], bXN[kc]], writes=[bPS[4 + th]], signal=(kc == 15))
                for th in range(2):
                    P.op("dve", lambda e, th=th, c=c:
                         e.scalar_tensor_tensor(out=SCt[th][:].rearrange("p (a b) -> p a b", a=4),
                                                in0=bank(2 + th).rearrange("p (a b) -> p a b", a=4),
                                                scalar=pcol("clg", c),
                                                in1=rap(BT[:, c, 0:1], [[0, 4], [1, 128]]),
                                                op0=ALU.mult, op1=ALU.add),
                         reads=[bPS[2 + th], bPT, bBT], writes=[bSCt[th]])
                    P.op("act", lambda e, th=th, c=c:
                         e.activation(out=UGt[th][:], in_=bank(4 + th), func=AF.Gelu_apprx_tanh, bias=pcol("cbi", c), scale=1.0),
                         reads=[bPS[4 + th], bPT], writes=[bUGt[th]])
                    P.op("dve", lambda e, th=th, c=c:
                         e.tensor_tensor(out=V[:, 4 * th:4 * th + 4, c * 128:(c + 1) * 128],
                                         in0=UGt[th][:].rearrange("p (a b) -> p a b", a=4),
                                         in1=SCt[th][:].rearrange("p (a b) -> p a b", a=4), op=ALU.mult),
                         reads=[bUGt[th], bSCt[th]], writes=[bV[c]])
            if CMS <= 4:
                P.barrier()
                return
            wout = dr["cm_w_out"][0]
            k = 0
            for oc in range(NCH):
                if oc % 2 == 0:
                    so, wo = load_wcols(wout, oc * 128, 256)
                for th in range(2):
                    pb = 4 + k % 4
                    k += 1
                    tok = slice(th * 512, (th + 1) * 512)
                    for kc in range(16):
                        P.op("pe", lambda e, kc=kc, wo=wo, oc=oc, pb=pb, th=th:
                             e.matmul(bank(pb), lhsT=wo[:, kc, (oc % 2) * 128:(oc % 2 + 1) * 128],
                                      rhs=V[:, 4 * th:4 * th + 4, kc * 128:(kc + 1) * 128],
                                      start=(kc == 0), stop=(kc == 15)),
                             reads=[bRING[so], bV[kc]], writes=[bPS[pb]], signal=(kc == 15))
                    P.op("dve", lambda e, oc=oc, pb=pb, tok=tok:
                         e.scalar_tensor_tensor(out=X[:, oc, tok], in0=bank(pb), scalar=modcol(l, 2, oc, g),
                                                in1=X[:, oc, tok], op0=ALU.mult, op1=ALU.add),
                         reads=[bPS[pb], bMOD, bX[oc]], writes=[bX[oc]])
            P.barrier()

    def ffn(l, g):
        S = 4 if g == 0 else 1
        L = T // S
        wup = dr["ffn_w_up"][l]
        wdn = dr["ffn_w_down"][l]
        NFB = DFF // 256
        with contextlib.ExitStack() as ph:
            DR = [sb(ph, "DR%d" % i, [128, 4096], BF16) for i in range(4)]
            bDR = [P.buf("DR%d" % i) for i in range(4)]
            HBf = [sb(ph, "HBf%d" % i, [128, 4, T], BF16) for i in range(2)]
            bHBf = [P.buf("HBf%d" % i) for i in range(2)]
            GC = [sb(ph, "GC%d" % i, [128, T], F32) for i in range(2)]
            VC = [sb(ph, "VC%d" % i, [128, T], F32) for i in range(2)]
            bGC = [P.buf("GC%d" % i) for i in range(2)]
            bVC = [P.buf("VC%d" % i) for i in range(2)]
            ring_state["n"] = 4
            ps01 = PS[:, 0:1024]
            ps23 = PS[:, 1024:2048]

            def seg3(ap2d, a, n):
                return rap(ap2d[:, a:a + 1], [[L, S], [1, n]])

            def conv_evac(psv, pbs, dst, bdst, fc):
                P.op("act", lambda e: e.activation(out=dst[:], in_=psv, func=AF.Identity,
                                                   scale=pcol("fcw", (l * 3 + 1) * 88 + fc), bias=pcol("fcb", l * 88 + fc)),
                     reads=pbs + [bPT], writes=[bdst])
                for (kk, dst0, src0) in ((0, 1, 0), (2, 0, 1)):
                    P.op("dve", lambda e, kk=kk, dst0=dst0, src0=src0:
                         e.scalar_tensor_tensor(out=seg3(dst, dst0, L - 1), in0=seg3(psv, src0, L - 1),
                                                scalar=pcol("fcw", (l * 3 + kk) * 88 + fc), in1=seg3(dst, dst0, L - 1),
                                                op0=ALU.mult, op1=ALU.add),
                         reads=pbs + [bPT, bdst], writes=[bdst])

            def up_gen(fb):
                sg_, wg_ = load_wcols(wup, fb * 256, 256)
                sv_, wv_ = load_wcols(wup, DFF + fb * 256, 256)
                src = wdn[fb * 256:(fb + 1) * 256, :].rearrange("(k p) c -> p k c", p=128)
                di = fb % 4
                ddst = DR[di][:, 0:4096].rearrange("p (k c) -> p k c", k=2)
                P.dma("pool", lambda e, ddst=ddst, src=src: e.dma_start(out=ddst, in_=src), writes=[bDR[di]])
                hb = (fb // 2) % 2
                for j in range(2):
                    fc = 2 * fb + j
                    ji = j % 2
                    hk = (fb % 2) * 2 + j
                    for (w_, s_, b0, isg) in ((wg_, sg_, 0, True), (wv_, sv_, 2, False)):
                        for th in range(2):
                            for kc in range(16):
                                P.op("pe", lambda e, th=th, kc=kc, w_=w_, j=j, b0=b0:
                                     e.matmul(bank(b0 + th), lhsT=w_[:, kc, j * 128:(j + 1) * 128],
                                              rhs=XN[:, kc, th * 512:(th + 1) * 512], start=(kc == 0), stop=(kc == 15)),
                                     reads=[bRING[s_], bXN[kc]], writes=[bPS[b0 + th]], signal=(kc == 15))
                            if th == 1:
                                if isg:
                                    conv_evac(ps01, [bPS[0], bPS[1]], GC[ji], bGC[ji], fc)
                                else:
                                    conv_evac(ps23, [bPS[2], bPS[3]], VC[ji], bVC[ji], NFC + fc)
                                    P.op("act", lambda e, ji=ji: e.activation(out=GC[ji][:], in_=GC[ji][:], func=AF.Silu),
                                         reads=[bGC[ji]], writes=[bGC[ji]])
                                    P.op("dve", lambda e, ji=ji, hb=hb, hk=hk:
                                         e.tensor_tensor(out=HBf[hb][:, hk, :], in0=GC[ji][:], in1=VC[ji][:], op=ALU.mult),
                                         reads=[bGC[ji], bVC[ji]], writes=[bHBf[hb]])
                            yield

            kd = [0]

            def down_gen(pi):
                hb = pi % 2
                n = 0
                for oc in range(NCH):
                    for th in range(2):
                        pb = 4 + kd[0] % 4
                        kd[0] += 1
                        tok = slice(th * 512, (th + 1) * 512)
                        for hk in range(4):
                            di = (2 * pi + hk // 2) % 4
                            wd_ = DR[di][:, 0:4096].rearrange("p (k c) -> p k c", k=2)
                            P.op("pe", lambda e, hk=hk, wd_=wd_, oc=oc, pb=pb, tok=tok, hb=hb:
                                 e.matmul(bank(pb), lhsT=wd_[:, hk % 2, oc * 128:(oc + 1) * 128], rhs=HBf[hb][:, hk, tok],
                                          start=(hk == 0), stop=(hk == 3)),
                                 reads=[bDR[di], bHBf[hb]], writes=[bPS[pb]], signal=(hk == 3))
                        P.op("dve", lambda e, oc=oc, pb=pb, tok=tok:
                             e.scalar_tensor_tensor(out=X[:, oc, tok], in0=bank(pb), scalar=modcol(l, 5, oc, g),
                                                    in1=X[:, oc, tok], op0=ALU.mult, op1=ALU.add),
                             reads=[bPS[pb], bMOD, bX[oc]], writes=[bX[oc]])
                        n += 1
                        if n % 2 == 0:
                            yield

            def drain(gen):
                for _ in gen:
                    pass

            def chain2(fb0):
                for fb in (fb0, fb0 + 1):
                    if fb < NFB:
                        for _ in up_gen(fb):
                            yield

            drain(chain2(0))
            for pi in range(NFB // 2):
                ug = chain2(2 * pi + 2)
                dg = down_gen(pi)
                while True:
                    a_ = next(ug, "end")
                    b_ = next(dg, "end")
                    if a_ == "end" and b_ == "end":
                        break
            P.barrier()

    def final(g):
        dst = yp if g == 0 else ys
        with contextlib.ExitStack() as ph:
            FGB = sb(ph, "FGB", [128, D], F32)
            bFGB = P.buf("FGB")
            OST = [sb(ph, "OST%d" % i, [128, D], F32) for i in range(2)]
            bOST = [P.buf("OST%d" % i) for i in range(2)]
            SSQ = sb(ph, "SSQ", [128, 8, 4], F32)
            bSSQ = P.buf("SSQ")
            RSTD = sb(ph, "RSTDF", [128, 8], F32)
            bRSTD = P.buf("RSTDF")
            JF = sb(ph, "JF", [128, 512], BF16)
            bJF = P.buf("JF")
            P.dma("sp", lambda e: e.dma_start(out=FGB[:], in_=dr["final_g"].rearrange("(a n) -> a n", a=1).partition_broadcast(128)),
                  writes=[bFGB])
            for tt in range(8):
                oi = tt % 2
                base = (tt % 2) * 4
                for q in range(4):
                    pb = base + q
                    for j in range(4):
                        c = q * 4 + j
                        P.op("pe", lambda e, c=c, pb=pb, j=j, tt=tt:
                             e.transpose(bank(pb)[:, j * 128:(j + 1) * 128], X[:, c, tt * 128:(tt + 1) * 128], IDF[:]),
                             reads=[bX[c], bIDF], writes=[bPS[pb]], signal=(j == 3))
                    P.op("act", lambda e, pb=pb, tt=tt, q=q:
                         e.activation(out=JF[:], in_=bank(pb), func=AF.Square, accum_out=SSQ[:, tt, q:q + 1]),
                         reads=[bPS[pb]], writes=[bJF, bSSQ])
                P.op("dve", lambda e, tt=tt: e.tensor_reduce(out=RSTD[:, tt:tt + 1], in_=SSQ[:, tt, :], axis=AX.X, op=ALU.add),
                     reads=[bSSQ], writes=[bRSTD])
                P.op("act", lambda e, tt=tt: e.activation(out=RSTD[:, tt:tt + 1], in_=RSTD[:, tt:tt + 1], func=AF.Sqrt,
                                                          scale=1.0 / D, bias=EPS_T[:, 0:1]),
                     reads=[bRSTD, bEPS], writes=[bRSTD])
                P.op("dve", lambda e, tt=tt: e.reciprocal(out=RSTD[:, tt:tt + 1], in_=RSTD[:, tt:tt + 1]),
                     reads=[bRSTD], writes=[bRSTD])
                for q in range(4):
                    pb = base + q
                    P.op("dve", lambda e, oi=oi, q=q, pb=pb, tt=tt:
                         e.scalar_tensor_tensor(out=OST[oi][:, q * 512:(q + 1) * 512], in0=bank(pb),
                                                scalar=RSTD[:, tt:tt + 1], in1=FGB[:, q * 512:(q + 1) * 512],
                                                op0=ALU.mult, op1=ALU.mult),
                         reads=[bPS[pb], bRSTD, bFGB], writes=[bOST[oi]])
                P.dma("sp", lambda e, oi=oi, tt=tt: e.dma_start(out=dst[tt * 128:(tt + 1) * 128, :], in_=OST[oi][:]),
                      reads=[bOST[oi]])
            P.barrier()


    def want(stage):
        return (stage < stop_after) if only is None else (stage in only)

    for g in groups:
        first = (g == groups[0])
        load_group(g, defer=(list(range(NSET1, NSET1 + 5)) if first else None))
        if want(0):
            norm_mod(0, 0, g, nxt="rg", defer=(list(range(NSET1 + 5, NSET1 + 10)) if first else None))
            rg_mixer(g)
        if want(1):
            norm_mod(0, 1, g, nxt="ffn", defer=(list(range(NSET1 + 10, 48)) if first else None), dlast=first)
            ffn(0, g)
        if want(2):
            norm_mod(1, 0, g, nxt="cm")
            cm_mixer(g)
        if want(3):
            norm_mod(1, 1, g, nxt="ffn")
            ffn(1, g)
        final(g)

    P.emit()
    P.close()
    es.close()
    return nc


_NC_CACHE = {}


def _get_nc(stop_after=99):
    if stop_after not in _NC_CACHE:
        _NC_CACHE[stop_after] = build(stop_after)
    return _NC_CACHE[stop_after]


def make_in_maps(inputs):
    f = lambda a: np.ascontiguousarray(np.asarray(a, dtype=np.float32))
    shared = {k: f(inputs[k]) for k in IN_SHAPES if k not in ("xp", "xs", "st", "cond")}
    xp = f(inputs["x_prompt"])
    xs = f(inputs["x_sample"])
    st = f(inputs["state_rglru"])
    c = f(inputs["c"])
    cctx = f(inputs["c_ctx"])
    maps = []
    for i in range(NCORES):
        m = dict(shared)
        m["xp"] = np.ascontiguousarray(xp[4 * i:4 * i + 4].reshape(T, D))
        m["xs"] = np.ascontiguousarray(xs[i])
        m["st"] = np.ascontiguousarray(st[i, 0])
        m["cond"] = np.ascontiguousarray(np.stack([cctx, c[i]], axis=0))
        maps.append(m)
    return maps


def kernel(**inputs):
    nc = _get_nc()
    maps = make_in_maps(inputs)
    res = run_bass_kernel_spmd(nc, maps, core_ids=list(range(NCORES)))
    outs = res.results
    y_prompt = np.concatenate([np.asarray(r["yp"], dtype=np.float32).reshape(4, 256, D) for r in outs], axis=0)
    y_sample = np.stack([np.asarray(r["ys"], dtype=np.float32) for r in outs], axis=0)
    new_state = np.concatenate([np.asarray(r["ns"], dtype=np.float32).reshape(4, 1, 2, D) for r in outs], axis=0)
    return (y_prompt, y_sample, new_state)
```

```python
import contextlib
import math
import os
import numpy as np
import concourse.bass as bass
import concourse.mybir as mybir
from concourse.bass_utils import run_bass_kernel_spmd

F32 = mybir.dt.float32
BF16 = mybir.dt.bfloat16
I32 = mybir.dt.int32
AF = mybir.ActivationFunctionType
ALU = mybir.AluOpType
AX = mybir.AxisListType

ENGS = ("sp", "act", "pe", "dve", "pool")
NCORES = 8
D = 2048
NCH = 16
T = 1024
DFF = 5632
NFC = DFF // 128
EPS = 1e-6
LN_EPS = 1e-5


class Buf:
    __slots__ = ("name", "w", "r", "sem", "semval")

    def __init__(self, name):
        self.name = name
        self.w = None
        self.r = []
        self.sem = None
        self.semval = 0


class Prog:
    def __init__(self, nc):
        self.nc = nc
        self.ops = {e: [] for e in ENGS}
        self.cnt = {e: 0 for e in ENGS}
        self.seen = {e: {} for e in ENGS}
        self.esem = {}
        self._ctx = []
        for e in ("act", "pe", "dve", "pool"):
            cm = nc.semaphore("s_" + e)
            self.esem[e] = cm.__enter__()
            self._ctx.append(cm)
        self.bufs = {}
        self.dma_pending = {}

    def buf(self, name):
        b = self.bufs.get(name)
        if b is None:
            b = Buf(name)
            self.bufs[name] = b
        return b

    def _dma_sem(self, b):
        if b.sem is None:
            cm = self.nc.semaphore("d_" + b.name)
            b.sem = cm.__enter__()
            self._ctx.append(cm)
        return b.sem

    def _collect(self, eng, reads, writes):
        best = {}
        def add(t):
            kind, key, val = t
            if kind == "e" and key == "pe" and eng == "pe":
                return
            k = (kind, key)
            if best.get(k, 0) < val:
                best[k] = val
        for b in reads:
            if b.w is not None:
                add(b.w)
        for b in writes:
            if b.w is not None:
                add(b.w)
            for t in b.r:
                add(t)
        waits = []
        seen = self.seen[eng]
        for k, val in best.items():
            if seen.get(k, 0) >= val:
                continue
            seen[k] = val
            kind, key = k
            sem = self.esem[key] if kind == "e" else key.sem
            waits.append((sem, val))
        return waits

    def op(self, eng, fn, reads=(), writes=(), signal=True):
        waits = self._collect(eng, reads, writes)
        if signal:
            self.cnt[eng] += 1
            t = ("e", eng, self.cnt[eng])
        else:
            t = ("e", eng, self.cnt[eng] + 1)
        self.ops[eng].append((fn, waits, signal, None))
        for b in writes:
            b.w = t
            b.r = []
        for b in reads:
            b.r.append(t)
        return t

    def dma(self, eng, fn, reads=(), writes=(), semb=None):
        if semb is None:
            semb = writes[0] if writes else reads[0]
        sem = self._dma_sem(semb)
        waits = self._collect(eng, reads, writes)
        semb.semval += 16
        t = ("d", semb, semb.semval)
        self.dma_pending[semb] = semb.semval
        self.ops[eng].append((fn, waits, False, (sem, 16)))
        for b in writes:
            b.w = t
            b.r = []
        for b in reads:
            b.r.append(t)
        return t

    def barrier(self, engs=ENGS):
        cur = {f: self.cnt[f] for f in ("act", "pe", "dve", "pool")}
        for e in engs:
            waits = []
            seen = self.seen[e]
            for f, v in cur.items():
                if f == e or v == 0:
                    continue
                k = ("e", f)
                if seen.get(k, 0) >= v:
                    continue
                seen[k] = v
                waits.append((self.esem[f], v))
            for b, val in self.dma_pending.items():
                k = ("d", b)
                if seen.get(k, 0) >= val:
                    continue
                seen[k] = val
                waits.append((b.sem, val))
            self.ops[e].append((None, waits, False, None))
        self.dma_pending = {}

    def emit(self):
        nc = self.nc
        ops = self.ops
        esem = self.esem

        def run(engname, e):
            for fn, waits, signal, dinc in ops[engname]:
                for sem, val in waits:
                    e.wait_ge(sem, val)
                if fn is None:
                    continue
                ins = fn(e)
                if dinc is not None:
                    ins.then_inc(dinc[0], dinc[1])
                elif signal:
                    ins.then_inc(esem[engname], 1)

        with nc.Block() as block:
            @block.sync
            def _(e):
                run("sp", e)

            @block.scalar
            def _(e):
                run("act", e)

            @block.tensor
            def _(e):
                run("pe", e)

            @block.vector
            def _(e):
                run("dve", e)

            @block.gpsimd
            def _(e):
                run("pool", e)

    def close(self):
        for cm in reversed(self._ctx):
            cm.__exit__(None, None, None)


def rap(base, dims):
    return bass.AP(base.tensor, base.offset, [list(base.ap[0])] + [list(d) for d in dims])


PARAM_SEGS = [
    ("cond", "cond", "g (n p) -> (g n) p", 32),
    ("st", "st", "d (n p) -> (d n) p", 32),
    ("n1g", "norm1_g", "l (n p) -> (l n) p", 32),
    ("n2g", "norm2_g", "l (n p) -> (l n) p", 32),
    ("fg", "final_g", "(n p) -> n p", 16),
    ("bada", "b_ada", "l (n p) -> (l n) p", 192),
    ("rcw", "rg_conv_w", "a k (n p) -> (a k n) p", 64),
    ("rcb", "rg_conv_b", "a (n p) -> (a n) p", 16),
    ("rba", "rg_b_a", "a d (n p) -> (a d n) p", 32),
    ("rbx", "rg_b_x", "a d (n p) -> (a d n) p", 32),
    ("lam", "rg_lam", "a d (n p) -> (a d n) p", 32),
    ("cbi", "cm_b_in", "a (n p) -> (a n) p", 32),
    ("clg", "cm_ln_g", "a (n p) -> (a n) p", 16),
    ("fcw", "ffn_conv_w", "l k (n p) -> (l k n) p", 528),
    ("fcb", "ffn_conv_b", "l (n p) -> (l n) p", 176),
]
NPROWS = sum(s[3] for s in PARAM_SEGS)
NPBLK = (NPROWS + 127) // 128

IN_SHAPES = {
    "xp": [T, D], "xs": [T, D], "st": [2, D], "cond": [2, D],
    "norm1_g": [2, D], "norm2_g": [2, D], "w_ada": [2, D, 6 * D], "b_ada": [2, 6 * D],
    "rg_w_in": [1, D, 2 * D], "rg_conv_w": [1, 4, D], "rg_conv_b": [1, D],
    "rg_w_a": [1, 2, 16, 128, 128], "rg_b_a": [1, 2, D], "rg_w_x": [1, 2, 16, 128, 128],
    "rg_b_x": [1, 2, D], "rg_lam": [1, 2, D], "rg_w_out": [1, D, D],
    "cm_w_in": [1, D, 2 * D], "cm_b_in": [1, 2 * D], "cm_ln_g": [1, D], "cm_ln_b": [1, D],
    "cm_w_s": [1, 8, 128, 128], "cm_b_s": [1, 8, 128], "cm_w_out": [1, D, D],
    "ffn_w_up": [2, D, 2 * DFF], "ffn_conv_w": [2, 3, 2 * DFF], "ffn_conv_b": [2, 2 * DFF],
    "ffn_w_down": [2, DFF, D], "final_g": [D],
}


def build(stop_after=99, groups=(0, 1), only=None):
    nc = bass.Bass("TRN2", target_bir_lowering=False)
    dr = {}
    for name, shp in IN_SHAPES.items():
        dr[name] = nc.dram_tensor(name, list(shp), F32, kind="ExternalInput").ap()
    yp = nc.dram_tensor("yp", [T, D], F32, kind="ExternalOutput").ap()
    ys = nc.dram_tensor("ys", [T, D], F32, kind="ExternalOutput").ap()
    nsd = nc.dram_tensor("ns", [8, D], F32, kind="ExternalOutput").ap()

    P = Prog(nc)
    es = contextlib.ExitStack()
    uid = [0]

    def sb(stack, name, shape, dt):
        uid[0] += 1
        return stack.enter_context(nc.sbuf_tensor("%s_%d" % (name, uid[0]), list(shape), dt))

    X = sb(es, "X", [128, NCH, T], F32)
    XN = sb(es, "XN", [128, NCH, T], BF16)
    PT = sb(es, "PT", [128, NPBLK * 128], F32)
    MOD = sb(es, "MOD", [128, 2, 96, 2], F32)
    GM = sb(es, "GM", [128, 2, 2, NCH, 2], F32)
    CL = sb(es, "CL", [128, 2, 32], F32)
    HB_ = sb(es, "HBt", [128, 2, 32], F32)
    TR = sb(es, "TR", [128, 8, 16], F32)
    TC = sb(es, "TC", [128, 8, 64], F32)
    NS = sb(es, "NS", [128, 128], F32)
    IDF = sb(es, "IDF", [128, 128], F32)
    ONB = sb(es, "ONB", [128, 128], BF16)
    ONF = sb(es, "ONF", [128, 128], F32)
    SC = sb(es, "SC", [128, NCH, 2], BF16)
    NRING = 4
    RINGT = sb(es, "RINGT", [128, 4 * 4096], BF16)
    RING = [RINGT[:, i * 4096:(i + 1) * 4096] for i in range(4)]
    PS = es.enter_context(nc.psum_tensor("PS", [128, 8 * 512], F32))

    bX = [P.buf("X%d" % c) for c in range(NCH)]
    bXN = [P.buf("XN%d" % c) for c in range(NCH)]
    bPT, bMOD, bGM, bCL, bHB, bTR, bTC, bNS = (P.buf(n) for n in ("PT", "MOD", "GM", "CL", "HB", "TR", "TC", "NS"))
    bIDF, bONB, bONF, bSC = (P.buf(n) for n in ("IDF", "ONB", "ONF", "SC"))
    bWG = [P.buf("WG%d" % i) for i in range(2)]
    bRING = [P.buf("RING%d" % i) for i in range(NRING)]
    bPS = [P.buf("PS%d" % i) for i in range(8)]

    def bank(b, n=512):
        return PS[:, b * 512:b * 512 + n]

    poff = {}
    o = 0
    for name, _, _, rows in PARAM_SEGS:
        poff[name] = o
        o += rows

    def pcol(name, idx):
        return PT[:, poff[name] + idx:poff[name] + idx + 1]

    ring_state = {"i": 0, "n": 4}

    def ring_load(src_ap, view):
        s = ring_state["i"] % ring_state["n"]
        ring_state["i"] += 1
        dst = view(RING[s])
        P.dma("pool", lambda e, dst=dst, src=src_ap: e.dma_start(out=dst, in_=src), writes=[bRING[s]])
        return s

    def v_k16(tile, ncols):
        return tile[:, 0:16 * ncols].rearrange("p (k c) -> p k c", k=16)

    prefetched = {}

    def load_wcols(w2d, c0, ncols=256):
        key = (w2d.tensor.name, w2d.offset, c0, ncols)
        if key in prefetched:
            return prefetched.pop(key)
        src = w2d[:, c0:c0 + ncols].rearrange("(k p) c -> p k c", p=128)
        s = ring_load(src, lambda t: v_k16(t, ncols))
        return s, v_k16(RING[s], ncols)

    pair_pre = {}

    def load_pair(w2d, c0, pair):
        key = (w2d.tensor.name, w2d.offset, c0, pair)
        dst = RINGT[:, pair * 8192:(pair + 1) * 8192].rearrange("p (k c) -> p k c", k=16)
        if key in pair_pre:
            pair_pre.pop(key)
            return dst
        src = w2d[:, c0:c0 + 512].rearrange("(k p) c -> p k c", p=128)
        P.dma("pool", lambda e, dst=dst, src=src: e.dma_start(out=dst, in_=src),
              writes=[bRING[2 * pair], bRING[2 * pair + 1]])
        return dst

    def prefetch_pair(w2d, c0, pair):
        load_pair(w2d, c0, pair)
        pair_pre[(w2d.tensor.name, w2d.offset, c0, pair)] = True

    def prefetch_wcols(w2d, c0, ncols=256):
        key = (w2d.tensor.name, w2d.offset, c0, ncols)
        assert key not in prefetched
        r_ = load_wcols(w2d, c0, ncols)
        prefetched[key] = r_

    NDEF = 15
    NSET1 = 48 - NDEF

    def mod_window_issue(stack, blks):
        items = []
        for i, blk in enumerate(blks):
            t = sb(stack, "DW%d" % i, [128, 4096], BF16)
            b = P.buf("DW%d" % i)
            src = dr["w_ada"][1][:, blk * 256:(blk + 1) * 256].rearrange("(k p) c -> p k c", p=128)
            dst = v_k16(t, 256)
            P.dma("pool", lambda e, dst=dst, src=src: e.dma_start(out=dst, in_=src), writes=[b])
            items.append((b, dst, blk))
        return items

    def mod_window_consume(items, last=False):
        for (b, wv, blk) in items:
            for n2 in range(2):
                n = blk * 2 + n2
                for kc in range(16):
                    P.op("pe", lambda e, wv=wv, n2=n2, kc=kc, n=n:
                         e.matmul(bank(6)[:, 2 * n:2 * n + 2], lhsT=wv[:, kc, n2 * 128:(n2 + 1) * 128],
                                  rhs=SC[:, kc, :], start=(kc == 0), stop=(kc == 15)),
                         reads=[b, bSC], writes=[bPS[6]], signal=(kc == 15))
        n0 = items[0][2] * 2
        cnt = len(items) * 2
        for g in range(2):
            P.op("dve", lambda e, g=g, n0=n0, cnt=cnt:
                 e.tensor_copy(out=MOD[:, 1, n0:n0 + cnt, g], in_=rap(bank(6)[:, 2 * n0 + g:2 * n0 + g + 1], [[2, cnt]])),
                 reads=[bPS[6]], writes=[bMOD])
        if last:
            for g in range(2):
                P.op("dve", lambda e, g=g:
                     e.tensor_tensor(out=MOD[:, 1, :, g], in0=MOD[:, 1, :, g],
                                     in1=PT[:, poff["bada"] + 96:poff["bada"] + 192], op=ALU.add),
                     reads=[bMOD, bPT], writes=[bMOD])
            mod_gm(1)

    def mod_gen(l, pb, loader=None, ncols=256, nblk=None):
        for blk in range(6 * D // ncols if nblk is None else nblk):
            if loader is None:
                s, wv = load_wcols(dr["w_ada"][l], blk * ncols, ncols)
                rb = bRING[s]
            else:
                rb, wv = loader(dr["w_ada"][l], blk * ncols)
            for n2 in range(ncols // 128):
                n = blk * (ncols // 128) + n2
                for kc in range(16):
                    P.op("pe", lambda e, wv=wv, n2=n2, kc=kc, n=n, pb=pb:
                         e.matmul(bank(pb)[:, 2 * n:2 * n + 2], lhsT=wv[:, kc, n2 * 128:(n2 + 1) * 128],
                                  rhs=SC[:, kc, :], start=(kc == 0), stop=(kc == 15)),
                         reads=[rb, bSC], writes=[bPS[pb]], signal=(kc == 15))
            yield

    def mod_finalize(l, pb):
        for g in range(2):
            P.op("dve", lambda e, l=l, g=g, pb=pb:
                 e.tensor_tensor(out=MOD[:, l, :, g], in0=rap(bank(pb)[:, g:g + 1], [[2, 96]]),
                                 in1=PT[:, poff["bada"] + l * 96:poff["bada"] + (l + 1) * 96], op=ALU.add),
                 reads=[bPS[pb], bPT], writes=[bMOD])
        mod_gm(l)

    def mod_gm(l):
        for which in range(2):
            nm = "n1g" if which == 0 else "n2g"
            for g in range(2):
                P.op("dve", lambda e, l=l, which=which, g=g, nm=nm:
                     e.scalar_tensor_tensor(out=GM[:, l, which, :, g],
                                            in0=MOD[:, l, (1 + 3 * which) * 16:(2 + 3 * which) * 16, g],
                                            scalar=1.0,
                                            in1=PT[:, poff[nm] + l * 16:poff[nm] + (l + 1) * 16],
                                            op0=ALU.add, op1=ALU.mult),
                     reads=[bMOD, bPT], writes=[bGM])

    EPS_T = sb(es, "EPS_T", [128, 1], F32)
    Q25_T = sb(es, "Q25_T", [128, 1], F32)
    LNE_T = sb(es, "LNE_T", [128, 1], F32)
    bEPS = P.buf("EPSC")
    P.op("pool", lambda e: e.memset(EPS_T[:], EPS), writes=[bEPS])
    P.op("pool", lambda e: e.memset(Q25_T[:], 0.25), writes=[bEPS])
    P.op("pool", lambda e: e.memset(LNE_T[:], LN_EPS), writes=[bEPS])

    with contextlib.ExitStack() as ph:
        STG = sb(ph, "STG", [128, NPBLK, 128], F32)
        bSTG = P.buf("STG")
        TMPA = sb(ph, "TMPA", [128, 512], F32)
        bTMPA = P.buf("TMPA")

        P.op("pool", lambda e: e.memset(IDF[:], 0.0), writes=[bIDF])
        P.op("pool", lambda e: e.affine_select(out=IDF[:], in_=IDF[:], compare_op=ALU.not_equal, fill=1.0,
                                               base=0, pattern=[[-1, 128]], channel_multiplier=1),
             reads=[bIDF], writes=[bIDF])
        P.op("pool", lambda e: e.memset(ONB[:], 1.0), writes=[bONB])
        P.op("pool", lambda e: e.memset(ONF[:], 1.0), writes=[bONF])
        bSTG0 = P.buf("STG0")
        P.op("dve", lambda e: e.memset(STG[:], 0.0), writes=[bSTG, bSTG0])
        P.op("dve", lambda e: e.memset(NS[:], 0.0), writes=[bNS])

        r = 0
        for name, dname, pat, rows in PARAM_SEGS:
            seg = dr[dname].rearrange(pat, p=128)
            a = 0
            while a < rows:
                blk, r0 = divmod(r + a, 128)
                n = min(rows - a, 128 - r0)
                P.dma("sp", lambda e, blk=blk, r0=r0, n=n, seg=seg, a=a:
                      e.dma_start(out=STG[r0:r0 + n, blk, :], in_=seg[a:a + n, :]), writes=[bSTG0 if blk == 0 else bSTG])
                a += n
            r += rows
        def pt_block(blk):
            b = blk % 2
            P.op("pe", lambda e, blk=blk, b=b: e.transpose(bank(b, 128), STG[:, blk, :], IDF[:]),
                 reads=[bSTG0 if blk == 0 else bSTG, bIDF], writes=[bPS[b]])
            P.op("dve", lambda e, blk=blk, b=b: e.tensor_copy(out=PT[:, blk * 128:(blk + 1) * 128], in_=bank(b, 128)),
                 reads=[bPS[b]], writes=[bPT])
        pt_block(0)

        for g in range(2):
            P.op("act", lambda e, g=g: e.activation(out=SC[:, :, g], in_=PT[:, poff["cond"] + g * 16:poff["cond"] + g * 16 + 16],
                                                    func=AF.Silu), reads=[bPT], writes=[bSC])

        ARG2 = sb(ph, "ARG2", [128, 640], F32)
        KI = sb(ph, "KI", [128, 640], I32)
        KF = sb(ph, "KF", [128, 640], F32)
        bARG2, bKI, bKF = P.buf("ARG2"), P.buf("KI"), P.buf("KF")
        OM2 = sb(ph, "OM2", [128, 4], F32)
        POS2 = sb(ph, "POS2", [128, 80], F32)
        TI2 = sb(ph, "TI2", [128, 96], I32)
        bOM2, bPOS2, bTI2 = P.buf("OM2"), P.buf("POS2"), P.buf("TI2")
        P.op("pool", lambda e: e.iota(out=TI2[:, 0:4], pattern=[[128, 4]], base=0, channel_multiplier=1), writes=[bTI2])
        P.op("pool", lambda e: e.iota(out=TI2[:, 16:32], pattern=[[1, 16]], base=0, channel_multiplier=0),
             reads=[bTI2], writes=[bTI2])
        P.op("pool", lambda e: e.iota(out=TI2[:, 32:96], pattern=[[1, 64]], base=0, channel_multiplier=0),
             reads=[bTI2], writes=[bTI2])
        P.op("dve", lambda e: e.tensor_copy(out=OM2[:], in_=TI2[:, 0:4]), reads=[bTI2], writes=[bOM2])
        P.op("dve", lambda e: e.tensor_copy(out=POS2[:], in_=TI2[:, 16:96]), reads=[bTI2], writes=[bPOS2])
        P.op("act", lambda e: e.activation(out=OM2[:], in_=OM2[:], func=AF.Exp, scale=-math.log(10000.0) / 512.0),
             reads=[bOM2], writes=[bOM2])
        A4 = ARG2[:].rearrange("p (a c n) -> p a c n", a=2, c=4)
        for cc in range(4):
            P.op("dve", lambda e, cc=cc: e.tensor_scalar(out=A4[:, 0, cc, :], in0=POS2[:], scalar1=OM2[:, cc:cc + 1],
                                                         scalar2=None, op0=ALU.mult),
                 reads=[bPOS2, bOM2], writes=[bARG2])
        P.op("dve", lambda e: e.tensor_scalar(out=A4[:, 1, :, :], in0=A4[:, 0, :, :], scalar1=math.pi / 2, scalar2=None,
                                              op0=ALU.add), reads=[bARG2], writes=[bARG2])
        P.op("dve", lambda e: e.tensor_scalar(out=KI[:], in0=ARG2[:], scalar1=1.0 / (2 * math.pi), scalar2=None,
                                              op0=ALU.mult), reads=[bARG2], writes=[bKI])
        P.op("dve", lambda e: e.tensor_copy(out=KF[:], in_=KI[:]), reads=[bKI], writes=[bKF])
        C1 = 6.28125
        C2 = 2 * math.pi - 6.28125
        P.op("dve", lambda e: e.scalar_tensor_tensor(out=ARG2[:], in0=KF[:], scalar=-C1, in1=ARG2[:], op0=ALU.mult,
                                                     op1=ALU.add), reads=[bKF, bARG2], writes=[bARG2])
        P.op("dve", lambda e: e.scalar_tensor_tensor(out=ARG2[:], in0=KF[:], scalar=-C2, in1=ARG2[:], op0=ALU.mult,
                                                     op1=ALU.add), reads=[bKF, bARG2], writes=[bARG2])
        P.op("dve", lambda e: e.tensor_scalar(out=ARG2[:], in0=ARG2[:], scalar1=-math.pi, scalar2=math.pi, op0=ALU.max,
                                              op1=ALU.min), reads=[bARG2], writes=[bARG2])
        P.op("act", lambda e: e.activation(out=ARG2[:], in_=ARG2[:], func=AF.Sin), reads=[bARG2], writes=[bARG2])
        for a in range(2):
            P.op("dve", lambda e, a=a: e.tensor_copy(out=TR[:, a * 4:(a + 1) * 4, :], in_=A4[:, a, :, 0:16]),
                 reads=[bARG2], writes=[bTR])
            P.op("dve", lambda e, a=a: e.tensor_copy(out=TC[:, a * 4:(a + 1) * 4, :], in_=A4[:, a, :, 16:80]),
                 reads=[bARG2], writes=[bTC])
        ring_state["n"] = 4
        for _ in mod_gen(0, 2):
            pass
        for blk in range(1, NPBLK):
            pt_block(blk)
        mod_finalize(0, 2)
        for _ in mod_gen(1, 3, nblk=NSET1):
            pass
        for g in range(2):
            P.op("dve", lambda e, g=g:
                 e.tensor_copy(out=MOD[:, 1, 0:2 * NSET1, g], in_=rap(bank(3)[:, g:g + 1], [[2, 2 * NSET1]])),
                 reads=[bPS[3]], writes=[bMOD])
        lam = PT[:, poff["lam"]:poff["lam"] + 32]
        P.op("act", lambda e: e.activation(out=TMPA[:, 0:32], in_=lam, func=AF.Exp, scale=-1.0), reads=[bPT], writes=[bTMPA])
        P.op("act", lambda e: e.activation(out=TMPA[:, 32:64], in_=TMPA[:, 0:32], func=AF.Ln, bias=1.0, scale=1.0),
             reads=[bTMPA], writes=[bTMPA])
        P.op("dve", lambda e: e.tensor_scalar(out=CL[:, 0, :], in0=TMPA[:, 32:64], scalar1=-4.0, scalar2=None, op0=ALU.mult),
             reads=[bTMPA], writes=[bCL])
        P.op("dve", lambda e: e.tensor_scalar(out=CL[:, 1, :], in0=TMPA[:, 32:64], scalar1=-8.0, scalar2=None, op0=ALU.mult),
             reads=[bTMPA], writes=[bCL])
        P.op("dve", lambda e: e.tensor_scalar(out=HB_[:, 0, :], in0=PT[:, poff["rba"]:poff["rba"] + 32], scalar1=0.5,
                                              scalar2=None, op0=ALU.mult), reads=[bPT], writes=[bHB])
        P.op("dve", lambda e: e.tensor_scalar(out=HB_[:, 1, :], in0=PT[:, poff["rbx"]:poff["rbx"] + 32], scalar1=0.5,
                                              scalar2=None, op0=ALU.mult), reads=[bPT], writes=[bHB])

        P.barrier()

    def modcol(l, k6, c, g):
        return MOD[:, l, k6 * 16 + c, g:g + 1]

    def load_group(g, defer=None):
        src = dr["xp"] if g == 0 else dr["xs"]
        with contextlib.ExitStack() as ph:
            STI = [sb(ph, "STI%d" % i, [128, D], F32) for i in range(2)]
            bSTI = [P.buf("STI%d" % i) for i in range(2)]
            ditems = mod_window_issue(ph, defer) if defer else None
            for tt in range(8):
                si = tt % 2
                P.dma("sp", lambda e, si=si, tt=tt: e.dma_start(out=STI[si][:], in_=src[tt * 128:(tt + 1) * 128, :]),
                      writes=[bSTI[si]])
                for q in range(4):
                    pb = (tt * 4 + q) % 4
                    for j in range(4):
                        c = q * 4 + j
                        P.op("pe", lambda e, si=si, c=c, pb=pb, j=j:
                             e.transpose(bank(pb)[:, j * 128:(j + 1) * 128], STI[si][:, c * 128:(c + 1) * 128], IDF[:]),
                             reads=[bSTI[si], bIDF], writes=[bPS[pb]], signal=(j == 3))
                    outap = X[:, q * 4:(q + 1) * 4, tt * 128:(tt + 1) * 128]
                    inap = bank(pb).rearrange("p (j t) -> p j t", j=4)
                    xb_ = [bX[q * 4 + j] for j in range(4)]
                    if g == 0:
                        P.op("act", lambda e, outap=outap, inap=inap: e.activation(out=outap, in_=inap, func=AF.Copy),
                             reads=[bPS[pb]], writes=xb_)
                    else:
                        o4 = rap(X[:, q * 4, tt * 128:tt * 128 + 1], [[T, 4], [64, 2], [1, 64]])
                        i4 = rap(bank(pb)[:, 0:1], [[128, 4], [64, 2], [1, 64]])
                        if q < 2:
                            t4 = rap(TR[:, q * 4, 2 * tt:2 * tt + 1], [[16, 4], [1, 2], [0, 64]])
                            rb = bTR
                        else:
                            t4 = rap(TC[:, (q - 2) * 4, 0:1], [[64, 4], [0, 2], [1, 64]])
                            rb = bTC
                        P.op("dve", lambda e, o4=o4, i4=i4, t4=t4: e.tensor_tensor(out=o4, in0=i4, in1=t4, op=ALU.add),
                             reads=[bPS[pb], rb], writes=xb_)
            if ditems:
                mod_window_consume(ditems)
            P.barrier()

    def rstd_half(ph_bufs, th):
        SQ, bSQ, RS, bRS = ph_bufs
        tok = slice(th * 512, (th + 1) * 512)
        for q in range(4):
            rd = [bX[q * 4 + j] for j in range(4)]
            if q % 2 == 0:
                P.op("act", lambda e, q=q, tok=tok: e.activation(out=SQ[q][:], in_=X[:, q * 4:(q + 1) * 4, tok], func=AF.Square),
                     reads=rd, writes=[bSQ[q]])
            else:
                P.op("dve", lambda e, q=q, tok=tok: e.tensor_tensor(out=SQ[q][:], in0=X[:, q * 4:(q + 1) * 4, tok],
                                                                    in1=X[:, q * 4:(q + 1) * 4, tok], op=ALU.mult),
                     reads=rd, writes=[bSQ[q]])
        for q in (0, 1, 2, 3):
            for j in range(4):
                P.op("pe", lambda e, q=q, j=j:
                     e.matmul(bank(7), lhsT=ONB[:], rhs=SQ[q][:, j, :],
                              start=(q == 0 and j == 0), stop=(q == 3 and j == 3)),
                     reads=[bSQ[q], bONB], writes=[bPS[7]], signal=(j == 3))
        P.op("act", lambda e, th=th: e.activation(out=RS[th][:], in_=bank(7), func=AF.Ln, scale=1.0 / D, bias=EPS_T[:, 0:1]),
             reads=[bPS[7], bEPS], writes=[bRS[th]])
        P.op("act", lambda e, th=th: e.activation(out=RS[th][:], in_=RS[th][:], func=AF.Exp, scale=-0.5),
             reads=[bRS[th]], writes=[bRS[th]])

    def norm_mod(l, which, g, nxt=None, defer=None, dlast=False):
        ring_state["n"] = 4
        if nxt == "rg":
            prefetch_wcols(dr["rg_w_in"][0], D)
            prefetch_wcols(dr["rg_w_in"][0], 0)
        elif nxt == "ffn":
            prefetch_wcols(dr["ffn_w_up"][l], 0)
            prefetch_wcols(dr["ffn_w_up"][l], DFF)
        elif nxt == "cm":
            prefetch_pair(dr["cm_w_in"][0], D, 0)
        with contextlib.ExitStack() as ph:
            SQ = [sb(ph, "SQ%d" % i, [128, 4, 512], BF16) for i in range(4)]
            bSQ = [P.buf("SQ%d" % i) for i in range(4)]
            RS = [sb(ph, "RS%d" % i, [128, 512], F32) for i in range(2)]
            bRS = [P.buf("RS%d" % i) for i in range(2)]
            TN = [sb(ph, "TN%d" % i, [128, 512], F32) for i in range(2)]
            bTN = [P.buf("TN%d" % i) for i in range(2)]
            ditems = mod_window_issue(ph, defer) if defer else None
            for th in range(2):
                tok = slice(th * 512, (th + 1) * 512)
                rstd_half((SQ, bSQ, RS, bRS), th)
                for c in range(NCH):
                    ti = c % 2
                    P.op("dve", lambda e, c=c, ti=ti, tok=tok, th=th:
                         e.scalar_tensor_tensor(out=TN[ti][:], in0=X[:, c, tok], scalar=GM[:, l, which, c, g:g + 1],
                                                in1=RS[th][:], op0=ALU.mult, op1=ALU.mult),
                         reads=[bX[c], bGM, bRS[th]], writes=[bTN[ti]])
                    P.op("act", lambda e, c=c, ti=ti, tok=tok:
                         e.activation(out=XN[:, c, tok], in_=TN[ti][:], func=AF.Identity,
                                      bias=modcol(l, 3 * which, c, g), scale=1.0),
                         reads=[bTN[ti], bMOD], writes=[bXN[c]])
            if ditems:
                mod_window_consume(ditems, last=dlast)
            P.barrier()

    def rg_mixer(g, modl=None):
        S = 4 if g == 0 else 1
        L = T // S
        l = 0
        ring_state["n"] = 4 if modl is None else 3
        dbk = 6 if modl is None else 0
        if modl is not None:
            bMRh = [P.buf("MRh%d" % i) for i in range(2)]
            mi = [0]

            def mloader(w2d, c0):
                i = mi[0] % 2
                mi[0] += 1
                src = w2d[:, c0:c0 + 128].rearrange("(k p) c -> p k c", p=128)
                dst = RING[3][:, i * 2048:(i + 1) * 2048].rearrange("p (k c) -> p k c", k=16)
                P.dma("pool", lambda e, dst=dst, src=src: e.dma_start(out=dst, in_=src), writes=[bMRh[i]])
                return bMRh[i], dst
            mg = mod_gen(modl, 7, mloader, 128)
        else:
            mg = iter(())

        def modstep(k):
            for _ in range(k):
                next(mg, None)
        with contextlib.ExitStack() as ph:
            YG = sb(ph, "YG", [128, NCH, T], BF16)
            bYG = [P.buf("YG%d" % c) for c in range(NCH)]
            WG = [sb(ph, "WG%d" % i, [128, 2, 2, 128], BF16) for i in range(2)]
            XB = sb(ph, "XB", [128, T], F32)
            XBb = sb(ph, "XBb", [128, T], BF16)
            bXB, bXBb = P.buf("XB"), P.buf("XBb")
            TT_ = [sb(ph, "T%d" % i, [128, T], F32) for i in range(5)]
            bT = [P.buf("T%d" % i) for i in range(5)]
            T1, T2, T3, T4, T5 = TT_
            win = dr["rg_w_in"][0]
            ps01 = PS[:, 0:1024]
            ps23 = PS[:, 1024:2048]
            ps45 = PS[:, 2048:3072]
            ps67 = PS[:, dbk * 512:dbk * 512 + 1024]

            def seg3(ap2d, a, n):
                return rap(ap2d[:, a:a + 1], [[L, S], [1, n]])

            XB2 = [XB, sb(ph, "XB2", [128, T], F32)]
            XBb2 = [XBb, sb(ph, "XBb2", [128, T], BF16)]
            bXB2 = [bXB, P.buf("XB2")]
            bXBb2 = [bXBb, P.buf("XBb2")]
            wslots = {}

            def loads(h):
                if h % 2 == 0:
                    wslots["x"] = load_wcols(win, D + h * 128, 256)
                    wslots["g"] = load_wcols(win, h * 128, 256)
                wi = h % 2
                P.dma("pool", lambda e, wi=wi, h=h: e.dma_start(out=WG[wi][:, 0, :, :],
                                                                in_=dr["rg_w_a"][0, :, h].rearrange("d i j -> i d j")),
                      writes=[bWG[wi]])
                P.dma("pool", lambda e, wi=wi, h=h: e.dma_start(out=WG[wi][:, 1, :, :],
                                                                in_=dr["rg_w_x"][0, :, h].rearrange("d i j -> i d j")),
                      writes=[bWG[wi]])
                return wslots["x"], wslots["g"]

            hw = {}

            def A_(h):
                (sx, wx), _ = hw[h]
                hh = h % 2
                for th in range(2):
                    for kc in range(16):
                        P.op("pe", lambda e, th=th, kc=kc, wx=wx, hh=hh:
                             e.matmul(bank(th), lhsT=wx[:, kc, hh * 128:(hh + 1) * 128], rhs=XN[:, kc, th * 512:(th + 1) * 512],
                                      start=(kc == 0), stop=(kc == 15)),
                             reads=[bRING[sx], bXN[kc]], writes=[bPS[th]], signal=(kc == 15))

            def D_(h):
                _, (sg_, wg_) = hw[h]
                hh = h % 2
                for th in range(2):
                    for kc in range(16):
                        P.op("pe", lambda e, th=th, kc=kc, wg_=wg_, hh=hh:
                             e.matmul(bank(dbk + th), lhsT=wg_[:, kc, hh * 128:(hh + 1) * 128],
                                      rhs=XN[:, kc, th * 512:(th + 1) * 512], start=(kc == 0), stop=(kc == 15)),
                             reads=[bRING[sg_], bXN[kc]], writes=[bPS[dbk + th]], signal=(kc == 15))

            def conv_a(h):
                xb, bxb = XB2[h % 2], bXB2[h % 2]
                P.op("act", lambda e, h=h, xb=xb: e.activation(out=xb[:], in_=ps01, func=AF.Identity,
                                                               scale=pcol("rcw", 2 * 16 + h), bias=pcol("rcb", h)),
                     reads=[bPS[0], bPS[1], bPT], writes=[bxb])
                for (k, dst0, src0, n) in ((0, 2, 0, L - 2), (1, 1, 0, L - 1), (3, 0, 1, L - 1)):
                    P.op("dve", lambda e, h=h, k=k, dst0=dst0, src0=src0, n=n, xb=xb:
                         e.scalar_tensor_tensor(out=seg3(xb, dst0, n), in0=seg3(ps01, src0, n),
                                                scalar=pcol("rcw", k * 16 + h), in1=seg3(xb, dst0, n),
                                                op0=ALU.mult, op1=ALU.add),
                         reads=[bPS[0], bPS[1], bPT, bxb], writes=[bxb])

            def conv_b(h):
                P.op("act", lambda e, h=h: e.activation(out=XBb2[h % 2][:], in_=XB2[h % 2][:], func=AF.Copy),
                     reads=[bXB2[h % 2]], writes=[bXBb2[h % 2]])

            def gelu_(h):
                P.op("act", lambda e, h=h: e.activation(out=YG[:, h, :], in_=ps67, func=AF.Gelu_apprx_tanh),
                     reads=[bPS[dbk], bPS[dbk + 1]], writes=[bYG[h]])

            def B_(h, d):
                wi = h % 2
                xbb, bxbb = XBb2[h % 2], bXBb2[h % 2]
                for th in range(2):
                    P.op("pe", lambda e, th=th, d=d, wi=wi, xbb=xbb:
                         e.matmul(bank(2 + th), lhsT=WG[wi][:, 0, d, :], rhs=xbb[:, th * 512:(th + 1) * 512],
                                  start=True, stop=True),
                         reads=[bWG[wi], bxbb], writes=[bPS[2 + th]])
                for th in range(2):
                    P.op("pe", lambda e, th=th, d=d, wi=wi, xbb=xbb:
                         e.matmul(bank(4 + th), lhsT=WG[wi][:, 1, d, :], rhs=xbb[:, th * 512:(th + 1) * 512],
                                  start=True, stop=True),
                         reads=[bWG[wi], bxbb], writes=[bPS[4 + th]])

            def tanh2(h, d):
                col = d * 16 + h
                P.op("act", lambda e, col=col: e.activation(out=T1[:], in_=ps23, func=AF.Tanh, scale=0.5,
                                                            bias=HB_[:, 0, col:col + 1]),
                     reads=[bPS[2], bPS[3], bHB], writes=[bT[0]])
                P.op("act", lambda e, col=col: e.activation(out=T3[:], in_=ps45, func=AF.Tanh, scale=0.5,
                                                            bias=HB_[:, 1, col:col + 1]),
                     reads=[bPS[4], bPS[5], bHB], writes=[bT[2]])

            def chain(h, d):
                col = d * 16 + h
                xb, bxb = XB2[h % 2], bXB2[h % 2]
                P.op("act", lambda e, col=col: e.activation(out=T2[:], in_=T1[:], func=AF.Exp,
                                                            scale=CL[:, 0, col:col + 1], bias=CL[:, 0, col:col + 1]),
                     reads=[bT[0], bCL], writes=[bT[1]])
                P.op("act", lambda e, col=col: e.activation(out=T1[:], in_=T1[:], func=AF.Exp,
                                                            scale=CL[:, 1, col:col + 1], bias=CL[:, 1, col:col + 1]),
                     reads=[bT[0], bCL], writes=[bT[0]])
                P.op("act", lambda e: e.activation(out=T1[:], in_=T1[:], func=AF.Sqrt, scale=-0.25, bias=Q25_T[:, 0:1]),
                     reads=[bT[0], bEPS], writes=[bT[0]])
                P.op("dve", lambda e, xb=xb: e.scalar_tensor_tensor(out=T3[:], in0=T3[:], scalar=1.0, in1=xb[:],
                                                                    op0=ALU.add, op1=ALU.mult),
                     reads=[bT[2], bxb], writes=[bT[2]])
                P.op("dve", lambda e: e.tensor_tensor(out=T3[:], in0=T3[:], in1=T1[:], op=ALU.mult),
                     reads=[bT[2], bT[0]], writes=[bT[2]])
                HD = T4 if d == 0 else T5
                bHD = bT[3] if d == 0 else bT[4]
                for s_ in range(S):
                    a0 = s_ * L
                    if d == 0:
                        oa, da, ua = HD[:, a0:a0 + L], T2[:, a0:a0 + L], T3[:, a0:a0 + L]
                    else:
                        oa = rap(HD[:, a0 + L - 1:a0 + L], [[-1, L]])
                        da = rap(T2[:, a0 + L - 1:a0 + L], [[-1, L]])
                        ua = rap(T3[:, a0 + L - 1:a0 + L], [[-1, L]])
                    init = 0.0 if g == 0 else pcol("st", col)
                    P.op("dve", lambda e, oa=oa, da=da, ua=ua, init=init:
                         e.tensor_tensor_scan(out=oa, data0=da, data1=ua, initial=init, op0=ALU.mult, op1=ALU.add),
                         reads=[bT[1], bT[2], bPT], writes=[bHD])
                if g == 0:
                    pos = (L - 1) if d == 0 else 0
                    P.op("dve", lambda e, HD=HD, pos=pos, col=col:
                         e.tensor_copy(out=rap(NS[:, col:col + 1], [[32, S]]), in_=rap(HD[:, pos:pos + 1], [[L, S]])),
                         reads=[bHD], writes=[bNS])

            hw[0] = loads(0)
            A_(0)
            conv_a(0)
            conv_b(0)
            D_(0)
            for h in range(NCH):
                nx = h + 1 < NCH
                if dbk == 0:
                    gelu_(h)
                if nx:
                    hw[h + 1] = loads(h + 1)
                    A_(h + 1)
                modstep(2)
                if dbk != 0:
                    gelu_(h)
                B_(h, 0)
                tanh2(h, 0)
                B_(h, 1)
                modstep(2)
                if nx:
                    conv_a(h + 1)
                    D_(h + 1)
                modstep(2)
                chain(h, 0)
                if nx:
                    conv_b(h + 1)
                tanh2(h, 1)
                chain(h, 1)
                P.op("dve", lambda e: e.tensor_tensor(out=T4[:], in0=T4[:], in1=T5[:], op=ALU.add),
                     reads=[bT[3], bT[4]], writes=[bT[3]])
                P.op("dve", lambda e, h=h: e.tensor_tensor(out=YG[:, h, :], in0=T4[:], in1=YG[:, h, :], op=ALU.mult),
                     reads=[bT[3], bYG[h]], writes=[bYG[h]])
            if modl is not None:
                for _ in mg:
                    pass
                mod_finalize(modl, 7)
            wout = dr["rg_w_out"][0]

            def op_mm(oc, th, pb, kcs, wo, so):
                tok = slice(th * 512, (th + 1) * 512)
                for kc in kcs:
                    P.op("pe", lambda e, kc=kc, wo=wo, oc=oc, pb=pb, tok=tok:
                         e.matmul(bank(pb), lhsT=wo[:, kc, (oc % 2) * 128:(oc % 2 + 1) * 128], rhs=YG[:, kc, tok],
                                  start=(kc == 0), stop=(kc == 15)),
                         reads=[bRING[so], bYG[kc]], writes=[bPS[pb]], signal=(kc == 15))

            def op_evac(oc, th, pb):
                tok = slice(th * 512, (th + 1) * 512)
                P.op("dve", lambda e, oc=oc, pb=pb, tok=tok:
                     e.scalar_tensor_tensor(out=X[:, oc, tok], in0=bank(pb), scalar=modcol(l, 2, oc, g),
                                            in1=X[:, oc, tok], op0=ALU.mult, op1=ALU.add),
                     reads=[bPS[pb], bMOD, bX[oc]], writes=[bX[oc]])

            so, wo = load_wcols(wout, 0, 256)
            head = [(0, 0), (0, 1), (1, 0), (1, 1)]
            for i, (oc, th) in enumerate(head):
                op_mm(oc, th, i, range(15), wo, so)
            for i, (oc, th) in enumerate(head):
                op_mm(oc, th, i, [15], wo, so)
                op_evac(oc, th, i)
            k = 4
            for oc in range(2, NCH):
                if oc % 2 == 0:
                    so, wo = load_wcols(wout, oc * 128, 256)
                for th in range(2):
                    pb = k % 4
                    k += 1
                    op_mm(oc, th, pb, range(16), wo, so)
                    op_evac(oc, th, pb)
            P.barrier()
        if g == 0:
            with contextlib.ExitStack() as ph:
                NSo = sb(ph, "NSo", [128, 128], F32)
                bNSo = P.buf("NSo")
                P.op("pe", lambda e: e.transpose(bank(0, 128), NS[:], IDF[:]), reads=[bNS, bIDF], writes=[bPS[0]])
                P.op("dve", lambda e: e.tensor_copy(out=NSo[:], in_=bank(0, 128)), reads=[bPS[0]], writes=[bNSo])
                P.dma("sp", lambda e: e.dma_start(out=nsd.rearrange("r (h p) -> (r h) p", p=128), in_=NSo[:]),
                      reads=[bNSo])
                P.barrier()

    def cm_mixer(g):
        l = 1
        ring_state["n"] = 4
        win = dr["cm_w_in"][0]
        with contextlib.ExitStack() as ph:
            bV = [P.buf("V%d" % c) for c in range(NCH)]
            WST = sb(ph, "WST", [128, 8, 128], BF16)
            BT = sb(ph, "BT", [128, NCH, 128], F32)
            bWST, bWSF, bBT = P.buf("WST"), P.buf("WSF"), P.buf("BT")
            ROW = sb(ph, "ROW", [33, 1024], F32)
            bROW = P.buf("ROW")
            BH = sb(ph, "BH", [33, D], BF16)
            bBH = P.buf("BH")
            ST1 = sb(ph, "ST1", [128, 2, 8, 8], F32)
            bST = P.buf("ST1")
            STT = sb(ph, "STT", [128, 6, 8], F32)
            bSTT = P.buf("STT")
            GS = [sb(ph, "GS%d" % i, [128, 512], F32) for i in range(2)]
            bGS = [P.buf("GS%d" % i) for i in range(2)]
            JK = sb(ph, "JK", [128, 512], BF16)
            bJK = P.buf("JK")
            SCt = [sb(ph, "SCt%d" % i, [128, 512], F32) for i in range(2)]
            UGt = [sb(ph, "UGt%d" % i, [128, 512], F32) for i in range(2)]
            bSCt = [P.buf("SCt%d" % i) for i in range(2)]
            bUGt = [P.buf("UGt%d" % i) for i in range(2)]
            CMS = float(os.environ.get("CM_STOP", "9"))
            if CMS <= 0:
                return
            with contextlib.ExitStack() as ph2:
                WS0 = sb(ph2, "WS0", [128, 8, 128], F32)
                WSF = sb(ph2, "WSF", [128, 8, 128], F32)
                LNB = sb(ph2, "LNB", [128, D], F32)
                bWS0, bLNB = P.buf("WS0"), P.buf("LNB")
                P.dma("sp", lambda e: e.dma_start(out=WS0[:], in_=dr["cm_w_s"][0].rearrange("g p q -> p g q")), writes=[bWS0])
                P.dma("sp", lambda e: e.dma_start(out=LNB[:], in_=dr["cm_ln_b"][0:1, :].partition_broadcast(128)),
                      writes=[bLNB])
                TB = sb(ph2, "TB", [33, D], F32)
                bTB = P.buf("TB")
                P.op("dve", lambda e: e.memset(BH[:], 0.0), writes=[bBH])
                P.dma("sp", lambda e: e.dma_start(out=TB[0:1, :], in_=dr["cm_b_in"][0:1, D:2 * D]), writes=[bTB])
                P.dma("sp", lambda e: e.dma_start(out=TB[32:33, :], in_=dr["cm_b_in"][0:1, D:2 * D]), writes=[bTB])
                P.op("dve", lambda e: e.tensor_copy(out=BH[0:1, :], in_=TB[0:1, :]), reads=[bTB, bBH], writes=[bBH])
                P.op("dve", lambda e: e.tensor_copy(out=BH[32:33, :], in_=TB[32:33, :]), reads=[bTB, bBH], writes=[bBH])
                P.op("dve", lambda e: e.tensor_tensor(out=TB[32:33, :], in0=TB[32:33, :], in1=BH[32:33, :], op=ALU.subtract),
                     reads=[bTB, bBH], writes=[bTB])
                P.op("dve", lambda e: e.tensor_copy(out=BH[32:33, :], in_=TB[32:33, :]), reads=[bTB, bBH], writes=[bBH])
                P.dma("sp", lambda e: e.dma_start(out=ROW[32:33, 0:1024], in_=dr["cm_b_s"][0:1].rearrange("a g p -> a (g p)")),
                      writes=[bROW])
                if CMS <= 0.3:
                    P.barrier()
                    return
                for gi in range(8):
                    pb = gi % 2
                    P.op("pe", lambda e, gi=gi, pb=pb: e.transpose(bank(pb, 128), WS0[:, gi, :], IDF[:]),
                         reads=[bWS0, bIDF], writes=[bPS[pb]])
                    if CMS <= 0.4:
                        continue
                    P.op("dve", lambda e, gi=gi, pb=pb: e.tensor_copy(out=WSF[:, gi, :], in_=bank(pb, 128)),
                         reads=[bPS[pb]], writes=[bWSF])
                    if CMS <= 0.5:
                        continue
                    P.op("dve", lambda e, gi=gi: e.tensor_copy(out=WST[:, gi, :], in_=WSF[:, gi, :]),
                         reads=[bWSF], writes=[bWST])
                if CMS <= 0.6:
                    P.barrier()
                    return
                for c in range(NCH):
                    gi = c // 2
                    pb = 2 + c % 2
                    P.op("pe", lambda e, c=c, gi=gi, pb=pb:
                         e.matmul(bank(pb, 128), lhsT=LNB[:, c * 128:(c + 1) * 128], rhs=WSF[:, gi, :], start=True, stop=False),
                         reads=[bLNB, bWSF], writes=[bPS[pb]], signal=False)
                    P.op("pe", lambda e, c=c, gi=gi, pb=pb:
                         e.matmul(bank(pb, 128), lhsT=ONF[32:33, :], rhs=ROW[32:33, gi * 128:(gi + 1) * 128], start=False, stop=True),
                         reads=[bONF, bROW], writes=[bPS[pb]])
                    P.op("act", lambda e, c=c, pb=pb: e.activation(out=BT[:, c, :], in_=bank(pb, 128), func=AF.Copy),
                         reads=[bPS[pb]], writes=[bBT])
                P.barrier()
            if CMS <= 1:
                return
            V = sb(ph, "V", [128, 8, D], BF16)
            k = 0
            for vp in range(4):
                pair = vp % 2
                wv = load_pair(win, D + vp * 512, pair)
                rbs = [bRING[2 * pair], bRING[2 * pair + 1]]
                for tt in range(8):
                    pb = k % 2
                    k += 1
                    for kc in range(16):
                        P.op("pe", lambda e, kc=kc, tt=tt, wv=wv, pb=pb:
                             e.matmul(bank(pb), lhsT=XN[:, kc, tt * 128:(tt + 1) * 128], rhs=wv[:, kc, :],
                                      start=(kc == 0), stop=False),
                             reads=rbs + [bXN[kc]], writes=[bPS[pb]], signal=False)
                    P.op("pe", lambda e, vp=vp, pb=pb:
                         e.matmul(bank(pb), lhsT=ONB[0:33, :], rhs=BH[0:33, vp * 512:(vp + 1) * 512], start=False, stop=True),
                         reads=[bONB, bBH], writes=[bPS[pb]])
                    P.op("act", lambda e, pb=pb, tt=tt, vp=vp:
                         e.activation(out=GS[pb][:], in_=bank(pb), func=AF.Gelu_apprx_tanh,
                                      accum_out=ST1[:, 0, tt, vp:vp + 1]),
                         reads=[bPS[pb]], writes=[bGS[pb], bST])
                    P.op("act", lambda e, pb=pb, tt=tt, vp=vp:
                         e.activation(out=JK[:], in_=GS[pb][:], func=AF.Square, accum_out=ST1[:, 1, tt, vp:vp + 1]),
                         reads=[bGS[pb]], writes=[bJK, bST])
                    P.op("dve", lambda e, pb=pb, tt=tt, vp=vp:
                         e.tensor_copy(out=V[:, tt, vp * 512:(vp + 1) * 512], in_=GS[pb][:]),
                         reads=[bGS[pb]], writes=[bV[4 * vp + q] for q in range(4)])
            if CMS <= 2:
                P.barrier()
                return
            MU, EX2, VAR, RSTD, NB, TMPs = (STT[:, i, :] for i in range(6))
            P.op("dve", lambda e: e.tensor_reduce(out=MU, in_=ST1[:, 0, :, 0:4], axis=AX.X, op=ALU.add), reads=[bST], writes=[bSTT])
            P.op("dve", lambda e: e.tensor_reduce(out=EX2, in_=ST1[:, 1, :, 0:4], axis=AX.X, op=ALU.add), reads=[bST], writes=[bSTT])
            P.op("dve", lambda e: e.tensor_scalar(out=MU, in0=MU, scalar1=1.0 / D, scalar2=None, op0=ALU.mult),
                 reads=[bSTT], writes=[bSTT])
            P.op("dve", lambda e: e.tensor_tensor(out=TMPs, in0=MU, in1=MU, op=ALU.mult), reads=[bSTT], writes=[bSTT])
            P.op("dve", lambda e: e.scalar_tensor_tensor(out=VAR, in0=EX2, scalar=1.0 / D, in1=TMPs, op0=ALU.mult,
                                                         op1=ALU.subtract), reads=[bSTT], writes=[bSTT])
            P.op("act", lambda e: e.activation(out=RSTD, in_=VAR, func=AF.Sqrt, scale=1.0, bias=LNE_T[:, 0:1]),
                 reads=[bSTT, bEPS], writes=[bSTT])
            P.op("dve", lambda e: e.reciprocal(out=RSTD, in_=RSTD), reads=[bSTT], writes=[bSTT])
            P.op("dve", lambda e: e.scalar_tensor_tensor(out=NB, in0=MU, scalar=-1.0, in1=RSTD, op0=ALU.mult, op1=ALU.mult),
                 reads=[bSTT], writes=[bSTT])
            for tt in range(8):
                P.op("dve", lambda e, tt=tt: e.tensor_scalar(out=V[:, tt, :], in0=V[:, tt, :], scalar1=STT[:, 3, tt:tt + 1],
                                                             scalar2=STT[:, 4, tt:tt + 1], op0=ALU.mult, op1=ALU.add),
                     reads=bV + [bSTT], writes=bV)
            if CMS <= 3:
                P.barrier()
                return
            for c in range(NCH):
                gi = c // 2
                if c % 2 == 0:
                    su, wu = load_wcols(win, c * 128, 256)
                for tt in range(8):
                    pb = 2 + tt // 4
                    P.op("pe", lambda e, c=c, tt=tt, gi=gi, pb=pb:
                         e.matmul(bank(pb)[:, (tt % 4) * 128:(tt % 4 + 1) * 128], lhsT=V[:, tt, c * 128:(c + 1) * 128],
                                  rhs=WST[:, gi, :], start=True, stop=True),
                         reads=[bV[c], bWST], writes=[bPS[pb]], signal=(tt % 4 == 3))
                for th in range(2):
                    for kc in range(16):
                        P.op("pe", lambda e, th=th, kc=kc, wu=wu, c=c:
                             e.matmul(bank(4 + th), lhsT=wu[:, kc, (c % 2) * 128:(c % 2 + 1) * 128],
                                      rhs=XN[:, kc, th * 512:(th + 1) * 512], start=(kc == 0), stop=(kc == 15)),
                             reads=[bRING[su], bXN[kc]], writes=[bPS[4 + th]], signal=(kc == 15))
                for th in range(2):
                    P.op("dve", lambda e, th=th, c=c:
                         e.scalar_tensor_tensor(out=SCt[th][:].rearrange("p (a b) -> p a b", a=4),
                                                in0=bank(2 + th).rearrange("p (a b) -> p a b", a=4),
                                                scalar=pcol("clg", c),
                                                in1=rap(BT[:, c, 0:1], [[0, 4], [1, 128]]),
                                                op0=ALU.mult, op1=ALU.add),
                         reads=[bPS[2 + th], bPT, bBT], writes=[bSCt[th]])
                    P.op("act", lambda e, th=th, c=c:
                         e.activation(out=UGt[th][:], in_=bank(4 + th), func=AF.Gelu_apprx_tanh, bias=pcol("cbi", c), scale=1.0),
                         reads=[bPS[4 + th], bPT], writes=[bUGt[th]])
                    P.op("dve", lambda e, th=th, c=c:
                         e.tensor_tensor(out=V[:, 4 * th:4 * th + 4, c * 128:(c + 1) * 128],
                                         in0=UGt[th][:].rearrange("p (a b) -> p a b", a=4),
                                         in1=SCt[th][:].rearrange("p (a b) -> p a b", a=4), op=ALU.mult),
                         reads=[bUGt[th], bSCt[th]], writes=[bV[c]])
            if CMS <= 4:
                P.barrier()
                return
            wout = dr["cm_w_out"][0]
            k = 0
            for oc in range(NCH):
                if oc % 2 == 0:
                    so, wo = load_wcols(wout, oc * 128, 256)
                for th in range(2):
                    pb = 4 + k % 4
                    k += 1
                    tok = slice(th * 512, (th + 1) * 512)
                    for kc in range(16):
                        P.op("pe", lambda e, kc=kc, wo=wo, oc=oc, pb=pb, th=th:
                             e.matmul(bank(pb), lhsT=wo[:, kc, (oc % 2) * 128:(oc % 2 + 1) * 128],
                                      rhs=V[:, 4 * th:4 * th + 4, kc * 128:(kc + 1) * 128],
                                      start=(kc == 0), stop=(kc == 15)),
                             reads=[bRING[so], bV[kc]], writes=[bPS[pb]], signal=(kc == 15))
                    P.op("dve", lambda e, oc=oc, pb=pb, tok=tok:
                         e.scalar_tensor_tensor(out=X[:, oc, tok], in0=bank(pb), scalar=modcol(l, 2, oc, g),
                                                in1=X[:, oc, tok], op0=ALU.mult, op1=ALU.add),
                         reads=[bPS[pb], bMOD, bX[oc]], writes=[bX[oc]])
            P.barrier()

    def ffn(l, g):
        S = 4 if g == 0 else 1
        L = T // S
        wup = dr["ffn_w_up"][l]
        wdn = dr["ffn_w_down"][l]
        NFB = DFF // 256
        with contextlib.ExitStack() as ph:
            DR = [sb(ph, "DR%d" % i, [128, 4096], BF16) for i in range(4)]
            bDR = [P.buf("DR%d" % i) for i in range(4)]
            HBf = [sb(ph, "HBf%d" % i, [128, 4, T], BF16) for i in range(2)]
            bHBf = [P.buf("HBf%d" % i) for i in range(2)]
            GC = [sb(ph, "GC%d" % i, [128, T], F32) for i in range(2)]
            VC = [sb(ph, "VC%d" % i, [128, T], F32) for i in range(2)]
            bGC = [P.buf("GC%d" % i) for i in range(2)]
            bVC = [P.buf("VC%d" % i) for i in range(2)]
            ring_state["n"] = 4
            ps01 = PS[:, 0:1024]
            ps23 = PS[:, 1024:2048]

            def seg3(ap2d, a, n):
                return rap(ap2d[:, a:a + 1], [[L, S], [1, n]])

            def conv_evac(psv, pbs, dst, bdst, fc):
                P.op("act", lambda e: e.activation(out=dst[:], in_=psv, func=AF.Identity,
                                                   scale=pcol("fcw", (l * 3 + 1) * 88 + fc), bias=pcol("fcb", l * 88 + fc)),
                     reads=pbs + [bPT], writes=[bdst])
                for (kk, dst0, src0) in ((0, 1, 0), (2, 0, 1)):
                    P.op("dve", lambda e, kk=kk, dst0=dst0, src0=src0:
                         e.scalar_tensor_tensor(out=seg3(dst, dst0, L - 1), in0=seg3(psv, src0, L - 1),
                                                scalar=pcol("fcw", (l * 3 + kk) * 88 + fc), in1=seg3(dst, dst0, L - 1),
                                                op0=ALU.mult, op1=ALU.add),
                         reads=pbs + [bPT, bdst], writes=[bdst])

            def up_gen(fb):
                sg_, wg_ = load_wcols(wup, fb * 256, 256)
                sv_, wv_ = load_wcols(wup, DFF + fb * 256, 256)
                src = wdn[fb * 256:(fb + 1) * 256, :].rearrange("(k p) c -> p k c", p=128)
                di = fb % 4
                ddst = DR[di][:, 0:4096].rearrange("p (k c) -> p k c", k=2)
                P.dma("pool", lambda e, ddst=ddst, src=src: e.dma_start(out=ddst, in_=src), writes=[bDR[di]])
                hb = (fb // 2) % 2
                for j in range(2):
                    fc = 2 * fb + j
                    ji = j % 2
                    hk = (fb % 2) * 2 + j
                    for (w_, s_, b0, isg) in ((wg_, sg_, 0, True), (wv_, sv_, 2, False)):
                        for th in range(2):
                            for kc in range(16):
                                P.op("pe", lambda e, th=th, kc=kc, w_=w_, j=j, b0=b0:
                                     e.matmul(bank(b0 + th), lhsT=w_[:, kc, j * 128:(j + 1) * 128],
                                              rhs=XN[:, kc, th * 512:(th + 1) * 512], start=(kc == 0), stop=(kc == 15)),
                                     reads=[bRING[s_], bXN[kc]], writes=[bPS[b0 + th]], signal=(kc == 15))
                            if th == 1:
                                if isg:
                                    conv_evac(ps01, [bPS[0], bPS[1]], GC[ji], bGC[ji], fc)
                                else:
                                    conv_evac(ps23, [bPS[2], bPS[3]], VC[ji], bVC[ji], NFC + fc)
                                    P.op("act", lambda e, ji=ji: e.activation(out=GC[ji][:], in_=GC[ji][:], func=AF.Silu),
                                         reads=[bGC[ji]], writes=[bGC[ji]])
                                    P.op("dve", lambda e, ji=ji, hb=hb, hk=hk:
                                         e.tensor_tensor(out=HBf[hb][:, hk, :], in0=GC[ji][:], in1=VC[ji][:], op=ALU.mult),
                                         reads=[bGC[ji], bVC[ji]], writes=[bHBf[hb]])
                            yield

            kd = [0]

            def down_gen(pi):
                hb = pi % 2
                n = 0
                for oc in range(NCH):
                    for th in range(2):
                        pb = 4 + kd[0] % 4
                        kd[0] += 1
                        tok = slice(th * 512, (th + 1) * 512)
                        for hk in range(4):
                            di = (2 * pi + hk // 2) % 4
                            wd_ = DR[di][:, 0:4096].rearrange("p (k c) -> p k c", k=2)
                            P.op("pe", lambda e, hk=hk, wd_=wd_, oc=oc, pb=pb, tok=tok, hb=hb:
                                 e.matmul(bank(pb), lhsT=wd_[:, hk % 2, oc * 128:(oc + 1) * 128], rhs=HBf[hb][:, hk, tok],
                                          start=(hk == 0), stop=(hk == 3)),
                                 reads=[bDR[di], bHBf[hb]], writes=[bPS[pb]], signal=(hk == 3))
                        P.op("dve", lambda e, oc=oc, pb=pb, tok=tok:
                             e.scalar_tensor_tensor(out=X[:, oc, tok], in0=bank(pb), scalar=modcol(l, 5, oc, g),
                                                    in1=X[:, oc, tok], op0=ALU.mult, op1=ALU.add),
                             reads=[bPS[pb], bMOD, bX[oc]], writes=[bX[oc]])
                        n += 1
                        if n % 2 == 0:
                            yield

            def drain(gen):
                for _ in gen:
                    pass

            def chain2(fb0):
                for fb in (fb0, fb0 + 1):
                    if fb < NFB:
                        for _ in up_gen(fb):
                            yield

            drain(chain2(0))
            for pi in range(NFB // 2):
                ug = chain2(2 * pi + 2)
                dg = down_gen(pi)
                while True:
                    a_ = next(ug, "end")
                    b_ = next(dg, "end")
                    if a_ == "end" and b_ == "end":
                        break
            P.barrier()

    def final(g):
        dst = yp if g == 0 else ys
        with contextlib.ExitStack() as ph:
            FGB = sb(ph, "FGB", [128, D], F32)
            bFGB = P.buf("FGB")
            OST = [sb(ph, "OST%d" % i, [128, D], F32) for i in range(2)]
            bOST = [P.buf("OST%d" % i) for i in range(2)]
            SSQ = sb(ph, "SSQ", [128, 8, 4], F32)
            bSSQ = P.buf("SSQ")
            RSTD = sb(ph, "RSTDF", [128, 8], F32)
            bRSTD = P.buf("RSTDF")
            JF = sb(ph, "JF", [128, 512], BF16)
            bJF = P.buf("JF")
            P.dma("sp", lambda e: e.dma_start(out=FGB[:], in_=dr["final_g"].rearrange("(a n) -> a n", a=1).partition_broadcast(128)),
                  writes=[bFGB])
            for tt in range(8):
                oi = tt % 2
                base = (tt % 2) * 4
                for q in range(4):
                    pb = base + q
                    for j in range(4):
                        c = q * 4 + j
                        P.op("pe", lambda e, c=c, pb=pb, j=j, tt=tt:
                             e.transpose(bank(pb)[:, j * 128:(j + 1) * 128], X[:, c, tt * 128:(tt + 1) * 128], IDF[:]),
                             reads=[bX[c], bIDF], writes=[bPS[pb]], signal=(j == 3))
                    P.op("act", lambda e, pb=pb, tt=tt, q=q:
                         e.activation(out=JF[:], in_=bank(pb), func=AF.Square, accum_out=SSQ[:, tt, q:q + 1]),
                         reads=[bPS[pb]], writes=[bJF, bSSQ])
                P.op("dve", lambda e, tt=tt: e.tensor_reduce(out=RSTD[:, tt:tt + 1], in_=SSQ[:, tt, :], axis=AX.X, op=ALU.add),
                     reads=[bSSQ], writes=[bRSTD])
                P.op("act", lambda e, tt=tt: e.activation(out=RSTD[:, tt:tt + 1], in_=RSTD[:, tt:tt + 1], func=AF.Sqrt,
                                                          scale=1.0 / D, bias=EPS_T[:, 0:1]),
                     reads=[bRSTD, bEPS], writes=[bRSTD])
                P.op("dve", lambda e, tt=tt: e.reciprocal(out=RSTD[:, tt:tt + 1], in_=RSTD[:, tt:tt + 1]),
                     reads=[bRSTD], writes=[bRSTD])
                for q in range(4):
                    pb = base + q
                    P.op("dve", lambda e, oi=oi, q=q, pb=pb, tt=tt:
                         e.scalar_tensor_tensor(out=OST[oi][:, q * 512:(q + 1) * 512], in0=bank(pb),
                                                scalar=RSTD[:, tt:tt + 1], in1=FGB[:, q * 512:(q + 1) * 512],
                                                op0=ALU.mult, op1=ALU.mult),
                         reads=[bPS[pb], bRSTD, bFGB], writes=[bOST[oi]])
                P.dma("sp", lambda e, oi=oi, tt=tt: e.dma_start(out=dst[tt * 128:(tt + 1) * 128, :], in_=OST[oi][:]),
                      reads=[bOST[oi]])
            P.barrier()


    def want(stage):
        return (stage < stop_after) if only is None else (stage in only)

    for g in groups:
        first = (g == groups[0])
        load_group(g, defer=(list(range(NSET1, NSET1 + 5)) if first else None))
        if want(0):
            norm_mod(0, 0, g, nxt="rg", defer=(list(range(NSET1 + 5, NSET1 + 10)) if first else None))
            rg_mixer(g)
        if want(1):
            norm_mod(0, 1, g, nxt="ffn", defer=(list(range(NSET1 + 10, 48)) if first else None), dlast=first)
            ffn(0, g)
        if want(2):
            norm_mod(1, 0, g, nxt="cm")
            cm_mixer(g)
        if want(3):
            norm_mod(1, 1, g, nxt="ffn")
            ffn(1, g)
        final(g)

    P.emit()
    P.close()
    es.close()
    return nc


_NC_CACHE = {}


def _get_nc(stop_after=99):
    if stop_after not in _NC_CACHE:
        _NC_CACHE[stop_after] = build(stop_after)
    return _NC_CACHE[stop_after]


def make_in_maps(inputs):
    f = lambda a: np.ascontiguousarray(np.asarray(a, dtype=np.float32))
    shared = {k: f(inputs[k]) for k in IN_SHAPES if k not in ("xp", "xs", "st", "cond")}
    xp = f(inputs["x_prompt"])
    xs = f(inputs["x_sample"])
    st = f(inputs["state_rglru"])
    c = f(inputs["c"])
    cctx = f(inputs["c_ctx"])
    maps = []
    for i in range(NCORES):
        m = dict(shared)
        m["xp"] = np.ascontiguousarray(xp[4 * i:4 * i + 4].reshape(T, D))
        m["xs"] = np.ascontiguousarray(xs[i])
        m["st"] = np.ascontiguousarray(st[i, 0])
        m["cond"] = np.ascontiguousarray(np.stack([cctx, c[i]], axis=0))
        maps.append(m)
    return maps


def kernel(**inputs):
    nc = _get_nc()
    maps = make_in_maps(inputs)
    res = run_bass_kernel_spmd(nc, maps, core_ids=list(range(NCORES)))
    outs = res.results
    y_prompt = np.concatenate([np.asarray(r["yp"], dtype=np.float32).reshape(4, 256, D) for r in outs], axis=0)
    y_sample = np.stack([np.asarray(r["ys"], dtype=np.float32) for r in outs], axis=0)
    new_state = np.concatenate([np.asarray(r["ns"], dtype=np.float32).reshape(4, 1, 2, D) for r in outs], axis=0)
    return (y_prompt, y_sample, new_state)
```
